# Optimizing a Trainium2 kernel written in Bass

```python
import jax, jax.numpy as jnp
from jax import lax
import numpy as np

D_MODEL = 2048
BATCH = 16
SEQ = 2048
DEPTH = 4
DEC_BATCH = 8
DEC_SEQ = 2048
PAST_LEN = 128

GRID_W = 64
HEAD_DIM = 128
A_Q = D_MODEL // 2
A_HEADS = A_Q // HEAD_DIM
A_KV_HEADS = A_HEADS // 4
A_GROUP = A_HEADS // A_KV_HEADS
A_KV = A_KV_HEADS * HEAD_DIM
ROPE_THETA = 10000.0
Q_BLOCK = 128
B_W = D_MODEL // 4
B_HEADS = B_W // HEAD_DIM
NA_ROWS = 8
NA_COLS = 16
C_V = D_MODEL // 4
C_HEADS = 4
C_DV = C_V // C_HEADS
C_DK = C_DV // 2
C_K = C_HEADS * C_DK
C_RANK = 16
C_TAU = 16.0
C_CHUNK = 64
D_FF = 5632
EPS = 1e-6
N_BRANCH = 3

SPLITS = (A_Q, A_KV, A_KV,
          B_W, B_W, B_W,
          C_K, C_K, C_V, C_V,
          C_RANK, C_RANK,
          D_MODEL, D_MODEL, D_MODEL)
N_IN = sum(SPLITS)
SPLIT_IDX = tuple(int(i) for i in np.cumsum(SPLITS)[:-1])

kernel_name = "hybrid_gqa_natten_gla_macaron_encoder"


def rms_norm(x, gain):
    x32 = x.astype(jnp.float32)
    y = x32 * lax.rsqrt(jnp.mean(x32 * x32, axis=-1, keepdims=True) + EPS)
    return (y * gain.astype(jnp.float32)).astype(x.dtype)


def swiglu(x, w_in, w_out):
    g, u = jnp.split(x @ w_in, 2, axis=-1)
    return (jax.nn.silu(g) * u) @ w_out


def axial_rope(n_tok):
    t = jnp.arange(n_tok)
    pos_r = (t // GRID_W).astype(jnp.float32)
    pos_c = (t % GRID_W).astype(jnp.float32)
    half = HEAD_DIM // 2
    inv = ROPE_THETA ** (-jnp.arange(0, half, 2, dtype=jnp.float32) / half)
    ang = jnp.concatenate([pos_r[:, None] * inv, pos_c[:, None] * inv], axis=-1)
    return jnp.cos(ang), jnp.sin(ang)


def apply_rope(x, cos, sin):
    xp = x.astype(jnp.float32).reshape(*x.shape[:-1], HEAD_DIM // 2, 2)
    x1, x2 = xp[..., 0], xp[..., 1]
    c = cos[None, :, None, :]
    s = sin[None, :, None, :]
    out = jnp.stack([x1 * c - x2 * s, x1 * s + x2 * c], axis=-1)
    return out.reshape(x.shape).astype(x.dtype)


def mixer_a(q, k, v, qk_gain, cos, sin):
    bsz, n_tok, _ = q.shape
    q = q.reshape(bsz, n_tok, A_HEADS, HEAD_DIM)
    k = k.reshape(bsz, n_tok, A_KV_HEADS, HEAD_DIM)
    v = v.reshape(bsz, n_tok, A_KV_HEADS, HEAD_DIM)
    q = apply_rope(rms_norm(q, qk_gain[0]), cos, sin)
    k = apply_rope(rms_norm(k, qk_gain[1]), cos, sin)
    n_blk = n_tok // Q_BLOCK
    qb_all = q.reshape(bsz, n_blk, Q_BLOCK, A_KV_HEADS, A_GROUP, HEAD_DIM).transpose(1, 0, 2, 3, 4, 5)
    scale = HEAD_DIM ** -0.5

    def block(qb):
        s = jnp.einsum('bqkgd,bskd->bkgqs', qb, k, preferred_element_type=jnp.float32) * scale
        p = jax.nn.softmax(s, axis=-1).astype(v.dtype)
        return jnp.einsum('bkgqs,bskd->bqkgd', p, v)

    o = lax.map(block, qb_all)
    return o.transpose(1, 0, 2, 3, 4, 5).reshape(bsz, n_tok, A_Q)


def mixer_b(q, k, v, rpb):
    bsz, n_tok, _ = q.shape
    rows = n_tok // GRID_W
    wr = min(NA_ROWS, rows)
    q = q.reshape(bsz, rows, GRID_W, B_HEADS, HEAD_DIM) * (HEAD_DIM ** -0.5)
    k = k.reshape(bsz, rows, GRID_W, B_HEADS, HEAD_DIM)
    v = v.reshape(bsz, rows, GRID_W, B_HEADS, HEAD_DIM)
    cols = jnp.arange(GRID_W)
    col_start = jnp.clip(cols - NA_COLS // 2, 0, GRID_W - NA_COLS)
    col_idx = col_start[:, None] + jnp.arange(NA_COLS)[None, :]
    dc = col_idx - cols[:, None] + (NA_COLS - 1)

    def row_block(r):
        rs = jnp.clip(r - wr // 2, 0, rows - wr)
        kb = lax.dynamic_slice_in_dim(k, rs, wr, axis=1)
        vb = lax.dynamic_slice_in_dim(v, rs, wr, axis=1)
        kw = kb[:, :, col_idx]
        vw = vb[:, :, col_idx]
        dr = rs + jnp.arange(wr) - r + (NA_ROWS - 1)
        bias = rpb[:, dr[:, None, None], dc[None, :, :]]
        qr = lax.dynamic_index_in_dim(q, r, axis=1, keepdims=False)
        s = jnp.einsum('bqhd,bwqjhd->bhqwj', qr, kw, preferred_element_type=jnp.float32)
        s = s + bias.transpose(0, 2, 1, 3)[None].astype(jnp.float32)
        p = jax.nn.softmax(s.reshape(bsz, B_HEADS, GRID_W, wr * NA_COLS), axis=-1)
        p = p.reshape(bsz, B_HEADS, GRID_W, wr, NA_COLS).astype(v.dtype)
        return jnp.einsum('bhqwj,bwqjhd->bqhd', p, vw)

    o = lax.map(row_block, jnp.arange(rows))
    return o.transpose(1, 0, 2, 3, 4).reshape(bsz, n_tok, B_W)


def gla_scan(q, k, v, g):
    bsz, n_tok, n_h, dk = q.shape
    dv = v.shape[-1]
    n_ch = n_tok // C_CHUNK

    def to_chunks(t):
        return t.reshape(bsz, n_ch, C_CHUNK, n_h, t.shape[-1]).transpose(1, 0, 3, 2, 4)

    mask = jnp.tril(jnp.ones((C_CHUNK, C_CHUNK), dtype=bool))

    def step(state, inp):
        qi, ki, vi, gi = inp
        b = jnp.cumsum(gi, axis=-2)
        diff = b[:, :, :, None, :] - b[:, :, None, :, :]
        decay = jnp.exp(jnp.where(mask[:, :, None], diff, -jnp.inf))
        attn = jnp.einsum('bhid,bhjd,bhijd->bhij', qi, ki, decay)
        o = jnp.einsum('bhij,bhjv->bhiv', attn, vi) + jnp.einsum('bhid,bhdv->bhiv', qi * jnp.exp(b), state)
        b_last = b[:, :, -1:, :]
        state = jnp.exp(b_last[:, :, 0, :, None]) * state + jnp.einsum('bhjd,bhjv->bhdv', ki * jnp.exp(b_last - b), vi)
        return state, o

    state0 = jnp.zeros((bsz, n_h, dk, dv), jnp.float32)
    _, o = lax.scan(step, state0, (to_chunks(q), to_chunks(k), to_chunks(v), to_chunks(g)))
    return o.transpose(1, 0, 3, 2, 4).reshape(bsz, n_tok, n_h, dv)


def mixer_c(xq, xk, xv, xog, lr_f, lr_b, w_decay, b_decay, onorm):
    bsz, n_tok, _ = xq.shape
    f32 = jnp.float32
    q = xq.astype(f32).reshape(bsz, n_tok, C_HEADS, C_DK) * (C_DK ** -0.5)
    k = xk.astype(f32).reshape(bsz, n_tok, C_HEADS, C_DK)
    v = xv.astype(f32).reshape(bsz, n_tok, C_HEADS, C_DV)

    def log_decay(lr, w2, b2):
        z = lr.astype(f32) @ w2.astype(f32) + b2.astype(f32)
        return (jax.nn.log_sigmoid(z) / C_TAU).reshape(bsz, n_tok, C_HEADS, C_DK)

    g_f = log_decay(lr_f, w_decay[0], b_decay[0])
    g_b = log_decay(lr_b, w_decay[1], b_decay[1])
    flip = lambda t: jnp.flip(t, axis=1)
    o_f = gla_scan(q, k, v, g_f)
    o_b = flip(gla_scan(flip(q), flip(k), flip(v), flip(g_b)))
    o = rms_norm(o_f + o_b, onorm).reshape(bsz, n_tok, C_V)
    o = o * jax.nn.silu(xog.astype(f32))
    return o.astype(xq.dtype)


def token_mixing(u, w_in, gate_bias, qk_norm_a, rpb_b, w_decay_c, b_decay_c, onorm_c,
                 w_br_a, w_br_b, w_br_c, w_out, cos, sin):
    proj = u @ w_in
    (aq, ak, av, bq, bk, bv, cq, ck, cv, cog, clf, clb, ga, gb, gc) = jnp.split(proj, SPLIT_IDX, axis=-1)
    ya = mixer_a(aq, ak, av, qk_norm_a, cos, sin) @ w_br_a
    yb = mixer_b(bq, bk, bv, rpb_b) @ w_br_b
    yc = mixer_c(cq, ck, cv, cog, clf, clb, w_decay_c, b_decay_c, onorm_c) @ w_br_c
    merged = (jax.nn.sigmoid(ga + gate_bias[0]) * ya
              + jax.nn.sigmoid(gb + gate_bias[1]) * yb
              + jax.nn.sigmoid(gc + gate_bias[2]) * yc)
    return merged @ w_out


def trunk(x, norm_gains, w_in, gate_bias, qk_norm_a, rpb_b, w_decay_c, b_decay_c, onorm_c,
          w_br_a, w_br_b, w_br_c, w_out, w_ffn1_in, w_ffn1_out, w_ffn2_in, w_ffn2_out):
    cos, sin = axial_rope(x.shape[1])
    for l in range(DEPTH):
        ng = norm_gains[l]
        x = x + 0.5 * rms_norm(swiglu(rms_norm(x, ng[0]), w_ffn1_in[l], w_ffn1_out[l]), ng[1])
        mix = token_mixing(rms_norm(x, ng[2]), w_in[l], gate_bias[l], qk_norm_a[l], rpb_b[l],
                           w_decay_c[l], b_decay_c[l], onorm_c[l], w_br_a[l], w_br_b[l], w_br_c[l],
                           w_out[l], cos, sin)
        x = x + rms_norm(mix, ng[3])
        x = x + 0.5 * rms_norm(swiglu(rms_norm(x, ng[4]), w_ffn2_in[l], w_ffn2_out[l]), ng[5])
    return x


def setup_inputs(seed: int = 0) -> dict:
    key = jax.random.key(seed)
    ks = jax.random.split(key, 20)
    f32 = jnp.float32

    def dense(k, shape, fan_in):
        return jax.random.normal(k, shape, f32) * (fan_in ** -0.5)

    return {
        "x_prompt": jax.random.normal(ks[0], (BATCH, SEQ, D_MODEL), f32),
        "x_sample": jax.random.normal(ks[1], (DEC_BATCH, DEC_SEQ, D_MODEL), f32),
        "norm_gains": 1.0 + 0.02 * jax.random.normal(ks[2], (DEPTH, 6, D_MODEL), f32),
        "w_in": dense(ks[3], (DEPTH, D_MODEL, N_IN), D_MODEL),
        "gate_bias": 0.02 * jax.random.normal(ks[4], (DEPTH, N_BRANCH, D_MODEL), f32),
        "qk_norm_a": 1.0 + 0.02 * jax.random.normal(ks[5], (DEPTH, 2, HEAD_DIM), f32),
        "rpb_b": 0.1 * jax.random.normal(ks[6], (DEPTH, B_HEADS, 2 * NA_ROWS - 1, 2 * NA_COLS - 1), f32),
        "w_decay_c": dense(ks[7], (DEPTH, 2, C_RANK, C_K), C_RANK),
        "b_decay_c": 0.1 * jax.random.normal(ks[8], (DEPTH, 2, C_K), f32),
        "onorm_c": 1.0 + 0.02 * jax.random.normal(ks[9], (DEPTH, C_DV), f32),
        "w_br_a": dense(ks[10], (DEPTH, A_Q, D_MODEL), A_Q),
        "w_br_b": dense(ks[11], (DEPTH, B_W, D_MODEL), B_W),
        "w_br_c": dense(ks[12], (DEPTH, C_V, D_MODEL), C_V),
        "w_out": dense(ks[13], (DEPTH, D_MODEL, D_MODEL), D_MODEL),
        "w_ffn1_in": dense(ks[14], (DEPTH, D_MODEL, 2 * D_FF), D_MODEL),
        "w_ffn1_out": dense(ks[15], (DEPTH, D_FF, D_MODEL), D_FF),
        "w_ffn2_in": dense(ks[16], (DEPTH, D_MODEL, 2 * D_FF), D_MODEL),
        "w_ffn2_out": dense(ks[17], (DEPTH, D_FF, D_MODEL), D_FF),
    }


def reference(x_prompt, x_sample, norm_gains, w_in, gate_bias, qk_norm_a, rpb_b, w_decay_c, b_decay_c,
              onorm_c, w_br_a, w_br_b, w_br_c, w_out, w_ffn1_in, w_ffn1_out, w_ffn2_in, w_ffn2_out):
    y_prompt = trunk(x_prompt, norm_gains, w_in, gate_bias, qk_norm_a, rpb_b, w_decay_c, b_decay_c, onorm_c,
                     w_br_a, w_br_b, w_br_c, w_out, w_ffn1_in, w_ffn1_out, w_ffn2_in, w_ffn2_out)
    y_sample = trunk(x_sample, norm_gains, w_in, gate_bias, qk_norm_a, rpb_b, w_decay_c, b_decay_c, onorm_c,
                     w_br_a, w_br_b, w_br_c, w_out, w_ffn1_in, w_ffn1_out, w_ffn2_in, w_ffn2_out)
    return (y_prompt, y_sample)
```

```python
import numpy as np
from contextlib import ExitStack
import concourse.bass as bass
import concourse.mybir as mybir
from concourse.bass_utils import run_bass_kernel_spmd

F32 = mybir.dt.float32
BF16 = mybir.dt.bfloat16
AF = mybir.ActivationFunctionType
ALU = mybir.AluOpType

D = 2048
S = 2048
DFF = 5632
NCH = 16
T = 512
NTT = S // T
EPS = 1e-6
NMIX = 4640
NMIXB = 37
GRID_W = 64


class Eng:
    def __init__(self, kb, raw, name):
        self.raw = raw
        self.name = name
        self.sem = kb.newsem("s_" + name)
        self.cnt = 0
        self.seen = {}

    def wait(self, *toks):
        for t in toks:
            if t is None:
                continue
            if isinstance(t, list):
                self.wait(*t)
                continue
            sem, v = t
            if self.seen.get(id(sem), 0) >= v:
                continue
            self.raw.wait_ge(sem, v)
            self.seen[id(sem)] = v

    def ms(self, ins):
        self.cnt += 1
        ins.then_inc(self.sem, 1)
        return (self.sem, self.cnt)


class DSem:
    def __init__(self, kb, name):
        self.sem = kb.newsem(name)
        self.cnt = 0
        kb.dsems.append(self)


class KB:
    def __init__(self):
        self.nc = bass.Bass("TRN2", target_bir_lowering=False)
        self.es = ExitStack()
        nc = self.nc
        self.work_sems = []
        self.dsems = []
        self.bar_a = self.es.enter_context(nc.semaphore("bar_a"))
        self.bar_b = self.es.enter_context(nc.semaphore("bar_b"))
        self.bar_k = 0
        self.pe = Eng(self, nc.tensor, "pe")
        self.act = Eng(self, nc.scalar, "act")
        self.dve = Eng(self, nc.vector, "dve")
        self.pool = Eng(self, nc.gpsimd, "pool")
        self.sp = Eng(self, nc.sync, "sp")
        self.engs = [self.pe, self.act, self.dve, self.pool, self.sp]
        self.pending = []

    def newsem(self, name):
        sem = self.es.enter_context(self.nc.semaphore(name))
        self.work_sems.append(sem)
        return sem

    def sb(self, name, shape, dt, es=None):
        self.uid = getattr(self, "uid", 0) + 1
        return (es or self.es).enter_context(self.nc.sbuf_tensor(f"{name}_{self.uid}", shape, dt))

    def dma(self, q, out, in_, ds, track=False):
        ds.cnt += 16
        q.raw.dma_start(out=out, in_=in_).then_inc(ds.sem, 16)
        tok = (ds.sem, ds.cnt)
        if track:
            self.pending.append(tok)
        return tok

    def barrier(self):
        best = {}
        for sem, v in [(e.sem, e.cnt) for e in self.engs if e.cnt > 0] + self.pending:
            if id(sem) not in best or best[id(sem)][1] < v:
                best[id(sem)] = (sem, v)
        toks = list(best.values())
        for e in self.engs:
            e.wait(*toks)
        self.pending = []


class Ring:
    def __init__(self, kb, name, n, kc):
        self.kb = kb
        self.n = n
        self.kc = kc
        self.name = name
        self.ds = [DSem(kb, f"{name}d{i}") for i in range(n)]
        self.bufs = None
        self.free = [None] * n
        self.i = 0

    def alloc(self, es):
        self.bufs = [self.kb.sb(f"{self.name}{i}", [128, self.kc, 128], BF16, es) for i in range(self.n)]
        self.free = [None] * self.n

    def load(self, src, kc):
        s = self.i % self.n
        self.i += 1
        kb = self.kb
        kb.pool.wait(self.free[s])
        tok = kb.dma(kb.pool, self.bufs[s][:, 0:kc, :], src, self.ds[s])
        return self.bufs[s], tok, s

    def release(self, s, tok):
        self.free[s] = tok


def pretile(w):
    K_, N_ = w.shape
    return np.ascontiguousarray(w.reshape(K_ // 128, 128, N_ // 128, 128).transpose(2, 1, 0, 3))


class Prog:
    def __init__(self, nseq, depth, mode="full"):
        self.nseq = nseq
        self.depth = depth
        self.mode = mode
        kb = self.kb = KB()
        nc = self.nc = kb.nc
        L = depth
        di = lambda name, shape, dt=F32: nc.dram_tensor(name, shape, dt, kind="ExternalInput").ap()
        self.xin = di("xin", [nseq, NCH, 128, S])
        self.yout = nc.dram_tensor("yout", [nseq, NCH, 128, S], F32, kind="ExternalOutput").ap()
        self.w_f1i = di("w_f1i", [L, 88, 128, 16, 128])
        self.w_f1o = di("w_f1o", [L, 16, 128, 44, 128])
        self.w_f2i = di("w_f2i", [L, 88, 128, 16, 128])
        self.w_f2o = di("w_f2o", [L, 16, 128, 44, 128])
        self.w_mix = di("w_mix", [L, NMIXB, 128, 16, 128])
        self.w_gate = di("w_gate", [L, 48, 128, 16, 128])
        self.w_br = di("w_br", [L, 16, 128, 16, 128])
        self.w_o = di("w_o", [L, 16, 128, 16, 128])
        self.gains = di("gains", [128, L * 6 * 16])
        self.gbias = di("gbias", [128, L * 3 * 16])
        self.qkg = di("qkg", [128, L * 2])
        self.onorm = di("onorm", [128, L])
        self.rpbT = di("rpbT", [L, 4, 128, 31 * 64])
        self.w2e = di("w2e", [L, 2, 33, 256])
        self.c_ones = di("c_ones", [128, 128])
        self.c_rotT = di("c_rotT", [128, 128])
        self.c_cos = di("c_cos", [128, S])
        self.c_sin = di("c_sin", [128, S])
        self.c_tri = di("c_tri", [4, 128, 128])
        self.c_rm = di("c_rm", [128, 28 * 8])
        dt_ = lambda name, shape: nc.dram_tensor(name, shape, BF16).ap()
        self.d_aq = dt_("d_aq", [nseq, 8, 128, S])
        self.d_ak = dt_("d_ak", [nseq, 2, 128, S])
        self.d_av = dt_("d_av", [nseq, 16, 128, 256])
        self.d_bq = dt_("d_bq", [nseq, 4, 128, S])
        self.d_bk = dt_("d_bk", [nseq, 4, 128, S])
        self.d_bv = dt_("d_bv", [nseq, 16, 128, 512])
        self.d_cq = dt_("d_cq", [nseq, 2, 128, S])
        self.d_ck = dt_("d_ck", [nseq, 2, 128, S])
        self.d_ckt = dt_("d_ckt", [nseq, 16, 128, 256])
        self.d_cv = dt_("d_cv", [nseq, 16, 128, 512])
        self.d_cog = dt_("d_cog", [nseq, 4, 128, S])
        self.d_clr = dt_("d_clr", [nseq, 32, S])
        self.d_om = dt_("d_om", [nseq, 16, 128, S])
        self.build()

    def gcol(self, l, i, c):
        return self.gain_sb[:, (l * 6 + i) * 16 + c:(l * 6 + i) * 16 + c + 1]

    def ghcol(self, l, i, c):
        return self.gainh_sb[:, (l * 6 + i) * 16 + c:(l * 6 + i) * 16 + c + 1]

    def rms_stats(self, src, nchunks, sqbuf, bank_idx, scale_n, rstd=None):
        kb = self.kb
        nc = self.nc
        rstd = self.rstd if rstd is None else rstd
        bank = self.ps[:, bank_idx, :]
        a = kb.act.ms(nc.scalar.activation(out=sqbuf[:, 0:nchunks, :], in_=src, func=AF.Square))
        kb.pe.wait(a, self.bank_free[bank_idx])
        for c in range(nchunks):
            ins = nc.tensor.matmul(bank, self.ones_bf[:], sqbuf[:, c, :], start=(c == 0), stop=(c == nchunks - 1))
        p = kb.pe.ms(ins)
        kb.act.wait(p, self.rstd_free)
        a2 = kb.act.ms(nc.scalar.activation(out=rstd[:], in_=bank, func=AF.Sqrt, bias=self.eps_col[:],
                                            scale=1.0 / scale_n))
        self.bank_free[bank_idx] = a2
        kb.dve.wait(a2)
        d = kb.dve.ms(nc.vector.reciprocal(out=rstd[:], in_=rstd[:]))
        return d

    def prenorm(self, l, gi):
        kb, nc = self.kb, self.nc
        kb.act.wait(self.x_ready, self.xn_free)
        self.rms_stats(self.xT[:], NCH, self.xnT, 6, D)
        kb.dve.wait(self.x_ready)
        for c in range(NCH):
            ins = nc.vector.scalar_tensor_tensor(out=self.xnT[:, c, :], in0=self.xT[:, c, :],
                                                 scalar=self.gcol(l, gi, c), in1=self.rstd[:],
                                                 op0=ALU.mult, op1=ALU.mult)
        t = kb.dve.ms(ins)
        self.rstd_free = t
        self.last_xn = t
        return t

    def postnorm_residual(self, l, gi, half):
        kb, nc = self.kb, self.nc
        gsel = self.ghcol if half else self.gcol
        kb.act.wait(self.xn_free)
        self.rms_stats(self.outT[:], NCH, self.xnT, 6, D)
        for c in range(NCH):
            nc.vector.scalar_tensor_tensor(out=self.outT[:, c, :], in0=self.outT[:, c, :], scalar=gsel(l, gi, c),
                                           in1=self.rstd[:], op0=ALU.mult, op1=ALU.mult)
            ins = nc.vector.tensor_tensor(out=self.xT[:, c, :], in0=self.xT[:, c, :], in1=self.outT[:, c, :],
                                          op=ALU.add)
        self.x_ready = kb.dve.ms(ins)
        self.out_free = self.x_ready
        self.rstd_free = self.x_ready
        self.xn_free = self.x_ready

    def ffn(self, l, which):
        kb, nc = self.kb, self.nc
        w_in = (self.w_f1i if which == 0 else self.w_f2i)[l]
        w_out = (self.w_f1o if which == 0 else self.w_f2o)[l]
        PS = self.ps
        xnT, actT, outT = self.xnT, self.actT, self.outT
        xn_ready = self.prenorm(l, 0 if which == 0 else 4)
        for j in range(DFF // 128):
            par = j % 2
            gb, ub = 2 * par, 2 * par + 1
            wg, tg, sg = self.ringA.load(w_in[j], 16)
            wu, tu, su = self.ringA.load(w_in[44 + j], 16)
            kb.pe.wait(xn_ready, tg, self.bank_free[gb])
            for c in range(NCH):
                ins = nc.tensor.matmul(PS[:, gb, :], wg[:, c, :], xnT[:, c, :], start=(c == 0), stop=(c == NCH - 1))
            pg = kb.pe.ms(ins)
            self.ringA.release(sg, pg)
            kb.pe.wait(tu, self.bank_free[ub])
            for c in range(NCH):
                ins = nc.tensor.matmul(PS[:, ub, :], wu[:, c, :], xnT[:, c, :], start=(c == 0), stop=(c == NCH - 1))
            pu = kb.pe.ms(ins)
            self.ringA.release(su, pu)
            kb.act.wait(pg, self.sg_free[par])
            a = kb.act.ms(nc.scalar.activation(out=self.sgbuf[:, par, :], in_=PS[:, gb, :], func=AF.Silu))
            self.bank_free[gb] = a
            kb.dve.wait(a, pu, self.act_free)
            dd = kb.dve.ms(nc.vector.tensor_tensor(out=actT[:, j, :], in0=self.sgbuf[:, par, :], in1=PS[:, ub, :],
                                                   op=ALU.mult))
            self.bank_free[ub] = dd
            self.sg_free[par] = dd
        act_ready = dd
        self.xn_free = pu
        for n in range(NCH):
            bk = 4 + n % 2
            wo, to, so = self.ringB.load(w_out[n], 44)
            kb.pe.wait(act_ready, to, self.bank_free[bk])
            for f in range(44):
                ins = nc.tensor.matmul(PS[:, bk, :], wo[:, f, :], actT[:, f, :], start=(f == 0), stop=(f == 43))
            p = kb.pe.ms(ins)
            self.ringB.release(so, p)
            kb.act.wait(p, self.out_free)
            a = kb.act.ms(nc.scalar.copy(out=outT[:, n, :], in_=PS[:, bk, :]))
            self.bank_free[bk] = a
        self.act_free = p
        self.postnorm_residual(l, 1 if which == 0 else 5, True)

    def stage_out(self, src_tok, dst_ap, src_ap, si):
        kb = self.kb
        kb.sp.wait(src_tok)
        return kb.dma(kb.sp, dst_ap, src_ap, self.stg_ds[si], track=True)

    def get_stage(self):
        i = self.stg_i % 4
        self.stg_i += 1
        return i

    def proj(self, l, s, tt):
        kb, nc = self.kb, self.nc
        PS = self.ps
        xnT = self.xnT
        tsl = slice(tt * T, (tt + 1) * T)
        xn_ready = self.prenorm(l, 2)
        kb.sp.wait(self.cs_free)
        tc1 = kb.dma(kb.sp, self.cosb[:], self.c_cos[:, tsl], self.cs_ds)
        tc2 = kb.dma(kb.sp, self.sinb[:], self.c_sin[:, tsl], self.cs_ds)
        plan = []
        for h in range(8):
            plan.append((h, "rope", self.d_aq[s, h, :, tsl], 0))
        for h in range(2):
            plan.append((8 + h, "rope", self.d_ak[s, h, :, tsl], 1))
        for i in range(2):
            plan.append((10 + i, "tok", self.d_av[s, tt * 4:(tt + 1) * 4, :, i * 128:(i + 1) * 128], None))
        for i in range(4):
            plan.append((12 + i, "copy", self.d_bq[s, i, :, tsl], 1.0))
        for i in range(4):
            plan.append((16 + i, "copy", self.d_bk[s, i, :, tsl], 1.0))
        for i in range(4):
            plan.append((20 + i, "tok", self.d_bv[s, tt * 4:(tt + 1) * 4, :, i * 128:(i + 1) * 128], None))
        for i in range(2):
            plan.append((24 + i, "copy", self.d_cq[s, i, :, tsl], 0.125))
        for i in range(2):
            plan.append((26 + i, "copy", self.d_ck[s, i, :, tsl], 1.0))
        for i in range(2):
            plan.append((26 + i, "tok", self.d_ckt[s, tt * 4:(tt + 1) * 4, :, i * 128:(i + 1) * 128], None))
        for i in range(4):
            plan.append((28 + i, "tok", self.d_cv[s, tt * 4:(tt + 1) * 4, :, i * 128:(i + 1) * 128], None))
        for i in range(4):
            plan.append((32 + i, "copy", self.d_cog[s, i, :, tsl], 1.0))
        plan.append((36, "lr", self.d_clr[s, :, tsl], 1.0))
        last_dve = None
        for it, (blk, kind, dst, par_) in enumerate(plan):
            bank = it % 4
            w, tw, sw = self.ringA.load(self.w_mix[l, blk], 16)
            kb.pe.wait(xn_ready, tw, self.bank_free[bank])
            if kind == "tok":
                for sub in range(4):
                    for c in range(NCH):
                        ins = nc.tensor.matmul(PS[:, bank, sub * 128:(sub + 1) * 128],
                                               xnT[:, c, sub * 128:(sub + 1) * 128], w[:, c, :],
                                               start=(c == 0), stop=(c == NCH - 1))
            else:
                for c in range(NCH):
                    ins = nc.tensor.matmul(PS[:, bank, :], w[:, c, :], xnT[:, c, :], start=(c == 0),
                                           stop=(c == NCH - 1))
            p = kb.pe.ms(ins)
            self.ringA.release(sw, p)
            si = self.get_stage()
            stg = self.stg[:, si, :]
            if kind in ("copy", "tok", "lr"):
                kb.act.wait(p, self.stg_free[si])
                if kind == "copy" and par_ != 1.0:
                    a = kb.act.ms(nc.scalar.mul(out=stg, in_=PS[:, bank, :], mul=float(par_)))
                else:
                    a = kb.act.ms(nc.scalar.copy(out=stg, in_=PS[:, bank, :]))
                self.bank_free[bank] = a
                if kind == "tok":
                    st = self.stage_out(a, dst.rearrange("k p n -> p k n"),
                                        self.stg[:, si, :].rearrange("p (k n) -> p k n", k=4), si)
                elif kind == "lr":
                    st = self.stage_out(a, dst, self.stg[0:32, si, :], si)
                else:
                    st = self.stage_out(a, dst, stg, si)
                self.stg_free[si] = st
            else:
                gcol = self.qkg_sb[:, l * 2 + par_:l * 2 + par_ + 1]
                kb.act.wait(p, self.sq1_free)
                a = kb.act.ms(nc.scalar.activation(out=self.sq1[:, 0, :], in_=PS[:, bank, :], func=AF.Square))
                kb.pe.wait(a, self.bank_free[5])
                p2 = kb.pe.ms(nc.tensor.matmul(PS[:, 5, :], self.ones_bf[:], self.sq1[:, 0, :], start=True, stop=True))
                self.sq1_free = p2
                kb.act.wait(p2, self.rq_free)
                a2 = kb.act.ms(nc.scalar.activation(out=self.rq[:], in_=PS[:, 5, :], func=AF.Sqrt,
                                                    bias=self.eps_col[:], scale=1.0 / 128))
                self.bank_free[5] = a2
                kb.dve.wait(a2, p, self.qn_free)
                nc.vector.reciprocal(out=self.rq[:], in_=self.rq[:])
                d1 = kb.dve.ms(nc.vector.scalar_tensor_tensor(out=self.qn[:], in0=PS[:, bank, :], scalar=gcol,
                                                              in1=self.rq[:], op0=ALU.mult, op1=ALU.mult))
                self.bank_free[bank] = d1
                self.rq_free = d1
                kb.pe.wait(d1, self.bank_free[4])
                p3 = kb.pe.ms(nc.tensor.matmul(PS[:, 4, :], self.rotT_bf[:], self.qn[:], start=True, stop=True))
                kb.dve.wait(p3, tc1, tc2, self.stg_free[si])
                nc.vector.tensor_tensor(out=self.t1[:], in0=self.qn[:], in1=self.cosb[:], op=ALU.mult)
                nc.vector.tensor_tensor(out=self.t2[:], in0=PS[:, 4, :], in1=self.sinb[:], op=ALU.mult)
                d2 = kb.dve.ms(nc.vector.tensor_tensor(out=stg, in0=self.t1[:], in1=self.t2[:], op=ALU.add))
                self.bank_free[4] = d2
                self.qn_free = d2
                last_dve = d2
                st = self.stage_out(d2, dst, stg, si)
                self.stg_free[si] = st
        self.cs_free = last_dve
        self.xn_free = p

    def merge(self, l, s, tt):
        kb, nc = self.kb, self.nc
        PS = self.ps
        xnT, actT, outT = self.xnT, self.actT, self.outT
        tsl = slice(tt * T, (tt + 1) * T)
        omT = actT[:, 0:16, :]
        mgT = actT[:, 16:32, :]
        kb.sp.wait(self.act_free)
        t_om = kb.dma(kb.sp, omT, self.d_om[s, :, :, tsl].rearrange("c p t -> p c t"), self.om_ds)
        xn_ready = self.prenorm(l, 2)
        segs = [(0, 8), (8, 12), (12, 16)]
        for n in range(NCH):
            wgs = [self.ringA.load(self.w_gate[l, br * 16 + n], 16) for br in range(3)]
            wb, tb, sbr = self.ringA.load(self.w_br[l, n], 16)
            for br in range(3):
                w, tw, sw = wgs[br]
                kb.pe.wait(xn_ready, tw, self.bank_free[br])
                for c in range(NCH):
                    ins = nc.tensor.matmul(PS[:, br, :], w[:, c, :], xnT[:, c, :], start=(c == 0), stop=(c == NCH - 1))
                pg = kb.pe.ms(ins)
                self.ringA.release(sw, pg)
                kb.act.wait(pg, self.sig_free[br])
                bcol = self.gb_sb[:, (l * 3 + br) * 16 + n:(l * 3 + br) * 16 + n + 1]
                a = kb.act.ms(nc.scalar.activation(out=self.sig[:, br, :], in_=PS[:, br, :], func=AF.Sigmoid,
                                                   bias=bcol, scale=1.0))
                self.bank_free[br] = a
                wgs[br] = a
            kb.pe.wait(tb, t_om)
            for br in range(3):
                c0, c1 = segs[br]
                kb.pe.wait(self.bank_free[3 + br])
                for c in range(c0, c1):
                    ins = nc.tensor.matmul(PS[:, 3 + br, :], wb[:, c, :], omT[:, c, :], start=(c == c0),
                                           stop=(c == c1 - 1))
            py = kb.pe.ms(ins)
            self.ringA.release(sbr, py)
            kb.dve.wait(py, wgs[0], wgs[1], wgs[2], self.mg_free)
            nc.vector.tensor_tensor(out=self.macc[:], in0=self.sig[:, 0, :], in1=PS[:, 3, :], op=ALU.mult)
            nc.vector.tensor_tensor(out=self.t1[:], in0=self.sig[:, 1, :], in1=PS[:, 4, :], op=ALU.mult)
            nc.vector.tensor_tensor(out=self.macc[:], in0=self.macc[:], in1=self.t1[:], op=ALU.add)
            nc.vector.tensor_tensor(out=self.t1[:], in0=self.sig[:, 2, :], in1=PS[:, 5, :], op=ALU.mult)
            d = kb.dve.ms(nc.vector.tensor_tensor(out=mgT[:, n, :], in0=self.macc[:], in1=self.t1[:], op=ALU.add))
            for br in range(3):
                self.bank_free[3 + br] = d
                self.sig_free[br] = d
        mg_ready = d
        self.xn_free = pg
        for n in range(NCH):
            bk = 6 + n % 2
            w, tw, sw = self.ringA.load(self.w_o[l, n], 16)
            kb.pe.wait(mg_ready, tw, self.bank_free[bk])
            for c in range(NCH):
                ins = nc.tensor.matmul(PS[:, bk, :], w[:, c, :], mgT[:, c, :], start=(c == 0), stop=(c == NCH - 1))
            p = kb.pe.ms(ins)
            self.ringA.release(sw, p)
            kb.act.wait(p, self.out_free)
            a = kb.act.ms(nc.scalar.copy(out=outT[:, n, :], in_=PS[:, bk, :]))
            self.bank_free[bk] = a
        self.act_free = p
        self.mg_free = p
        self.postnorm_residual(l, 3, False)

    def tl_phase(self, l):
        kb, nc = self.kb, self.nc
        L = self.depth
        with ExitStack() as es:
            self.xT = kb.sb("xT", [128, NCH, T], F32, es)
            self.xnT = kb.sb("xnT", [128, NCH, T], BF16, es)
            self.actT = kb.sb("actT", [128, 44, T], BF16, es)
            self.outT = kb.sb("outT", [128, NCH, T], F32, es)
            self.sgbuf = kb.sb("sgbuf", [128, 2, T], F32, es)
            self.stg = kb.sb("stg", [128, 4, T], BF16, es)
            self.cosb = kb.sb("cosb", [128, T], F32, es)
            self.sinb = kb.sb("sinb", [128, T], F32, es)
            self.sq1 = kb.sb("sq1", [128, 1, T], BF16, es)
            self.rq = kb.sb("rq", [128, T], F32, es)
            self.qn = kb.sb("qn", [128, T], BF16, es)
            self.t1 = kb.sb("t1", [128, T], F32, es)
            self.t2 = kb.sb("t2", [128, T], F32, es)
            self.macc = kb.sb("macc", [128, T], F32, es)
            self.sig = kb.sb("sig", [128, 3, T], F32, es)
            self.ringA.alloc(es)
            self.ringB.alloc(es)
            self.sg_free = [None, None]
            self.sig_free = [None, None, None]
            self.stg_free = [None] * 4
            self.act_free = self.out_free = self.xn_free = self.rstd_free = None
            self.sq1_free = self.rq_free = self.qn_free = self.cs_free = self.mg_free = None
            self.x_free = None
            self.last_xn = None
            self.bank_free = [None] * 8
            for s in range(self.nseq):
                for tt in range(NTT):
                    tsl = slice(tt * T, (tt + 1) * T)
                    src = self.xin if l == 0 else self.yout
                    kb.sp.wait(self.x_free, self.last_xn)
                    self.x_ready = kb.dma(kb.sp, self.xT[:], src[s, :, :, tsl].rearrange("c p t -> p c t"), self.x_ds)
                    if l > 0:
                        self.merge(l - 1, s, tt)
                        self.ffn(l - 1, 1)
                    if l < L:
                        self.ffn(l, 0)
                        self.proj(l, s, tt)
                    kb.sp.wait(self.x_ready)
                    self.x_free = kb.dma(kb.sp, self.yout[s, :, :, tsl].rearrange("c p t -> p c t"), self.xT[:],
                                         self.y_ds, track=True)
            kb.barrier()

    def attn_block(self, kT, qrhs, vfn, kcs, scale, maskfn, dst):
        kb, nc = self.kb, self.nc
        PS = self.ps
        LA = 2
        blk = self.ablk
        self.ablk += 1
        ob, db = 4 + blk % 2, 6 + blk % 2
        n = len(kcs)
        qk_tok = [None] * n
        qk_bank = [None] * n
        pv = None
        for i in range(n + LA):
            if i < n:
                bank = self.sbank % 4
                self.sbank += 1
                kb.pe.wait(self.bank_free[bank])
                qk_tok[i] = kb.pe.ms(nc.tensor.matmul(PS[:, bank, :], kT(kcs[i]), qrhs, start=True, stop=True))
                qk_bank[i] = bank
            j = i - LA
            if j >= 0:
                slot = self.pslot % 4
                self.pslot += 1
                pb = self.pbuf[:, slot, :]
                kb.act.wait(qk_tok[j], self.pbuf_free[slot])
                a = kb.act.ms(nc.scalar.activation(out=pb, in_=PS[:, qk_bank[j], :], func=AF.Exp, scale=float(scale)))
                self.bank_free[qk_bank[j]] = a
                ptok = a
                if maskfn is not None:
                    ge, rm = maskfn(kcs[j])
                    kb.dve.wait(a)
                    pb3 = pb.rearrange("p (r c) -> p r c", r=8)
                    nc.vector.tensor_tensor(out=pb3, in0=pb3, in1=ge, op=ALU.mult)
                    ptok = kb.dve.ms(nc.vector.tensor_tensor(out=pb3, in0=pb3, in1=rm, op=ALU.mult))
                kb.pe.wait(ptok)
                if j == 0:
                    kb.pe.wait(self.bank_free[ob], self.bank_free[db])
                nc.tensor.matmul(PS[:, ob, :], vfn(kcs[j]), pb, start=(j == 0), stop=(j == n - 1))
                pv = kb.pe.ms(nc.tensor.matmul(PS[:, db, :], self.ones_bf[:], pb, start=(j == 0), stop=(j == n - 1)))
                self.pbuf_free[slot] = pv
        so = blk % 2
        kb.dve.wait(pv, self.ost_free[so])
        nc.vector.reciprocal(out=self.rden[:], in_=PS[:, db, :])
        d = kb.dve.ms(nc.vector.tensor_tensor(out=self.ost[:, so, :], in0=PS[:, ob, :], in1=self.rden[:], op=ALU.mult))
        self.bank_free[ob] = d
        self.bank_free[db] = d
        kb.sp.wait(d)
        self.ost_free[so] = kb.dma(kb.sp, dst, self.ost[:, so, :], self.ost_ds[so], track=True)

    def attn_common_alloc(self, es):
        kb = self.kb
        self.pbuf = kb.sb("pbuf", [128, 4, 512], BF16, es)
        self.ost = kb.sb("ost", [128, 2, 512], BF16, es)
        self.rden = kb.sb("rden", [128, 512], F32, es)
        self.pbuf_free = [None] * 4
        self.ost_free = [None, None]
        self.bank_free = [None] * 8
        self.ablk = 0
        self.sbank = 0
        self.pslot = 0

    def mixer_a(self, l, s):
        kb, nc = self.kb, self.nc
        with ExitStack() as es:
            q = kb.sb("a_q", [128, 8, S], BF16, es)
            k = kb.sb("a_k", [128, 2, S], BF16, es)
            v = kb.sb("a_v", [128, 16, 256], BF16, es)
            self.attn_common_alloc(es)
            toks = [kb.dma(kb.sp, k[:], self.d_ak[s].rearrange("c p t -> p c t"), self.ld_ds),
                    kb.dma(kb.sp, v[:], self.d_av[s].rearrange("k p n -> p k n"), self.ld_ds)]
            for h in range(8):
                toks.append(kb.dma(kb.sp, q[:, h, :], self.d_aq[s, h], self.ld_ds))
            kb.pe.wait(toks)
            for h in range(8):
                kv = h // 4
                for qt in range(4):
                    self.attn_block(lambda kc: k[:, kv, kc * 128:(kc + 1) * 128], q[:, h, qt * 512:(qt + 1) * 512],
                                    lambda kc: v[:, kc, kv * 128:(kv + 1) * 128], list(range(16)),
                                    128 ** -0.5, None, self.d_om[s, h, :, qt * 512:(qt + 1) * 512])
            kb.barrier()

    def mixer_b(self, l, s):
        kb, nc = self.kb, self.nc
        with ExitStack() as es:
            q = kb.sb("b_q", [128, 4, S], BF16, es)
            k = kb.sb("b_k", [128, 4, S], BF16, es)
            v = kb.sb("b_v", [128, 16, 512], BF16, es)
            ge = kb.sb("b_ge", [128, 4, 31 * 64], BF16, es)
            gst = kb.sb("b_gst", [128, 31 * 64], F32, es)
            self.attn_common_alloc(es)
            toks = [kb.dma(kb.sp, q[:], self.d_bq[s].rearrange("c p t -> p c t"), self.ld_ds),
                    kb.dma(kb.sp, k[:], self.d_bk[s].rearrange("c p t -> p c t"), self.ld_ds),
                    kb.dma(kb.sp, v[:], self.d_bv[s].rearrange("k p n -> p k n"), self.ld_ds)]
            gfree = None
            for h in range(4):
                kb.sp.wait(gfree)
                tg = kb.dma(kb.sp, gst[:], self.rpbT[l, h], self.ld_ds)
                kb.act.wait(tg)
                gfree = kb.act.ms(nc.scalar.activation(out=ge[:, h, :], in_=gst[:], func=AF.Exp))
            kb.dve.wait(gfree)
            kb.pe.wait(toks)
            KCS = [list(range(0, 6)), list(range(2, 10)), list(range(6, 14)), list(range(10, 16))]
            pair_idx = {}
            pi = 0
            for qt in range(4):
                for kc in KCS[qt]:
                    pair_idx[(qt, kc)] = pi
                    pi += 1
            for h in range(4):
                for qt in range(4):
                    def maskfn(kc, h=h, qt=qt):
                        a0 = 8 * qt - 2 * kc + 7 + 8
                        g = ge[:, h, a0 * 64:(a0 + 8) * 64].rearrange("p (r c) -> p r c", r=8)
                        pidx = pair_idx[(qt, kc)]
                        rm = self.rm_bf[:, pidx * 8:(pidx + 1) * 8, None].broadcast_to([128, 8, 64])
                        return g, rm
                    self.attn_block(lambda kc: k[:, h, kc * 128:(kc + 1) * 128], q[:, h, qt * 512:(qt + 1) * 512],
                                    lambda kc: v[:, kc, h * 128:(h + 1) * 128], KCS[qt],
                                    128 ** -0.5, maskfn, self.d_om[s, 8 + h, :, qt * 512:(qt + 1) * 512])
            kb.barrier()

    def mixer_c(self, l, s):
        kb, nc = self.kb, self.nc
        PS = self.ps
        NC_ = 16
        with ExitStack() as es:
            q = kb.sb("c_q", [128, 2, S], BF16, es)
            k = kb.sb("c_k", [128, 2, S], BF16, es)
            kt = kb.sb("c_kt", [128, 16, 256], BF16, es)
            v = kb.sb("c_v", [128, 16, 512], BF16, es)
            lr = kb.sb("c_lr", [33, S], BF16, es)
            w2f = kb.sb("c_w2f", [33, 2, 256], F32, es)
            w2 = kb.sb("c_w2", [33, 2, 256], BF16, es)
            G = kb.sb("c_G", [128, 16, 2, 256], F32, es)
            etmp = kb.sb("c_etmp", [128, 2, 512], F32, es)
            ebuf = etmp[:, 0, :]
            qf = kb.sb("c_qf", [128, 2, S], BF16, es)
            qb = kb.sb("c_qb", [128, 2, S], BF16, es)
            kf = kb.sb("c_kf", [128, 4, S], BF16, es)
            kbw = kb.sb("c_kb", [128, 4, S], BF16, es)
            kdf = kb.sb("c_kdf", [128, 16, 256], BF16, es)
            kdb = kb.sb("c_kdb", [128, 16, 256], BF16, es)
            decf = kb.sb("c_decf", [128, 2, 16], F32, es)
            decb = kb.sb("c_decb", [128, 2, 16], F32, es)
            Sf = kb.sb("c_Sf", [128, 2, 128], F32, es)
            Sb = kb.sb("c_Sb", [128, 2, 128], F32, es)
            Sfa = kb.sb("c_Sfa", [128, 16, 4, 128], BF16, es)
            Sba = kb.sb("c_Sba", [128, 16, 4, 128], BF16, es)
            attn = kb.sb("c_attn", [128, 2, 4, 128], BF16, es)
            atmp = kb.sb("c_atmp", [128, 4, 128], F32, es)
            ogb = kb.sb("c_og", [128, 2, 512], BF16, es)
            sq = kb.sb("c_sq", [128, 1, 512], BF16, es)
            rs = kb.sb("c_rs", [128, 512], F32, es)
            on = kb.sb("c_on", [128, 512], F32, es)
            sgo = kb.sb("c_sgo", [128, 512], F32, es)
            ost = kb.sb("c_ost", [128, 2, 512], BF16, es)
            self.bank_free = [None] * 8
            self.rstd_free = None
            ld = [kb.dma(kb.sp, q[:], self.d_cq[s].rearrange("c p t -> p c t"), self.ld_ds),
                  kb.dma(kb.sp, k[:], self.d_ck[s].rearrange("c p t -> p c t"), self.ld_ds),
                  kb.dma(kb.sp, kt[:], self.d_ckt[s].rearrange("k p n -> p k n"), self.ld_ds),
                  kb.dma(kb.sp, v[:], self.d_cv[s].rearrange("k p n -> p k n"), self.ld_ds),
                  kb.dma(kb.sp, lr[0:32, :], self.d_clr[s], self.ld_ds),
                  kb.dma(kb.sp, w2f[:], self.w2e[l].rearrange("d k n -> k d n"), self.ld_ds)]
            for e in (kb.pe, kb.act, kb.dve):
                e.wait(ld)
            nc.vector.memset(lr[32:33, :], 1.0)
            nc.vector.memset(Sf[:], 0.0)
            nc.vector.memset(Sb[:], 0.0)
            nc.vector.memset(kf[:], 0.0)
            nc.vector.memset(kbw[:], 0.0)
            nc.vector.memset(Sfa[:], 0.0)
            nc.vector.memset(Sba[:], 0.0)
            d0 = kb.dve.ms(nc.vector.tensor_copy(out=w2[:], in_=w2f[:]))
            kb.pe.wait(d0)
            for n in range(NC_):
                bank = n % 2
                kb.pe.wait(self.bank_free[bank])
                for dr in range(2):
                    ins = nc.tensor.matmul(PS[:, bank, dr * 256:(dr + 1) * 256], lr[0:33, n * 128:(n + 1) * 128],
                                           w2[0:33, dr, :], start=True, stop=True)
                p = kb.pe.ms(ins)
                kb.act.wait(p)
                nc.scalar.activation(out=ebuf, in_=PS[:, bank, :], func=AF.Exp, scale=-1.0)
                a = kb.act.ms(nc.scalar.activation(out=G[:, n, :, :].rearrange("p d n -> p (d n)"), in_=ebuf,
                                                   func=AF.Ln, bias=self.one_col[:], scale=1.0))
                self.bank_free[bank] = a
            g_ready = a
            CS = 9
            if CS <= 1:
                kb.barrier(); return
            kb.pe.wait(g_ready)
            for ft in range(2):
                for tt in range(4):
                    tsl = slice(tt * 512, (tt + 1) * 512)
                    kb.pe.wait(self.bank_free[0], self.bank_free[1])
                    for dr in range(2):
                        for cc in range(4):
                            n = tt * 4 + cc
                            ins = nc.tensor.matmul(PS[:, dr, cc * 128:(cc + 1) * 128],
                                                   G[:, n, dr, ft * 128:(ft + 1) * 128], self.tri_f[:, 2 * dr, :],
                                                   start=True, stop=True)
                    p = kb.pe.ms(ins)
                    kb.act.wait(p, self.bank_free[2])
                    nc.scalar.activation(out=etmp[:, 0, :], in_=PS[:, 0, :], func=AF.Exp, scale=-1.0 / 16)
                    a1 = kb.act.ms(nc.scalar.activation(out=etmp[:, 1, :], in_=PS[:, 0, :], func=AF.Exp, scale=1.0 / 16))
                    kb.dve.wait(a1)
                    nc.vector.tensor_tensor(out=qf[:, ft, tsl], in0=q[:, ft, tsl], in1=etmp[:, 0, :], op=ALU.mult)
                    for hp in range(2):
                        r0 = hp * 64
                        nc.vector.tensor_tensor(out=kf[r0:r0 + 64, 2 * ft + hp, tsl], in0=k[r0:r0 + 64, ft, tsl],
                                                in1=etmp[r0:r0 + 64, 1, :], op=ALU.mult)
                    d1 = kb.dve.ms(nc.vector.tensor_copy(
                        out=decf[:, ft, tt * 4:(tt + 1) * 4],
                        in_=etmp[:, 0, :].rearrange("p (c t) -> p c t", c=4)[:, :, 127]))
                    kb.act.wait(d1)
                    nc.scalar.activation(out=etmp[:, 0, :], in_=PS[:, 1, :], func=AF.Exp, scale=-1.0 / 16)
                    a2 = kb.act.ms(nc.scalar.activation(out=etmp[:, 1, :], in_=PS[:, 1, :], func=AF.Exp, scale=1.0 / 16))
                    self.bank_free[0] = a2
                    self.bank_free[1] = a2
                    kb.dve.wait(a2)
                    nc.vector.tensor_tensor(out=qb[:, ft, tsl], in0=q[:, ft, tsl], in1=etmp[:, 0, :], op=ALU.mult)
                    for hp in range(2):
                        r0 = hp * 64
                        nc.vector.tensor_tensor(out=kbw[r0:r0 + 64, 2 * ft + hp, tsl], in0=k[r0:r0 + 64, ft, tsl],
                                                in1=etmp[r0:r0 + 64, 1, :], op=ALU.mult)
                    d2 = kb.dve.ms(nc.vector.tensor_copy(
                        out=decb[:, ft, tt * 4:(tt + 1) * 4],
                        in_=etmp[:, 0, :].rearrange("p (c t) -> p c t", c=4)[:, :, 0]))
                    kb.act.wait(d2)
            if CS <= 2:
                kb.barrier(); return
            for n in range(NC_):
                bank = 2 + n % 2
                kb.pe.wait(self.bank_free[bank])
                nc.tensor.matmul(PS[:, bank, 0:256], self.tri_f[:, 1, :], G[:, n, 0, :], start=True, stop=True)
                p = kb.pe.ms(nc.tensor.matmul(PS[:, bank, 256:512], self.tri_f[:, 3, :], G[:, n, 1, :], start=True,
                                              stop=True))
                kb.act.wait(p, d2)
                a = kb.act.ms(nc.scalar.activation(out=etmp[:, n % 2, :], in_=PS[:, bank, :], func=AF.Exp,
                                                   scale=-1.0 / 16))
                self.bank_free[bank] = a
                kb.dve.wait(a)
                nc.vector.tensor_tensor(out=kdf[:, n, :], in0=kt[:, n, :], in1=etmp[:, n % 2, 0:256], op=ALU.mult)
                d2 = kb.dve.ms(nc.vector.tensor_tensor(out=kdb[:, n, :], in0=kt[:, n, :], in1=etmp[:, n % 2, 256:512],
                                                       op=ALU.mult))
            kd_ready = d2
            if CS <= 3:
                kb.barrier(); return
            kb.pe.wait(kd_ready)
            for step in range(NC_):
                for dr, (Sx, Sall, kd, dec) in enumerate(((Sf, Sfa, kdf, decf), (Sb, Sba, kdb, decb))):
                    n = step if dr == 0 else NC_ - 1 - step
                    nc.vector.tensor_copy(out=Sall[0:64, n, 0:4:2, :], in_=Sx[0:64, :, :])
                    dsn = kb.dve.ms(nc.vector.tensor_copy(out=Sall[64:128, n, 1:4:2, :], in_=Sx[64:128, :, :]))
                    if step == NC_ - 1:
                        continue
                    bank = 4 + dr
                    kb.pe.wait(self.bank_free[bank])
                    for h in range(4):
                        ft = h // 2
                        ins = nc.tensor.matmul(PS[:, bank, h * 128:(h + 1) * 128], kd[:, n, ft * 128:(ft + 1) * 128],
                                               v[:, n, h * 128:(h + 1) * 128], start=True, stop=True)
                    p = kb.pe.ms(ins)
                    kb.dve.wait(p)
                    for h in range(4):
                        ft, r0 = h // 2, (h % 2) * 64
                        ins = nc.vector.scalar_tensor_tensor(
                            out=Sx[r0:r0 + 64, ft, :], in0=Sx[r0:r0 + 64, ft, :], scalar=dec[r0:r0 + 64, ft, n:n + 1],
                            in1=PS[r0:r0 + 64, bank, h * 128:(h + 1) * 128], op0=ALU.mult, op1=ALU.add)
                    self.bank_free[bank] = kb.dve.ms(ins)
            st_ready = (kb.dve.sem, kb.dve.cnt)
            if CS <= 4:
                kb.barrier(); return
            MF = self.tri_f[:, 0, None, :].broadcast_to([128, 4, 128])
            MB = self.tri_f[:, 2, None, :].broadcast_to([128, 4, 128])
            kb.pe.wait(st_ready)
            og_ds_free = [None, None]
            ost_free = [None, None]
            attn_free = [None, None]
            it = 0
            for tt in range(4):
                tsl = slice(tt * 512, (tt + 1) * 512)
                for h in range(4):
                    kb.pe.wait(self.bank_free[4 + h])
                for cc in range(4):
                    n = tt * 4 + cc
                    csl = slice(n * 128, (n + 1) * 128)
                    ap_ = it % 2
                    it += 1
                    kb.pe.wait(self.bank_free[0], self.bank_free[1])
                    for h in range(4):
                        ft, r0 = h // 2, (h % 2) * 64
                        nc.tensor.matmul(PS[:, 0, h * 128:(h + 1) * 128], kf[:, h, csl],
                                         qf[:, ft, csl], start=True, stop=True)
                        ins = nc.tensor.matmul(PS[:, 1, h * 128:(h + 1) * 128], kbw[:, h, csl],
                                               qb[:, ft, csl], start=True, stop=True)
                    p = kb.pe.ms(ins)
                    kb.dve.wait(p, attn_free[ap_])
                    nc.vector.tensor_tensor(out=atmp[:], in0=PS[:, 0, :].rearrange("p (h t) -> p h t", h=4), in1=MF,
                                            op=ALU.mult)
                    nc.vector.tensor_tensor(out=attn[:, ap_, :, :], in0=PS[:, 1, :].rearrange("p (h t) -> p h t", h=4),
                                            in1=MB, op=ALU.mult)
                    d = kb.dve.ms(nc.vector.tensor_tensor(out=attn[:, ap_, :, :], in0=attn[:, ap_, :, :], in1=atmp[:],
                                                          op=ALU.add))
                    self.bank_free[0] = d
                    self.bank_free[1] = d
                    kb.pe.wait(d)
                    for h in range(4):
                        ft, r0 = h // 2, (h % 2) * 64
                        ob = PS[:, 4 + h, cc * 128:(cc + 1) * 128]
                        nc.tensor.matmul(ob, v[:, n, h * 128:(h + 1) * 128], attn[:, ap_, h, :], start=True, stop=False)
                        nc.tensor.matmul(ob, Sfa[:, n, h, :], qf[:, ft, csl], start=False, stop=False)
                        ins = nc.tensor.matmul(ob, Sba[:, n, h, :], qb[:, ft, csl], start=False, stop=True)
                    attn_free[ap_] = kb.pe.ms(ins)
                o_ready = attn_free[(it - 1) % 2]
                for h in range(4 if CS > 5 else 0):
                    so = h % 2
                    kb.sp.wait(og_ds_free[so])
                    tog = kb.dma(kb.sp, ogb[:, so, :], self.d_cog[s, h, :, tsl], self.og_ds[so])
                    kb.act.wait(o_ready)
                    self.rms_stats(PS[:, 4 + h, :].rearrange("p (c t) -> p c t", c=1), 1, sq, 2, 128, rstd=rs)
                    kb.act.wait(tog)
                    a = kb.act.ms(nc.scalar.activation(out=sgo[:], in_=ogb[:, so, :], func=AF.Silu))
                    og_ds_free[so] = a
                    nc.vector.scalar_tensor_tensor(out=on[:], in0=PS[:, 4 + h, :], scalar=self.onorm_sb[:, l:l + 1],
                                                   in1=rs[:], op0=ALU.mult, op1=ALU.mult)
                    kb.dve.wait(a, ost_free[so])
                    d = kb.dve.ms(nc.vector.tensor_tensor(out=ost[:, so, :], in0=on[:], in1=sgo[:], op=ALU.mult))
                    self.bank_free[4 + h] = d
                    self.rstd_free = d
                    kb.act.wait(d)
                    kb.sp.wait(d)
                    ost_free[so] = kb.dma(kb.sp, self.d_om[s, 12 + h, :, tsl], ost[:, so, :], self.ost_ds[so],
                                          track=True)
            kb.barrier()

    def build(self):
        kb, nc = self.kb, self.nc
        L = self.depth
        sb = kb.sb
        self.gain_sb = sb("gain_sb", [128, L * 6 * 16], F32)
        self.gainh_sb = sb("gainh_sb", [128, L * 6 * 16], F32)
        self.gb_sb = sb("gb_sb", [128, L * 3 * 16], F32)
        self.qkg_sb = sb("qkg_sb", [128, L * 2], F32)
        self.onorm_sb = sb("onorm_sb", [128, L], F32)
        self.ones_f = sb("ones_f", [128, 128], F32)
        self.ones_bf = sb("ones_bf", [128, 128], BF16)
        self.rot_f = sb("rot_f", [128, 128], F32)
        self.rotT_bf = sb("rotT_bf", [128, 128], BF16)
        self.tri_f = sb("tri_f", [128, 4, 128], F32)
        self.rm_f = sb("rm_f", [128, 28 * 8], F32)
        self.rm_bf = sb("rm_bf", [128, 28 * 8], BF16)
        self.rstd = sb("rstd", [128, T], F32)
        self.eps_col = sb("eps_col", [128, 1], F32)
        self.one_col = sb("one_col", [128, 1], F32)
        self.ps = kb.es.enter_context(nc.psum_tensor("ps", [128, 8, 512], F32))
        self.ringA = Ring(kb, "rA", 5, 16)
        self.ringB = Ring(kb, "rB", 2, 44)
        self.bank_free = [None] * 8
        self.rstd_free = None
        self.x_ds = DSem(kb, "xld")
        self.y_ds = DSem(kb, "yst")
        self.stg_ds = [DSem(kb, f"stg{i}") for i in range(4)]
        self.cs_ds = DSem(kb, "csld")
        self.om_ds = DSem(kb, "omld")
        self.ld_ds = DSem(kb, "mixld")
        self.ost_ds = [DSem(kb, "ost0"), DSem(kb, "ost1")]
        self.og_ds = [DSem(kb, "og0"), DSem(kb, "og1")]
        self.stg_i = 0
        cs = DSem(kb, "cst")
        toks = [kb.dma(kb.sp, self.gain_sb[:], self.gains, cs),
                kb.dma(kb.sp, self.gb_sb[:], self.gbias, cs),
                kb.dma(kb.sp, self.qkg_sb[:], self.qkg, cs),
                kb.dma(kb.sp, self.onorm_sb[:], self.onorm, cs),
                kb.dma(kb.sp, self.ones_f[:], self.c_ones, cs),
                kb.dma(kb.sp, self.rot_f[:], self.c_rotT, cs),
                kb.dma(kb.sp, self.tri_f[:], self.c_tri.rearrange("k p n -> p k n"), cs),
                kb.dma(kb.sp, self.rm_f[:], self.c_rm, cs)]
        kb.dve.wait(toks)
        nc.vector.memset(self.eps_col[:], EPS)
        nc.vector.memset(self.one_col[:], 1.0)
        nc.vector.tensor_copy(out=self.ones_bf[:], in_=self.ones_f[:])
        nc.vector.tensor_copy(out=self.rotT_bf[:], in_=self.rot_f[:])
        nc.vector.tensor_copy(out=self.rm_bf[:], in_=self.rm_f[:])
        ins = nc.vector.tensor_scalar(out=self.gainh_sb[:], in0=self.gain_sb[:], scalar1=0.5, scalar2=None,
                                      op0=ALU.mult)
        kb.dve.ms(ins)
        kb.barrier()
        for l in range(L + 1):
            self.tl_phase(l)
            if l < L:
                for s in range(self.nseq):
                    if self.mode in ("full", "A"):
                        self.mixer_a(l, s)
                    if self.mode in ("full", "B"):
                        self.mixer_b(l, s)
                    if self.mode in ("full", "C"):
                        self.mixer_c(l, s)


def _consts():
    c = {}
    c["c_ones"] = np.ones((128, 128), np.float32)
    R = np.zeros((128, 128), np.float32)
    for i in range(64):
        R[2 * i, 2 * i + 1] = -1.0
        R[2 * i + 1, 2 * i] = 1.0
    c["c_rotT"] = np.ascontiguousarray(R.T)
    t = np.arange(S)
    pos_r = (t // GRID_W).astype(np.float32)
    pos_c = (t % GRID_W).astype(np.float32)
    half = 64
    inv = (np.float32(10000.0) ** (-np.arange(0, half, 2, dtype=np.float32) / half)).astype(np.float32)
    ang = np.concatenate([pos_r[:, None] * inv, pos_c[:, None] * inv], axis=-1).astype(np.float32)
    cosT = np.cos(ang).T.astype(np.float32)
    sinT = np.sin(ang).T.astype(np.float32)
    c["c_cos"] = np.ascontiguousarray(np.repeat(cosT, 2, axis=0))
    c["c_sin"] = np.ascontiguousarray(np.repeat(sinT, 2, axis=0))
    j = np.arange(128)[:, None]
    i = np.arange(128)[None, :]
    c["c_tri"] = np.stack([(j <= i), (j > i), (j >= i), (j < i)]).astype(np.float32)
    KCS = [list(range(0, 6)), list(range(2, 10)), list(range(6, 14)), list(range(10, 16))]
    rm = np.zeros((128, 28, 8), np.float32)
    pi = 0
    for qt in range(4):
        for kc in KCS[qt]:
            for krl in range(2):
                kr = 2 * kc + krl
                for qrl in range(8):
                    qr = 8 * qt + qrl
                    rs = min(max(qr - 4, 0), 24)
                    if rs <= kr <= rs + 7:
                        rm[krl * 64:(krl + 1) * 64, pi, qrl] = 1.0
            pi += 1
    c["c_rm"] = rm.reshape(128, 28 * 8)
    return c


def _rpb_table(rpb):
    L = rpb.shape[0]
    out = np.full((L, 4, 128, 31, 64), -30000.0, np.float32)
    kcol = np.arange(64)[:, None]
    qcol = np.arange(64)[None, :]
    cs = np.clip(qcol - 8, 0, 48)
    cm = (kcol >= cs) & (kcol <= cs + 15)
    dc = np.clip(kcol - qcol + 15, 0, 30)
    for krl in range(2):
        for a2 in range(31):
            a = a2 - 8 - krl
            if 0 <= a <= 14:
                vals = rpb[:, :, 14 - a, :][:, :, dc]
                out[:, :, krl * 64:(krl + 1) * 64, a2, :] = np.where(cm[None, None], vals, np.float32(-30000.0))
    return out.reshape(L, 4, 128, 31 * 64)


def host_prep(inputs, depth):
    g = lambda k: np.asarray(inputs[k], dtype=np.float32)
    L = depth
    out = {}
    out["w_f1i"] = np.stack([pretile(g("w_ffn1_in")[l]) for l in range(L)])
    out["w_f1o"] = np.stack([pretile(g("w_ffn1_out")[l]) for l in range(L)])
    out["w_f2i"] = np.stack([pretile(g("w_ffn2_in")[l]) for l in range(L)])
    out["w_f2o"] = np.stack([pretile(g("w_ffn2_out")[l]) for l in range(L)])
    w_in = g("w_in")
    wm = np.zeros((L, D, NMIXB * 128), np.float32)
    wm[:, :, :NMIX] = w_in[:L, :, :NMIX]
    out["w_mix"] = np.stack([pretile(wm[l]) for l in range(L)])
    out["w_gate"] = np.stack([pretile(w_in[l][:, NMIX:]) for l in range(L)])
    out["w_br"] = np.stack([pretile(np.concatenate([g("w_br_a")[l], g("w_br_b")[l], g("w_br_c")[l]], axis=0))
                            for l in range(L)])
    out["w_o"] = np.stack([pretile(g("w_out")[l]) for l in range(L)])
    ng = g("norm_gains")[:L]
    out["gains"] = np.ascontiguousarray(ng.reshape(L, 6, 16, 128).transpose(3, 0, 1, 2).reshape(128, L * 6 * 16))
    gbi = g("gate_bias")[:L]
    out["gbias"] = np.ascontiguousarray(gbi.reshape(L, 3, 16, 128).transpose(3, 0, 1, 2).reshape(128, L * 3 * 16))
    out["qkg"] = np.ascontiguousarray(g("qk_norm_a")[:L].transpose(2, 0, 1).reshape(128, L * 2))
    out["onorm"] = np.ascontiguousarray(g("onorm_c")[:L].T)
    out["rpbT"] = _rpb_table(g("rpb_b")[:L])
    w2 = g("w_decay_c")[:L]
    b2 = g("b_decay_c")[:L]
    w2e = np.zeros((L, 2, 33, 256), np.float32)
    w2e[:, 0, 0:16, :] = w2[:, 0]
    w2e[:, 1, 16:32, :] = w2[:, 1]
    w2e[:, :, 32, :] = b2
    out["w2e"] = w2e
    out.update(_consts())
    return out


def to_fm(x):
    n = x.shape[0]
    return np.ascontiguousarray(x.reshape(n, S, NCH, 128).transpose(0, 2, 3, 1))


def from_fm(y):
    n = y.shape[0]
    return np.ascontiguousarray(y.transpose(0, 3, 1, 2).reshape(n, S, D))


_PROG = {}


def kernel(**inputs):
    n_cores = 8
    depth = int(np.asarray(inputs["norm_gains"]).shape[0])
    xp = np.asarray(inputs["x_prompt"], dtype=np.float32)
    xs = np.asarray(inputs["x_sample"], dtype=np.float32)
    nb_p, nb_s = xp.shape[0], xs.shape[0]
    xall = np.concatenate([xp, xs], axis=0)
    nseq = xall.shape[0] // n_cores
    key = (nseq, depth)
    if key not in _PROG:
        _PROG[key] = Prog(nseq, depth)
    prog = _PROG[key]
    hp = host_prep(inputs, depth)
    in_maps = []
    for c in range(n_cores):
        m = dict(hp)
        m["xin"] = to_fm(xall[c * nseq:(c + 1) * nseq])
        in_maps.append(m)
    res = run_bass_kernel_spmd(prog.nc, in_maps, core_ids=list(range(n_cores)))
    ys = [from_fm(np.asarray(r["yout"])) for r in res.results]
    yall = np.concatenate(ys, axis=0)
    return yall[:nb_p], yall[nb_p:nb_p + nb_s]
```

```python
import numpy as np
from contextlib import ExitStack
import concourse.bass as bass
import concourse.mybir as mybir
from concourse.bass_utils import run_bass_kernel_spmd

F32 = mybir.dt.float32
BF16 = mybir.dt.bfloat16
AF = mybir.ActivationFunctionType
ALU = mybir.AluOpType

D = 2048
S = 2048
DFF = 5632
NCH = 16
T = 512
NTT = S // T
EPS = 1e-6
NMIX = 4640
NMIXB = 37
GRID_W = 64


class Eng:
    def __init__(self, kb, raw, name):
        self.raw = raw
        self.name = name
        self.sem = kb.newsem("s_" + name)
        self.cnt = 0
        self.seen = {}

    def wait(self, *toks):
        for t in toks:
            if t is None:
                continue
            if isinstance(t, list):
                self.wait(*t)
                continue
            sem, v = t
            if self.seen.get(id(sem), 0) >= v:
                continue
            self.raw.wait_ge(sem, v)
            self.seen[id(sem)] = v

    def ms(self, ins):
        self.cnt += 1
        ins.then_inc(self.sem, 1)
        return (self.sem, self.cnt)


class DSem:
    def __init__(self, kb, name):
        self.sem = kb.newsem(name)
        self.cnt = 0
        kb.dsems.append(self)


class KB:
    def __init__(self):
        self.nc = bass.Bass("TRN2", target_bir_lowering=False)
        self.es = ExitStack()
        nc = self.nc
        self.work_sems = []
        self.dsems = []
        self.bar_a = self.es.enter_context(nc.semaphore("bar_a"))
        self.bar_b = self.es.enter_context(nc.semaphore("bar_b"))
        self.bar_k = 0
        self.pe = Eng(self, nc.tensor, "pe")
        self.act = Eng(self, nc.scalar, "act")
        self.dve = Eng(self, nc.vector, "dve")
        self.pool = Eng(self, nc.gpsimd, "pool")
        self.sp = Eng(self, nc.sync, "sp")
        self.engs = [self.pe, self.act, self.dve, self.pool, self.sp]
        self.pending = []

    def newsem(self, name):
        sem = self.es.enter_context(self.nc.semaphore(name))
        self.work_sems.append(sem)
        return sem

    def sb(self, name, shape, dt, es=None):
        self.uid = getattr(self, "uid", 0) + 1
        return (es or self.es).enter_context(self.nc.sbuf_tensor(f"{name}_{self.uid}", shape, dt))

    def dma(self, q, out, in_, ds, track=False):
        ds.cnt += 16
        q.raw.dma_start(out=out, in_=in_).then_inc(ds.sem, 16)
        tok = (ds.sem, ds.cnt)
        if track:
            self.pending.append(tok)
        return tok

    def barrier(self):
        best = {}
        for sem, v in [(e.sem, e.cnt) for e in self.engs if e.cnt > 0] + self.pending:
            if id(sem) not in best or best[id(sem)][1] < v:
                best[id(sem)] = (sem, v)
        toks = list(best.values())
        for e in self.engs:
            e.wait(*toks)
        self.pending = []


class Ring:
    def __init__(self, kb, name, n, kc):
        self.kb = kb
        self.n = n
        self.kc = kc
        self.name = name
        self.ds = [DSem(kb, f"{name}d{i}") for i in range(n)]
        self.bufs = None
        self.free = [None] * n
        self.i = 0

    def alloc(self, es):
        self.bufs = [self.kb.sb(f"{self.name}{i}", [128, self.kc, 128], BF16, es) for i in range(self.n)]
        self.free = [None] * self.n

    def load(self, src, kc):
        s = self.i % self.n
        self.i += 1
        kb = self.kb
        kb.pool.wait(self.free[s])
        tok = kb.dma(kb.pool, self.bufs[s][:, 0:kc, :], src, self.ds[s])
        return self.bufs[s], tok, s

    def release(self, s, tok):
        self.free[s] = tok


def pretile(w):
    K_, N_ = w.shape
    return np.ascontiguousarray(w.reshape(K_ // 128, 128, N_ // 128, 128).transpose(2, 1, 0, 3))


class Prog:
    def __init__(self, nseq, depth, mode="full"):
        self.nseq = nseq
        self.depth = depth
        self.mode = mode
        kb = self.kb = KB()
        nc = self.nc = kb.nc
        L = depth
        di = lambda name, shape, dt=F32: nc.dram_tensor(name, shape, dt, kind="ExternalInput").ap()
        self.xin = di("xin", [nseq, NCH, 128, S])
        self.yout = nc.dram_tensor("yout", [nseq, NCH, 128, S], F32, kind="ExternalOutput").ap()
        self.w_f1i = di("w_f1i", [L, 88, 128, 16, 128])
        self.w_f1o = di("w_f1o", [L, 16, 128, 44, 128])
        self.w_f2i = di("w_f2i", [L, 88, 128, 16, 128])
        self.w_f2o = di("w_f2o", [L, 16, 128, 44, 128])
        self.w_mix = di("w_mix", [L, NMIXB, 128, 16, 128])
        self.w_gate = di("w_gate", [L, 48, 128, 16, 128])
        self.w_br = di("w_br", [L, 16, 128, 16, 128])
        self.w_o = di("w_o", [L, 16, 128, 16, 128])
        self.gains = di("gains", [128, L * 6 * 16])
        self.gbias = di("gbias", [128, L * 3 * 16])
        self.qkg = di("qkg", [128, L * 2])
        self.onorm = di("onorm", [128, L])
        self.rpbT = di("rpbT", [L, 4, 128, 31 * 64])
        self.w2e = di("w2e", [L, 2, 33, 256])
        self.c_ones = di("c_ones", [128, 128])
        self.c_rotT = di("c_rotT", [128, 128])
        self.c_cos = di("c_cos", [128, S])
        self.c_sin = di("c_sin", [128, S])
        self.c_tri = di("c_tri", [4, 128, 128])
        self.c_rm = di("c_rm", [128, 28 * 8])
        dt_ = lambda name, shape: nc.dram_tensor(name, shape, BF16).ap()
        self.d_aq = dt_("d_aq", [nseq, 8, 128, S])
        self.d_ak = dt_("d_ak", [nseq, 2, 128, S])
        self.d_av = dt_("d_av", [nseq, 16, 128, 256])
        self.d_bq = dt_("d_bq", [nseq, 4, 128, S])
        self.d_bk = dt_("d_bk", [nseq, 4, 128, S])
        self.d_bv = dt_("d_bv", [nseq, 16, 128, 512])
        self.d_cq = dt_("d_cq", [nseq, 2, 128, S])
        self.d_ck = dt_("d_ck", [nseq, 2, 128, S])
        self.d_ckt = dt_("d_ckt", [nseq, 16, 128, 256])
        self.d_cv = dt_("d_cv", [nseq, 16, 128, 512])
        self.d_cog = dt_("d_cog", [nseq, 4, 128, S])
        self.d_clr = dt_("d_clr", [nseq, 32, S])
        self.d_om = dt_("d_om", [nseq, 16, 128, S])
        self.build()

    def gcol(self, l, i, c):
        return self.gain_sb[:, (l * 6 + i) * 16 + c:(l * 6 + i) * 16 + c + 1]

    def ghcol(self, l, i, c):
        return self.gainh_sb[:, (l * 6 + i) * 16 + c:(l * 6 + i) * 16 + c + 1]

    def rms_stats(self, src, nchunks, sqbuf, bank_idx, scale_n, rstd=None):
        kb = self.kb
        nc = self.nc
        rstd = self.rstd if rstd is None else rstd
        bank = self.ps[:, bank_idx, :]
        a = kb.act.ms(nc.scalar.activation(out=sqbuf[:, 0:nchunks, :], in_=src, func=AF.Square))
        kb.pe.wait(a, self.bank_free[bank_idx])
        for c in range(nchunks):
            ins = nc.tensor.matmul(bank, self.ones_bf[:], sqbuf[:, c, :], start=(c == 0), stop=(c == nchunks - 1))
        p = kb.pe.ms(ins)
        return self.stats_finish(p, bank_idx, scale_n, rstd)

    def stats_finish(self, p_tok, bank_idx, scale_n, rstd=None):
        kb, nc = self.kb, self.nc
        rstd = self.rstd if rstd is None else rstd
        kb.act.wait(p_tok, self.rstd_free)
        a2 = kb.act.ms(nc.scalar.activation(out=rstd[:], in_=self.ps[:, bank_idx, :], func=AF.Sqrt,
                                            bias=self.eps_col[:], scale=1.0 / scale_n))
        self.bank_free[bank_idx] = a2
        kb.dve.wait(a2)
        d = kb.dve.ms(nc.vector.reciprocal(out=rstd[:], in_=rstd[:]))
        return d

    def emit_ones(self, n, a):
        kb, nc = self.kb, self.nc
        kb.pe.wait(a)
        if n == 0:
            kb.pe.wait(self.bank_free[6])
        tok = kb.pe.ms(nc.tensor.matmul(self.ps[:, 6, :], self.ones_bf[:], self.sqs[:, n % 2, :], start=(n == 0),
                                        stop=(n == NCH - 1)))
        self.sqs_free[n % 2] = tok
        return tok

    def prenorm(self, l, gi):
        kb, nc = self.kb, self.nc
        if self.xstats is not None:
            p = self.xstats
            self.xstats = None
            self.stats_finish(p, 7, D)
        else:
            kb.act.wait(self.x_ready, self.xn_free)
            self.rms_stats(self.xT[:], NCH, self.xnT, 6, D)
        kb.dve.wait(self.x_ready)
        for c in range(NCH):
            ins = nc.vector.scalar_tensor_tensor(out=self.xnT[:, c, :], in0=self.xT[:, c, :],
                                                 scalar=self.gcol(l, gi, c), in1=self.rstd[:],
                                                 op0=ALU.mult, op1=ALU.mult)
        t = kb.dve.ms(ins)
        self.rstd_free = t
        self.last_xn = t
        return t

    def postnorm_residual(self, l, gi, half, stats_tok, next_stats):
        kb, nc = self.kb, self.nc
        gsel = self.ghcol if half else self.gcol
        self.stats_finish(stats_tok, 6, D)
        if next_stats:
            kb.act.wait(self.xn_free)
            kb.pe.wait(self.bank_free[7])
        for c in range(NCH):
            nc.vector.scalar_tensor_tensor(out=self.outT[:, c, :], in0=self.outT[:, c, :], scalar=gsel(l, gi, c),
                                           in1=self.rstd[:], op0=ALU.mult, op1=ALU.mult)
            ins = nc.vector.tensor_tensor(out=self.xT[:, c, :], in0=self.xT[:, c, :], in1=self.outT[:, c, :],
                                          op=ALU.add)
            if next_stats or c == NCH - 1:
                dc = kb.dve.ms(ins)
            if next_stats:
                kb.act.wait(dc)
                a = kb.act.ms(nc.scalar.activation(out=self.xnT[:, c, :], in_=self.xT[:, c, :], func=AF.Square))
                kb.pe.wait(a)
                pins = nc.tensor.matmul(self.ps[:, 7, :], self.ones_bf[:], self.xnT[:, c, :], start=(c == 0),
                                        stop=(c == NCH - 1))
        self.x_ready = dc
        if next_stats:
            self.xstats = kb.pe.ms(pins)
        self.out_free = self.x_ready
        self.rstd_free = self.x_ready
        self.xn_free = self.xstats if next_stats else self.x_ready

    def ffn(self, l, which, next_stats):
        kb, nc = self.kb, self.nc
        w_in = (self.w_f1i if which == 0 else self.w_f2i)[l]
        w_out = (self.w_f1o if which == 0 else self.w_f2o)[l]
        PS = self.ps
        xnT, actT, outT = self.xnT, self.actT, self.outT
        xn_ready = self.prenorm(l, 0 if which == 0 else 4)
        for j in range(DFF // 128):
            par = j % 2
            gb, ub = 2 * par, 2 * par + 1
            wg, tg, sg = self.ringA.load(w_in[j], 16)
            wu, tu, su = self.ringA.load(w_in[44 + j], 16)
            kb.pe.wait(xn_ready, tg, self.bank_free[gb])
            for c in range(NCH):
                ins = nc.tensor.matmul(PS[:, gb, :], wg[:, c, :], xnT[:, c, :], start=(c == 0), stop=(c == NCH - 1))
            pg = kb.pe.ms(ins)
            self.ringA.release(sg, pg)
            kb.pe.wait(tu, self.bank_free[ub])
            for c in range(NCH):
                ins = nc.tensor.matmul(PS[:, ub, :], wu[:, c, :], xnT[:, c, :], start=(c == 0), stop=(c == NCH - 1))
            pu = kb.pe.ms(ins)
            self.ringA.release(su, pu)
            kb.act.wait(pg, self.sg_free[par])
            a = kb.act.ms(nc.scalar.activation(out=self.sgbuf[:, par, :], in_=PS[:, gb, :], func=AF.Silu))
            self.bank_free[gb] = a
            kb.dve.wait(a, pu, self.act_free)
            dd = kb.dve.ms(nc.vector.tensor_tensor(out=actT[:, j, :], in0=self.sgbuf[:, par, :], in1=PS[:, ub, :],
                                                   op=ALU.mult))
            self.bank_free[ub] = dd
            self.sg_free[par] = dd
        act_ready = dd
        self.xn_free = pu
        prev = None
        for n in range(NCH):
            bk = 4 + n % 2
            wo, to, so = self.ringB.load(w_out[n], 44)
            kb.pe.wait(act_ready, to, self.bank_free[bk])
            for f in range(44):
                ins = nc.tensor.matmul(PS[:, bk, :], wo[:, f, :], actT[:, f, :], start=(f == 0), stop=(f == 43))
            p = kb.pe.ms(ins)
            self.ringB.release(so, p)
            kb.act.wait(p, self.out_free, self.sqs_free[n % 2])
            nc.scalar.copy(out=outT[:, n, :], in_=PS[:, bk, :])
            a = kb.act.ms(nc.scalar.activation(out=self.sqs[:, n % 2, :], in_=PS[:, bk, :], func=AF.Square))
            self.bank_free[bk] = a
            if prev is not None:
                self.emit_ones(*prev)
            prev = (n, a)
        pst = self.emit_ones(*prev)
        self.act_free = p
        self.postnorm_residual(l, 1 if which == 0 else 5, True, pst, next_stats)

    def stage_out(self, src_tok, dst_ap, src_ap, si):
        kb = self.kb
        kb.sp.wait(src_tok)
        return kb.dma(kb.sp, dst_ap, src_ap, self.stg_ds[si], track=True)

    def get_stage(self):
        i = self.stg_i % 4
        self.stg_i += 1
        return i

    def proj(self, l, s, tt):
        kb, nc = self.kb, self.nc
        PS = self.ps
        xnT = self.xnT
        tsl = slice(tt * T, (tt + 1) * T)
        xn_ready = self.prenorm(l, 2)
        kb.sp.wait(self.cs_free)
        tc1 = kb.dma(kb.sp, self.cosb[:], self.c_cos[:, tsl], self.cs_ds)
        tc2 = kb.dma(kb.sp, self.sinb[:], self.c_sin[:, tsl], self.cs_ds)
        plan = []
        for h in range(8):
            plan.append((h, "rope", self.d_aq[s, h, :, tsl], 0))
        for h in range(2):
            plan.append((8 + h, "rope", self.d_ak[s, h, :, tsl], 1))
        for i in range(2):
            plan.append((10 + i, "tok", self.d_av[s, tt * 4:(tt + 1) * 4, :, i * 128:(i + 1) * 128], None))
        for i in range(4):
            plan.append((12 + i, "copy", self.d_bq[s, i, :, tsl], 1.0))
        for i in range(4):
            plan.append((16 + i, "copy", self.d_bk[s, i, :, tsl], 1.0))
        for i in range(4):
            plan.append((20 + i, "tok", self.d_bv[s, tt * 4:(tt + 1) * 4, :, i * 128:(i + 1) * 128], None))
        for i in range(2):
            plan.append((24 + i, "copy", self.d_cq[s, i, :, tsl], 0.125))
        for i in range(2):
            plan.append((26 + i, "copy", self.d_ck[s, i, :, tsl], 1.0))
        for i in range(2):
            plan.append((26 + i, "tok", self.d_ckt[s, tt * 4:(tt + 1) * 4, :, i * 128:(i + 1) * 128], None))
        for i in range(4):
            plan.append((28 + i, "tok", self.d_cv[s, tt * 4:(tt + 1) * 4, :, i * 128:(i + 1) * 128], None))
        for i in range(4):
            plan.append((32 + i, "copy", self.d_cog[s, i, :, tsl], 1.0))
        plan.append((36, "lr", self.d_clr[s, :, tsl], 1.0))
        last_dve = None
        for it, (blk, kind, dst, par_) in enumerate(plan):
            bank = it % 4
            w, tw, sw = self.ringA.load(self.w_mix[l, blk], 16)
            kb.pe.wait(xn_ready, tw, self.bank_free[bank])
            if kind == "tok":
                for sub in range(4):
                    for c in range(NCH):
                        ins = nc.tensor.matmul(PS[:, bank, sub * 128:(sub + 1) * 128],
                                               xnT[:, c, sub * 128:(sub + 1) * 128], w[:, c, :],
                                               start=(c == 0), stop=(c == NCH - 1))
            else:
                for c in range(NCH):
                    ins = nc.tensor.matmul(PS[:, bank, :], w[:, c, :], xnT[:, c, :], start=(c == 0),
                                           stop=(c == NCH - 1))
            p = kb.pe.ms(ins)
            self.ringA.release(sw, p)
            si = self.get_stage()
            stg = self.stg[:, si, :]
            if kind in ("copy", "tok", "lr"):
                kb.act.wait(p, self.stg_free[si])
                if kind == "copy" and par_ != 1.0:
                    a = kb.act.ms(nc.scalar.mul(out=stg, in_=PS[:, bank, :], mul=float(par_)))
                else:
                    a = kb.act.ms(nc.scalar.copy(out=stg, in_=PS[:, bank, :]))
                self.bank_free[bank] = a
                if kind == "tok":
                    st = self.stage_out(a, dst.rearrange("k p n -> p k n"),
                                        self.stg[:, si, :].rearrange("p (k n) -> p k n", k=4), si)
                elif kind == "lr":
                    st = self.stage_out(a, dst, self.stg[0:32, si, :], si)
                else:
                    st = self.stage_out(a, dst, stg, si)
                self.stg_free[si] = st
            else:
                gcol = self.qkg_sb[:, l * 2 + par_:l * 2 + par_ + 1]
                kb.act.wait(p, self.sq1_free)
                a = kb.act.ms(nc.scalar.activation(out=self.sq1[:, 0, :], in_=PS[:, bank, :], func=AF.Square))
                kb.pe.wait(a, self.bank_free[5])
                p2 = kb.pe.ms(nc.tensor.matmul(PS[:, 5, :], self.ones_bf[:], self.sq1[:, 0, :], start=True, stop=True))
                self.sq1_free = p2
                kb.act.wait(p2, self.rq_free)
                a2 = kb.act.ms(nc.scalar.activation(out=self.rq[:], in_=PS[:, 5, :], func=AF.Sqrt,
                                                    bias=self.eps_col[:], scale=1.0 / 128))
                self.bank_free[5] = a2
                kb.dve.wait(a2, p, self.qn_free)
                nc.vector.reciprocal(out=self.rq[:], in_=self.rq[:])
                d1 = kb.dve.ms(nc.vector.scalar_tensor_tensor(out=self.qn[:], in0=PS[:, bank, :], scalar=gcol,
                                                              in1=self.rq[:], op0=ALU.mult, op1=ALU.mult))
                self.bank_free[bank] = d1
                self.rq_free = d1
                kb.pe.wait(d1, self.bank_free[4])
                p3 = kb.pe.ms(nc.tensor.matmul(PS[:, 4, :], self.rotT_bf[:], self.qn[:], start=True, stop=True))
                kb.dve.wait(p3, tc1, tc2, self.stg_free[si])
                nc.vector.tensor_tensor(out=self.t1[:], in0=self.qn[:], in1=self.cosb[:], op=ALU.mult)
                nc.vector.tensor_tensor(out=self.t2[:], in0=PS[:, 4, :], in1=self.sinb[:], op=ALU.mult)
                d2 = kb.dve.ms(nc.vector.tensor_tensor(out=stg, in0=self.t1[:], in1=self.t2[:], op=ALU.add))
                self.bank_free[4] = d2
                self.qn_free = d2
                last_dve = d2
                st = self.stage_out(d2, dst, stg, si)
                self.stg_free[si] = st
        self.cs_free = last_dve
        self.xn_free = p

    def merge(self, l, s, tt):
        kb, nc = self.kb, self.nc
        PS = self.ps
        xnT, actT, outT = self.xnT, self.actT, self.outT
        tsl = slice(tt * T, (tt + 1) * T)
        omT = actT[:, 0:16, :]
        mgT = actT[:, 16:32, :]
        kb.sp.wait(self.act_free)
        t_om = kb.dma(kb.sp, omT, self.d_om[s, :, :, tsl].rearrange("c p t -> p c t"), self.om_ds)
        xn_ready = self.prenorm(l, 2)
        segs = [(0, 8), (8, 12), (12, 16)]
        for n in range(NCH):
            wgs = [self.ringA.load(self.w_gate[l, br * 16 + n], 16) for br in range(3)]
            wb, tb, sbr = self.ringA.load(self.w_br[l, n], 16)
            for br in range(3):
                w, tw, sw = wgs[br]
                kb.pe.wait(xn_ready, tw, self.bank_free[br])
                for c in range(NCH):
                    ins = nc.tensor.matmul(PS[:, br, :], w[:, c, :], xnT[:, c, :], start=(c == 0), stop=(c == NCH - 1))
                pg = kb.pe.ms(ins)
                self.ringA.release(sw, pg)
                kb.act.wait(pg, self.sig_free[br])
                bcol = self.gb_sb[:, (l * 3 + br) * 16 + n:(l * 3 + br) * 16 + n + 1]
                a = kb.act.ms(nc.scalar.activation(out=self.sig[:, br, :], in_=PS[:, br, :], func=AF.Sigmoid,
                                                   bias=bcol, scale=1.0))
                self.bank_free[br] = a
                wgs[br] = a
            kb.pe.wait(tb, t_om)
            for br in range(3):
                c0, c1 = segs[br]
                kb.pe.wait(self.bank_free[3 + br])
                for c in range(c0, c1):
                    ins = nc.tensor.matmul(PS[:, 3 + br, :], wb[:, c, :], omT[:, c, :], start=(c == c0),
                                           stop=(c == c1 - 1))
            py = kb.pe.ms(ins)
            self.ringA.release(sbr, py)
            kb.dve.wait(py, wgs[0], wgs[1], wgs[2], self.mg_free)
            nc.vector.tensor_tensor(out=self.macc[:], in0=self.sig[:, 0, :], in1=PS[:, 3, :], op=ALU.mult)
            nc.vector.tensor_tensor(out=self.t1[:], in0=self.sig[:, 1, :], in1=PS[:, 4, :], op=ALU.mult)
            nc.vector.tensor_tensor(out=self.macc[:], in0=self.macc[:], in1=self.t1[:], op=ALU.add)
            nc.vector.tensor_tensor(out=self.t1[:], in0=self.sig[:, 2, :], in1=PS[:, 5, :], op=ALU.mult)
            d = kb.dve.ms(nc.vector.tensor_tensor(out=mgT[:, n, :], in0=self.macc[:], in1=self.t1[:], op=ALU.add))
            for br in range(3):
                self.bank_free[3 + br] = d
                self.sig_free[br] = d
        mg_ready = d
        self.xn_free = pg
        prev = None
        for n in range(NCH):
            bk = n % 2
            w, tw, sw = self.ringA.load(self.w_o[l, n], 16)
            kb.pe.wait(mg_ready, tw, self.bank_free[bk])
            for c in range(NCH):
                ins = nc.tensor.matmul(PS[:, bk, :], w[:, c, :], mgT[:, c, :], start=(c == 0), stop=(c == NCH - 1))
            p = kb.pe.ms(ins)
            self.ringA.release(sw, p)
            kb.act.wait(p, self.out_free, self.sqs_free[n % 2])
            nc.scalar.copy(out=outT[:, n, :], in_=PS[:, bk, :])
            a = kb.act.ms(nc.scalar.activation(out=self.sqs[:, n % 2, :], in_=PS[:, bk, :], func=AF.Square))
            self.bank_free[bk] = a
            if prev is not None:
                self.emit_ones(*prev)
            prev = (n, a)
        pst = self.emit_ones(*prev)
        self.act_free = p
        self.mg_free = p
        self.postnorm_residual(l, 3, False, pst, True)

    def tl_phase(self, l):
        kb, nc = self.kb, self.nc
        L = self.depth
        with ExitStack() as es:
            self.xT = kb.sb("xT", [128, NCH, T], F32, es)
            self.xnT = kb.sb("xnT", [128, NCH, T], BF16, es)
            self.actT = kb.sb("actT", [128, 44, T], BF16, es)
            self.outT = kb.sb("outT", [128, NCH, T], F32, es)
            self.sgbuf = kb.sb("sgbuf", [128, 2, T], F32, es)
            self.stg = kb.sb("stg", [128, 4, T], BF16, es)
            self.cosb = kb.sb("cosb", [128, T], F32, es)
            self.sinb = kb.sb("sinb", [128, T], F32, es)
            self.sqs = kb.sb("sqs", [128, 2, T], BF16, es)
            self.sq1 = self.sqs[:, 0:1, :]
            self.sqs_free = [None, None]
            self.xstats = None
            self.rq = kb.sb("rq", [128, T], F32, es)
            self.qn = kb.sb("qn", [128, T], BF16, es)
            self.t1 = kb.sb("t1", [128, T], F32, es)
            self.t2 = kb.sb("t2", [128, T], F32, es)
            self.macc = kb.sb("macc", [128, T], F32, es)
            self.sig = kb.sb("sig", [128, 3, T], F32, es)
            self.ringA.alloc(es)
            self.ringB.alloc(es)
            self.sg_free = [None, None]
            self.sig_free = [None, None, None]
            self.stg_free = [None] * 4
            self.act_free = self.out_free = self.xn_free = self.rstd_free = None
            self.sq1_free = self.rq_free = self.qn_free = self.cs_free = self.mg_free = None
            self.x_free = None
            self.last_xn = None
            self.bank_free = [None] * 8
            for s in range(self.nseq):
                for tt in range(NTT):
                    tsl = slice(tt * T, (tt + 1) * T)
                    src = self.xin if l == 0 else self.yout
                    kb.sp.wait(self.x_free, self.last_xn)
                    self.x_ready = kb.dma(kb.sp, self.xT[:], src[s, :, :, tsl].rearrange("c p t -> p c t"), self.x_ds)
                    self.xstats = None
                    if l > 0:
                        self.merge(l - 1, s, tt)
                        self.ffn(l - 1, 1, l < L)
                    if l < L:
                        self.ffn(l, 0, True)
                        self.proj(l, s, tt)
                    kb.sp.wait(self.x_ready)
                    self.x_free = kb.dma(kb.sp, self.yout[s, :, :, tsl].rearrange("c p t -> p c t"), self.xT[:],
                                         self.y_ds, track=True)
            kb.barrier()

    def attn_block(self, kT, qrhs, vfn, kcs, scale, maskfn, dst):
        kb, nc = self.kb, self.nc
        PS = self.ps
        LA = 2
        blk = self.ablk
        self.ablk += 1
        ob, db = 4 + blk % 2, 6 + blk % 2
        n = len(kcs)
        qk_tok = [None] * n
        qk_bank = [None] * n
        pv = None
        for i in range(n + LA):
            if i < n:
                bank = self.sbank % 4
                self.sbank += 1
                kb.pe.wait(self.bank_free[bank])
                qk_tok[i] = kb.pe.ms(nc.tensor.matmul(PS[:, bank, :], kT(kcs[i]), qrhs, start=True, stop=True))
                qk_bank[i] = bank
            j = i - LA
            if j >= 0:
                slot = self.pslot % 4
                self.pslot += 1
                pb = self.pbuf[:, slot, :]
                kb.act.wait(qk_tok[j], self.pbuf_free[slot])
                a = kb.act.ms(nc.scalar.activation(out=pb, in_=PS[:, qk_bank[j], :], func=AF.Exp, scale=float(scale)))
                self.bank_free[qk_bank[j]] = a
                ptok = a
                if maskfn is not None:
                    ge, rm = maskfn(kcs[j])
                    kb.dve.wait(a)
                    pb3 = pb.rearrange("p (r c) -> p r c", r=8)
                    nc.vector.tensor_tensor(out=pb3, in0=pb3, in1=ge, op=ALU.mult)
                    ptok = kb.dve.ms(nc.vector.tensor_tensor(out=pb3, in0=pb3, in1=rm, op=ALU.mult))
                kb.pe.wait(ptok)
                if j == 0:
                    kb.pe.wait(self.bank_free[ob], self.bank_free[db])
                nc.tensor.matmul(PS[:, ob, :], vfn(kcs[j]), pb, start=(j == 0), stop=(j == n - 1))
                pv = kb.pe.ms(nc.tensor.matmul(PS[:, db, :], self.ones_bf[:], pb, start=(j == 0), stop=(j == n - 1)))
                self.pbuf_free[slot] = pv
        so = blk % 2
        kb.dve.wait(pv, self.ost_free[so])
        nc.vector.reciprocal(out=self.rden[:], in_=PS[:, db, :])
        d = kb.dve.ms(nc.vector.tensor_tensor(out=self.ost[:, so, :], in0=PS[:, ob, :], in1=self.rden[:], op=ALU.mult))
        self.bank_free[ob] = d
        self.bank_free[db] = d
        kb.sp.wait(d)
        self.ost_free[so] = kb.dma(kb.sp, dst, self.ost[:, so, :], self.ost_ds[so], track=True)

    def attn_common_alloc(self, es):
        kb = self.kb
        self.pbuf = kb.sb("pbuf", [128, 4, 512], BF16, es)
        self.ost = kb.sb("ost", [128, 2, 512], BF16, es)
        self.rden = kb.sb("rden", [128, 512], F32, es)
        self.pbuf_free = [None] * 4
        self.ost_free = [None, None]
        self.bank_free = [None] * 8
        self.ablk = 0
        self.sbank = 0
        self.pslot = 0

    def mixer_a(self, l, s):
        kb, nc = self.kb, self.nc
        with ExitStack() as es:
            q = kb.sb("a_q", [128, 8, S], BF16, es)
            k = kb.sb("a_k", [128, 2, S], BF16, es)
            v = kb.sb("a_v", [128, 16, 256], BF16, es)
            self.attn_common_alloc(es)
            toks = [kb.dma(kb.sp, k[:], self.d_ak[s].rearrange("c p t -> p c t"), self.ld_ds),
                    kb.dma(kb.sp, v[:], self.d_av[s].rearrange("k p n -> p k n"), self.ld_ds)]
            for h in range(8):
                toks.append(kb.dma(kb.sp, q[:, h, :], self.d_aq[s, h], self.ld_ds))
            kb.pe.wait(toks)
            for h in range(8):
                kv = h // 4
                for qt in range(4):
                    self.attn_block(lambda kc: k[:, kv, kc * 128:(kc + 1) * 128], q[:, h, qt * 512:(qt + 1) * 512],
                                    lambda kc: v[:, kc, kv * 128:(kv + 1) * 128], list(range(16)),
                                    128 ** -0.5, None, self.d_om[s, h, :, qt * 512:(qt + 1) * 512])
            kb.barrier()

    def mixer_b(self, l, s):
        kb, nc = self.kb, self.nc
        with ExitStack() as es:
            q = kb.sb("b_q", [128, 4, S], BF16, es)
            k = kb.sb("b_k", [128, 4, S], BF16, es)
            v = kb.sb("b_v", [128, 16, 512], BF16, es)
            ge = kb.sb("b_ge", [128, 4, 31 * 64], BF16, es)
            gst = kb.sb("b_gst", [128, 31 * 64], F32, es)
            self.attn_common_alloc(es)
            toks = [kb.dma(kb.sp, q[:], self.d_bq[s].rearrange("c p t -> p c t"), self.ld_ds),
                    kb.dma(kb.sp, k[:], self.d_bk[s].rearrange("c p t -> p c t"), self.ld_ds),
                    kb.dma(kb.sp, v[:], self.d_bv[s].rearrange("k p n -> p k n"), self.ld_ds)]
            gfree = None
            for h in range(4):
                kb.sp.wait(gfree)
                tg = kb.dma(kb.sp, gst[:], self.rpbT[l, h], self.ld_ds)
                kb.act.wait(tg)
                gfree = kb.act.ms(nc.scalar.activation(out=ge[:, h, :], in_=gst[:], func=AF.Exp))
            kb.dve.wait(gfree)
            kb.pe.wait(toks)
            KCS = [list(range(0, 6)), list(range(2, 10)), list(range(6, 14)), list(range(10, 16))]
            pair_idx = {}
            pi = 0
            for qt in range(4):
                for kc in KCS[qt]:
                    pair_idx[(qt, kc)] = pi
                    pi += 1
            for h in range(4):
                for qt in range(4):
                    def maskfn(kc, h=h, qt=qt):
                        a0 = 8 * qt - 2 * kc + 7 + 8
                        g = ge[:, h, a0 * 64:(a0 + 8) * 64].rearrange("p (r c) -> p r c", r=8)
                        pidx = pair_idx[(qt, kc)]
                        rm = self.rm_bf[:, pidx * 8:(pidx + 1) * 8, None].broadcast_to([128, 8, 64])
                        return g, rm
                    self.attn_block(lambda kc: k[:, h, kc * 128:(kc + 1) * 128], q[:, h, qt * 512:(qt + 1) * 512],
                                    lambda kc: v[:, kc, h * 128:(h + 1) * 128], KCS[qt],
                                    128 ** -0.5, maskfn, self.d_om[s, 8 + h, :, qt * 512:(qt + 1) * 512])
            kb.barrier()

    def mixer_c(self, l, s):
        kb, nc = self.kb, self.nc
        PS = self.ps
        NC_ = 16
        with ExitStack() as es:
            q = kb.sb("c_q", [128, 2, S], BF16, es)
            k = kb.sb("c_k", [128, 2, S], BF16, es)
            kt = kb.sb("c_kt", [128, 16, 256], BF16, es)
            v = kb.sb("c_v", [128, 16, 512], BF16, es)
            lr = kb.sb("c_lr", [33, S], BF16, es)
            w2f = kb.sb("c_w2f", [33, 2, 256], F32, es)
            w2 = kb.sb("c_w2", [33, 2, 256], BF16, es)
            G = kb.sb("c_G", [128, 16, 2, 256], F32, es)
            etmp = kb.sb("c_etmp", [128, 2, 512], F32, es)
            ebuf = etmp[:, 0, :]
            qf = kb.sb("c_qf", [128, 2, S], BF16, es)
            qb = kb.sb("c_qb", [128, 2, S], BF16, es)
            kf = kb.sb("c_kf", [128, 4, S], BF16, es)
            kbw = kb.sb("c_kb", [128, 4, S], BF16, es)
            kdf = kb.sb("c_kdf", [128, 16, 256], BF16, es)
            kdb = kb.sb("c_kdb", [128, 16, 256], BF16, es)
            decf = kb.sb("c_decf", [128, 2, 16], F32, es)
            decb = kb.sb("c_decb", [128, 2, 16], F32, es)
            Sf = kb.sb("c_Sf", [128, 2, 128], F32, es)
            Sb = kb.sb("c_Sb", [128, 2, 128], F32, es)
            Sfa = kb.sb("c_Sfa", [128, 16, 4, 128], BF16, es)
            Sba = kb.sb("c_Sba", [128, 16, 4, 128], BF16, es)
            attn = kb.sb("c_attn", [128, 2, 4, 128], BF16, es)
            atmp = kb.sb("c_atmp", [128, 4, 128], F32, es)
            ogb = kb.sb("c_og", [128, 2, 512], BF16, es)
            sq = kb.sb("c_sq", [128, 1, 512], BF16, es)
            rs = kb.sb("c_rs", [128, 512], F32, es)
            on = kb.sb("c_on", [128, 512], F32, es)
            sgo = kb.sb("c_sgo", [128, 512], F32, es)
            ost = kb.sb("c_ost", [128, 2, 512], BF16, es)
            self.bank_free = [None] * 8
            self.rstd_free = None
            ld = [kb.dma(kb.sp, q[:], self.d_cq[s].rearrange("c p t -> p c t"), self.ld_ds),
                  kb.dma(kb.sp, k[:], self.d_ck[s].rearrange("c p t -> p c t"), self.ld_ds),
                  kb.dma(kb.sp, kt[:], self.d_ckt[s].rearrange("k p n -> p k n"), self.ld_ds),
                  kb.dma(kb.sp, v[:], self.d_cv[s].rearrange("k p n -> p k n"), self.ld_ds),
                  kb.dma(kb.sp, lr[0:32, :], self.d_clr[s], self.ld_ds),
                  kb.dma(kb.sp, w2f[:], self.w2e[l].rearrange("d k n -> k d n"), self.ld_ds)]
            for e in (kb.pe, kb.act, kb.dve):
                e.wait(ld)
            nc.vector.memset(lr[32:33, :], 1.0)
            nc.vector.memset(Sf[:], 0.0)
            nc.vector.memset(Sb[:], 0.0)
            nc.vector.memset(kf[:], 0.0)
            nc.vector.memset(kbw[:], 0.0)
            nc.vector.memset(Sfa[:], 0.0)
            nc.vector.memset(Sba[:], 0.0)
            d0 = kb.dve.ms(nc.vector.tensor_copy(out=w2[:], in_=w2f[:]))
            kb.pe.wait(d0)
            for n in range(NC_):
                bank = n % 2
                kb.pe.wait(self.bank_free[bank])
                for dr in range(2):
                    ins = nc.tensor.matmul(PS[:, bank, dr * 256:(dr + 1) * 256], lr[0:33, n * 128:(n + 1) * 128],
                                           w2[0:33, dr, :], start=True, stop=True)
                p = kb.pe.ms(ins)
                kb.act.wait(p)
                nc.scalar.activation(out=ebuf, in_=PS[:, bank, :], func=AF.Exp, scale=-1.0)
                a = kb.act.ms(nc.scalar.activation(out=G[:, n, :, :].rearrange("p d n -> p (d n)"), in_=ebuf,
                                                   func=AF.Ln, bias=self.one_col[:], scale=1.0))
                self.bank_free[bank] = a
            g_ready = a
            CS = 9
            if CS <= 1:
                kb.barrier(); return
            kb.pe.wait(g_ready)
            for ft in range(2):
                for tt in range(4):
                    tsl = slice(tt * 512, (tt + 1) * 512)
                    kb.pe.wait(self.bank_free[0], self.bank_free[1])
                    for dr in range(2):
                        for cc in range(4):
                            n = tt * 4 + cc
                            ins = nc.tensor.matmul(PS[:, dr, cc * 128:(cc + 1) * 128],
                                                   G[:, n, dr, ft * 128:(ft + 1) * 128], self.tri_f[:, 2 * dr, :],
                                                   start=True, stop=True)
                    p = kb.pe.ms(ins)
                    kb.act.wait(p, self.bank_free[2])
                    nc.scalar.activation(out=etmp[:, 0, :], in_=PS[:, 0, :], func=AF.Exp, scale=-1.0 / 16)
                    a1 = kb.act.ms(nc.scalar.activation(out=etmp[:, 1, :], in_=PS[:, 0, :], func=AF.Exp, scale=1.0 / 16))
                    kb.dve.wait(a1)
                    nc.vector.tensor_tensor(out=qf[:, ft, tsl], in0=q[:, ft, tsl], in1=etmp[:, 0, :], op=ALU.mult)
                    for hp in range(2):
                        r0 = hp * 64
                        nc.vector.tensor_tensor(out=kf[r0:r0 + 64, 2 * ft + hp, tsl], in0=k[r0:r0 + 64, ft, tsl],
                                                in1=etmp[r0:r0 + 64, 1, :], op=ALU.mult)
                    d1 = kb.dve.ms(nc.vector.tensor_copy(
                        out=decf[:, ft, tt * 4:(tt + 1) * 4],
                        in_=etmp[:, 0, :].rearrange("p (c t) -> p c t", c=4)[:, :, 127]))
                    kb.act.wait(d1)
                    nc.scalar.activation(out=etmp[:, 0, :], in_=PS[:, 1, :], func=AF.Exp, scale=-1.0 / 16)
                    a2 = kb.act.ms(nc.scalar.activation(out=etmp[:, 1, :], in_=PS[:, 1, :], func=AF.Exp, scale=1.0 / 16))
                    self.bank_free[0] = a2
                    self.bank_free[1] = a2
                    kb.dve.wait(a2)
                    nc.vector.tensor_tensor(out=qb[:, ft, tsl], in0=q[:, ft, tsl], in1=etmp[:, 0, :], op=ALU.mult)
                    for hp in range(2):
                        r0 = hp * 64
                        nc.vector.tensor_tensor(out=kbw[r0:r0 + 64, 2 * ft + hp, tsl], in0=k[r0:r0 + 64, ft, tsl],
                                                in1=etmp[r0:r0 + 64, 1, :], op=ALU.mult)
                    d2 = kb.dve.ms(nc.vector.tensor_copy(
                        out=decb[:, ft, tt * 4:(tt + 1) * 4],
                        in_=etmp[:, 0, :].rearrange("p (c t) -> p c t", c=4)[:, :, 0]))
                    kb.act.wait(d2)
            if CS <= 2:
                kb.barrier(); return
            for n in range(NC_):
                bank = 2 + n % 2
                kb.pe.wait(self.bank_free[bank])
                nc.tensor.matmul(PS[:, bank, 0:256], self.tri_f[:, 1, :], G[:, n, 0, :], start=True, stop=True)
                p = kb.pe.ms(nc.tensor.matmul(PS[:, bank, 256:512], self.tri_f[:, 3, :], G[:, n, 1, :], start=True,
                                              stop=True))
                kb.act.wait(p, d2)
                a = kb.act.ms(nc.scalar.activation(out=etmp[:, n % 2, :], in_=PS[:, bank, :], func=AF.Exp,
                                                   scale=-1.0 / 16))
                self.bank_free[bank] = a
                kb.dve.wait(a)
                nc.vector.tensor_tensor(out=kdf[:, n, :], in0=kt[:, n, :], in1=etmp[:, n % 2, 0:256], op=ALU.mult)
                d2 = kb.dve.ms(nc.vector.tensor_tensor(out=kdb[:, n, :], in0=kt[:, n, :], in1=etmp[:, n % 2, 256:512],
                                                       op=ALU.mult))
            kd_ready = d2
            if CS <= 3:
                kb.barrier(); return
            kb.pe.wait(kd_ready)
            for step in range(NC_):
                for dr, (Sx, Sall, kd, dec) in enumerate(((Sf, Sfa, kdf, decf), (Sb, Sba, kdb, decb))):
                    n = step if dr == 0 else NC_ - 1 - step
                    nc.vector.tensor_copy(out=Sall[0:64, n, 0:4:2, :], in_=Sx[0:64, :, :])
                    dsn = kb.dve.ms(nc.vector.tensor_copy(out=Sall[64:128, n, 1:4:2, :], in_=Sx[64:128, :, :]))
                    if step == NC_ - 1:
                        continue
                    bank = 4 + dr
                    kb.pe.wait(self.bank_free[bank])
                    for h in range(4):
                        ft = h // 2
                        ins = nc.tensor.matmul(PS[:, bank, h * 128:(h + 1) * 128], kd[:, n, ft * 128:(ft + 1) * 128],
                                               v[:, n, h * 128:(h + 1) * 128], start=True, stop=True)
                    p = kb.pe.ms(ins)
                    kb.dve.wait(p)
                    for h in range(4):
                        ft, r0 = h // 2, (h % 2) * 64
                        ins = nc.vector.scalar_tensor_tensor(
                            out=Sx[r0:r0 + 64, ft, :], in0=Sx[r0:r0 + 64, ft, :], scalar=dec[r0:r0 + 64, ft, n:n + 1],
                            in1=PS[r0:r0 + 64, bank, h * 128:(h + 1) * 128], op0=ALU.mult, op1=ALU.add)
                    self.bank_free[bank] = kb.dve.ms(ins)
            st_ready = (kb.dve.sem, kb.dve.cnt)
            if CS <= 4:
                kb.barrier(); return
            MF = self.tri_f[:, 0, None, :].broadcast_to([128, 4, 128])
            MB = self.tri_f[:, 2, None, :].broadcast_to([128, 4, 128])
            kb.pe.wait(st_ready)
            og_ds_free = [None, None]
            ost_free = [None, None]
            attn_free = [None, None]
            it = 0
            for tt in range(4):
                tsl = slice(tt * 512, (tt + 1) * 512)
                for h in range(4):
                    kb.pe.wait(self.bank_free[4 + h])
                for cc in range(4):
                    n = tt * 4 + cc
                    csl = slice(n * 128, (n + 1) * 128)
                    ap_ = it % 2
                    it += 1
                    kb.pe.wait(self.bank_free[0], self.bank_free[1])
                    for h in range(4):
                        ft, r0 = h // 2, (h % 2) * 64
                        nc.tensor.matmul(PS[:, 0, h * 128:(h + 1) * 128], kf[:, h, csl],
                                         qf[:, ft, csl], start=True, stop=True)
                        ins = nc.tensor.matmul(PS[:, 1, h * 128:(h + 1) * 128], kbw[:, h, csl],
                                               qb[:, ft, csl], start=True, stop=True)
                    p = kb.pe.ms(ins)
                    kb.dve.wait(p, attn_free[ap_])
                    nc.vector.tensor_tensor(out=atmp[:], in0=PS[:, 0, :].rearrange("p (h t) -> p h t", h=4), in1=MF,
                                            op=ALU.mult)
                    nc.vector.tensor_tensor(out=attn[:, ap_, :, :], in0=PS[:, 1, :].rearrange("p (h t) -> p h t", h=4),
                                            in1=MB, op=ALU.mult)
                    d = kb.dve.ms(nc.vector.tensor_tensor(out=attn[:, ap_, :, :], in0=attn[:, ap_, :, :], in1=atmp[:],
                                                          op=ALU.add))
                    self.bank_free[0] = d
                    self.bank_free[1] = d
                    kb.pe.wait(d)
                    for h in range(4):
                        ft, r0 = h // 2, (h % 2) * 64
                        ob = PS[:, 4 + h, cc * 128:(cc + 1) * 128]
                        nc.tensor.matmul(ob, v[:, n, h * 128:(h + 1) * 128], attn[:, ap_, h, :], start=True, stop=False)
                        nc.tensor.matmul(ob, Sfa[:, n, h, :], qf[:, ft, csl], start=False, stop=False)
                        ins = nc.tensor.matmul(ob, Sba[:, n, h, :], qb[:, ft, csl], start=False, stop=True)
                    attn_free[ap_] = kb.pe.ms(ins)
                o_ready = attn_free[(it - 1) % 2]
                for h in range(4 if CS > 5 else 0):
                    so = h % 2
                    kb.sp.wait(og_ds_free[so])
                    tog = kb.dma(kb.sp, ogb[:, so, :], self.d_cog[s, h, :, tsl], self.og_ds[so])
                    kb.act.wait(o_ready)
                    self.rms_stats(PS[:, 4 + h, :].rearrange("p (c t) -> p c t", c=1), 1, sq, 2, 128, rstd=rs)
                    kb.act.wait(tog)
                    a = kb.act.ms(nc.scalar.activation(out=sgo[:], in_=ogb[:, so, :], func=AF.Silu))
                    og_ds_free[so] = a
                    nc.vector.scalar_tensor_tensor(out=on[:], in0=PS[:, 4 + h, :], scalar=self.onorm_sb[:, l:l + 1],
                                                   in1=rs[:], op0=ALU.mult, op1=ALU.mult)
                    kb.dve.wait(a, ost_free[so])
                    d = kb.dve.ms(nc.vector.tensor_tensor(out=ost[:, so, :], in0=on[:], in1=sgo[:], op=ALU.mult))
                    self.bank_free[4 + h] = d
                    self.rstd_free = d
                    kb.act.wait(d)
                    kb.sp.wait(d)
                    ost_free[so] = kb.dma(kb.sp, self.d_om[s, 12 + h, :, tsl], ost[:, so, :], self.ost_ds[so],
                                          track=True)
            kb.barrier()

    def build(self):
        kb, nc = self.kb, self.nc
        L = self.depth
        sb = kb.sb
        self.gain_sb = sb("gain_sb", [128, L * 6 * 16], F32)
        self.gainh_sb = sb("gainh_sb", [128, L * 6 * 16], F32)
        self.gb_sb = sb("gb_sb", [128, L * 3 * 16], F32)
        self.qkg_sb = sb("qkg_sb", [128, L * 2], F32)
        self.onorm_sb = sb("onorm_sb", [128, L], F32)
        self.ones_f = sb("ones_f", [128, 128], F32)
        self.ones_bf = sb("ones_bf", [128, 128], BF16)
        self.rot_f = sb("rot_f", [128, 128], F32)
        self.rotT_bf = sb("rotT_bf", [128, 128], BF16)
        self.tri_f = sb("tri_f", [128, 4, 128], F32)
        self.rm_f = sb("rm_f", [128, 28 * 8], F32)
        self.rm_bf = sb("rm_bf", [128, 28 * 8], BF16)
        self.rstd = sb("rstd", [128, T], F32)
        self.eps_col = sb("eps_col", [128, 1], F32)
        self.one_col = sb("one_col", [128, 1], F32)
        self.ps = kb.es.enter_context(nc.psum_tensor("ps", [128, 8, 512], F32))
        self.ringA = Ring(kb, "rA", 5, 16)
        self.ringB = Ring(kb, "rB", 2, 44)
        self.bank_free = [None] * 8
        self.rstd_free = None
        self.x_ds = DSem(kb, "xld")
        self.y_ds = DSem(kb, "yst")
        self.stg_ds = [DSem(kb, f"stg{i}") for i in range(4)]
        self.cs_ds = DSem(kb, "csld")
        self.om_ds = DSem(kb, "omld")
        self.ld_ds = DSem(kb, "mixld")
        self.ost_ds = [DSem(kb, "ost0"), DSem(kb, "ost1")]
        self.og_ds = [DSem(kb, "og0"), DSem(kb, "og1")]
        self.stg_i = 0
        cs = DSem(kb, "cst")
        toks = [kb.dma(kb.sp, self.gain_sb[:], self.gains, cs),
                kb.dma(kb.sp, self.gb_sb[:], self.gbias, cs),
                kb.dma(kb.sp, self.qkg_sb[:], self.qkg, cs),
                kb.dma(kb.sp, self.onorm_sb[:], self.onorm, cs),
                kb.dma(kb.sp, self.ones_f[:], self.c_ones, cs),
                kb.dma(kb.sp, self.rot_f[:], self.c_rotT, cs),
                kb.dma(kb.sp, self.tri_f[:], self.c_tri.rearrange("k p n -> p k n"), cs),
                kb.dma(kb.sp, self.rm_f[:], self.c_rm, cs)]
        kb.dve.wait(toks)
        nc.vector.memset(self.eps_col[:], EPS)
        nc.vector.memset(self.one_col[:], 1.0)
        nc.vector.tensor_copy(out=self.ones_bf[:], in_=self.ones_f[:])
        nc.vector.tensor_copy(out=self.rotT_bf[:], in_=self.rot_f[:])
        nc.vector.tensor_copy(out=self.rm_bf[:], in_=self.rm_f[:])
        ins = nc.vector.tensor_scalar(out=self.gainh_sb[:], in0=self.gain_sb[:], scalar1=0.5, scalar2=None,
                                      op0=ALU.mult)
        kb.dve.ms(ins)
        kb.barrier()
        for l in range(L + 1):
            self.tl_phase(l)
            if l < L:
                for s in range(self.nseq):
                    if self.mode in ("full", "A"):
                        self.mixer_a(l, s)
                    if self.mode in ("full", "B"):
                        self.mixer_b(l, s)
                    if self.mode in ("full", "C"):
                        self.mixer_c(l, s)


def _consts():
    c = {}
    c["c_ones"] = np.ones((128, 128), np.float32)
    R = np.zeros((128, 128), np.float32)
    for i in range(64):
        R[2 * i, 2 * i + 1] = -1.0
        R[2 * i + 1, 2 * i] = 1.0
    c["c_rotT"] = np.ascontiguousarray(R.T)
    t = np.arange(S)
    pos_r = (t // GRID_W).astype(np.float32)
    pos_c = (t % GRID_W).astype(np.float32)
    half = 64
    inv = (np.float32(10000.0) ** (-np.arange(0, half, 2, dtype=np.float32) / half)).astype(np.float32)
    ang = np.concatenate([pos_r[:, None] * inv, pos_c[:, None] * inv], axis=-1).astype(np.float32)
    cosT = np.cos(ang).T.astype(np.float32)
    sinT = np.sin(ang).T.astype(np.float32)
    c["c_cos"] = np.ascontiguousarray(np.repeat(cosT, 2, axis=0))
    c["c_sin"] = np.ascontiguousarray(np.repeat(sinT, 2, axis=0))
    j = np.arange(128)[:, None]
    i = np.arange(128)[None, :]
    c["c_tri"] = np.stack([(j <= i), (j > i), (j >= i), (j < i)]).astype(np.float32)
    KCS = [list(range(0, 6)), list(range(2, 10)), list(range(6, 14)), list(range(10, 16))]
    rm = np.zeros((128, 28, 8), np.float32)
    pi = 0
    for qt in range(4):
        for kc in KCS[qt]:
            for krl in range(2):
                kr = 2 * kc + krl
                for qrl in range(8):
                    qr = 8 * qt + qrl
                    rs = min(max(qr - 4, 0), 24)
                    if rs <= kr <= rs + 7:
                        rm[krl * 64:(krl + 1) * 64, pi, qrl] = 1.0
            pi += 1
    c["c_rm"] = rm.reshape(128, 28 * 8)
    return c


def _rpb_table(rpb):
    L = rpb.shape[0]
    out = np.full((L, 4, 128, 31, 64), -30000.0, np.float32)
    kcol = np.arange(64)[:, None]
    qcol = np.arange(64)[None, :]
    cs = np.clip(qcol - 8, 0, 48)
    cm = (kcol >= cs) & (kcol <= cs + 15)
    dc = np.clip(kcol - qcol + 15, 0, 30)
    for krl in range(2):
        for a2 in range(31):
            a = a2 - 8 - krl
            if 0 <= a <= 14:
                vals = rpb[:, :, 14 - a, :][:, :, dc]
                out[:, :, krl * 64:(krl + 1) * 64, a2, :] = np.where(cm[None, None], vals, np.float32(-30000.0))
    return out.reshape(L, 4, 128, 31 * 64)


def host_prep(inputs, depth):
    g = lambda k: np.asarray(inputs[k], dtype=np.float32)
    L = depth
    out = {}
    out["w_f1i"] = np.stack([pretile(g("w_ffn1_in")[l]) for l in range(L)])
    out["w_f1o"] = np.stack([pretile(g("w_ffn1_out")[l]) for l in range(L)])
    out["w_f2i"] = np.stack([pretile(g("w_ffn2_in")[l]) for l in range(L)])
    out["w_f2o"] = np.stack([pretile(g("w_ffn2_out")[l]) for l in range(L)])
    w_in = g("w_in")
    wm = np.zeros((L, D, NMIXB * 128), np.float32)
    wm[:, :, :NMIX] = w_in[:L, :, :NMIX]
    out["w_mix"] = np.stack([pretile(wm[l]) for l in range(L)])
    out["w_gate"] = np.stack([pretile(w_in[l][:, NMIX:]) for l in range(L)])
    out["w_br"] = np.stack([pretile(np.concatenate([g("w_br_a")[l], g("w_br_b")[l], g("w_br_c")[l]], axis=0))
                            for l in range(L)])
    out["w_o"] = np.stack([pretile(g("w_out")[l]) for l in range(L)])
    ng = g("norm_gains")[:L]
    out["gains"] = np.ascontiguousarray(ng.reshape(L, 6, 16, 128).transpose(3, 0, 1, 2).reshape(128, L * 6 * 16))
    gbi = g("gate_bias")[:L]
    out["gbias"] = np.ascontiguousarray(gbi.reshape(L, 3, 16, 128).transpose(3, 0, 1, 2).reshape(128, L * 3 * 16))
    out["qkg"] = np.ascontiguousarray(g("qk_norm_a")[:L].transpose(2, 0, 1).reshape(128, L * 2))
    out["onorm"] = np.ascontiguousarray(g("onorm_c")[:L].T)
    out["rpbT"] = _rpb_table(g("rpb_b")[:L])
    w2 = g("w_decay_c")[:L]
    b2 = g("b_decay_c")[:L]
    w2e = np.zeros((L, 2, 33, 256), np.float32)
    w2e[:, 0, 0:16, :] = w2[:, 0]
    w2e[:, 1, 16:32, :] = w2[:, 1]
    w2e[:, :, 32, :] = b2
    out["w2e"] = w2e
    out.update(_consts())
    return out


def to_fm(x):
    n = x.shape[0]
    return np.ascontiguousarray(x.reshape(n, S, NCH, 128).transpose(0, 2, 3, 1))


def from_fm(y):
    n = y.shape[0]
    return np.ascontiguousarray(y.transpose(0, 3, 1, 2).reshape(n, S, D))


_PROG = {}


def kernel(**inputs):
    n_cores = 8
    depth = int(np.asarray(inputs["norm_gains"]).shape[0])
    xp = np.asarray(inputs["x_prompt"], dtype=np.float32)
    xs = np.asarray(inputs["x_sample"], dtype=np.float32)
    nb_p, nb_s = xp.shape[0], xs.shape[0]
    xall = np.concatenate([xp, xs], axis=0)
    nseq = xall.shape[0] // n_cores
    key = (nseq, depth)
    if key not in _PROG:
        _PROG[key] = Prog(nseq, depth)
    prog = _PROG[key]
    hp = host_prep(inputs, depth)
    in_maps = []
    for c in range(n_cores):
        m = dict(hp)
        m["xin"] = to_fm(xall[c * nseq:(c + 1) * nseq])
        in_maps.append(m)
    res = run_bass_kernel_spmd(prog.nc, in_maps, core_ids=list(range(n_cores)))
    ys = [from_fm(np.asarray(r["yout"])) for r in res.results]
    yall = np.concatenate(ys, axis=0)
    return yall[:nb_p], yall[nb_p:nb_p + nb_s]
```

```python
import numpy as np
from contextlib import ExitStack
import concourse.bass as bass
import concourse.mybir as mybir
from concourse.bass_utils import run_bass_kernel_spmd

F32 = mybir.dt.float32
BF16 = mybir.dt.bfloat16
AF = mybir.ActivationFunctionType
ALU = mybir.AluOpType

D = 2048
S = 2048
DFF = 5632
NCH = 16
T = 512
NTT = S // T
EPS = 1e-6
NMIX = 4640
NMIXB = 37
GRID_W = 64


class Eng:
    def __init__(self, kb, raw, name):
        self.raw = raw
        self.name = name
        self.sem = kb.newsem("s_" + name)
        self.cnt = 0
        self.seen = {}

    def wait(self, *toks):
        for t in toks:
            if t is None:
                continue
            if isinstance(t, list):
                self.wait(*t)
                continue
            sem, v = t
            if self.seen.get(id(sem), 0) >= v:
                continue
            self.raw.wait_ge(sem, v)
            self.seen[id(sem)] = v

    def ms(self, ins):
        self.cnt += 1
        ins.then_inc(self.sem, 1)
        return (self.sem, self.cnt)


class DSem:
    def __init__(self, kb, name):
        self.sem = kb.newsem(name)
        self.cnt = 0
        kb.dsems.append(self)


class KB:
    def __init__(self):
        self.nc = bass.Bass("TRN2", target_bir_lowering=False)
        self.es = ExitStack()
        nc = self.nc
        self.work_sems = []
        self.dsems = []
        self.bar_a = self.es.enter_context(nc.semaphore("bar_a"))
        self.bar_b = self.es.enter_context(nc.semaphore("bar_b"))
        self.bar_k = 0
        self.pe = Eng(self, nc.tensor, "pe")
        self.act = Eng(self, nc.scalar, "act")
        self.dve = Eng(self, nc.vector, "dve")
        self.pool = Eng(self, nc.gpsimd, "pool")
        self.sp = Eng(self, nc.sync, "sp")
        self.engs = [self.pe, self.act, self.dve, self.pool, self.sp]
        self.pending = []

    def newsem(self, name):
        sem = self.es.enter_context(self.nc.semaphore(name))
        self.work_sems.append(sem)
        return sem

    def sb(self, name, shape, dt, es=None):
        self.uid = getattr(self, "uid", 0) + 1
        return (es or self.es).enter_context(self.nc.sbuf_tensor(f"{name}_{self.uid}", shape, dt))

    def dma(self, q, out, in_, ds, track=False):
        ds.cnt += 16
        q.raw.dma_start(out=out, in_=in_).then_inc(ds.sem, 16)
        tok = (ds.sem, ds.cnt)
        if track:
            self.pending.append(tok)
        return tok

    def barrier(self):
        best = {}
        for sem, v in [(e.sem, e.cnt) for e in self.engs if e.cnt > 0] + self.pending:
            if id(sem) not in best or best[id(sem)][1] < v:
                best[id(sem)] = (sem, v)
        toks = list(best.values())
        for e in self.engs:
            e.wait(*toks)
        self.pending = []


class Ring:
    def __init__(self, kb, name, n, kc):
        self.kb = kb
        self.n = n
        self.kc = kc
        self.name = name
        self.ds = [DSem(kb, f"{name}d{i}") for i in range(n)]
        self.bufs = None
        self.free = [None] * n
        self.i = 0

    def alloc(self, es):
        self.bufs = [self.kb.sb(f"{self.name}{i}", [128, self.kc, 128], BF16, es) for i in range(self.n)]
        self.free = [None] * self.n

    def load(self, src, kc):
        s = self.i % self.n
        self.i += 1
        kb = self.kb
        kb.pool.wait(self.free[s])
        tok = kb.dma(kb.pool, self.bufs[s][:, 0:kc, :], src, self.ds[s])
        return self.bufs[s], tok, s

    def release(self, s, tok):
        self.free[s] = tok


def pretile(w):
    K_, N_ = w.shape
    return np.ascontiguousarray(w.reshape(K_ // 128, 128, N_ // 128, 128).transpose(2, 1, 0, 3))


class Prog:
    def __init__(self, nseq, depth, mode="full"):
        self.nseq = nseq
        self.depth = depth
        self.mode = mode
        kb = self.kb = KB()
        nc = self.nc = kb.nc
        L = depth
        di = lambda name, shape, dt=F32: nc.dram_tensor(name, shape, dt, kind="ExternalInput").ap()
        self.xin = di("xin", [nseq, NCH, 128, S])
        self.yout = nc.dram_tensor("yout", [nseq, NCH, 128, S], F32, kind="ExternalOutput").ap()
        self.w_f1i = di("w_f1i", [L, 88, 128, 16, 128])
        self.w_f1o = di("w_f1o", [L, 16, 128, 44, 128])
        self.w_f2i = di("w_f2i", [L, 88, 128, 16, 128])
        self.w_f2o = di("w_f2o", [L, 16, 128, 44, 128])
        self.w_mix = di("w_mix", [L, NMIXB, 128, 16, 128])
        self.w_gate = di("w_gate", [L, 48, 128, 16, 128])
        self.w_br = di("w_br", [L, 16, 128, 16, 128])
        self.w_o = di("w_o", [L, 16, 128, 16, 128])
        self.gains = di("gains", [128, L * 6 * 16])
        self.gbias = di("gbias", [128, L * 3 * 16])
        self.qkg = di("qkg", [128, L * 2])
        self.onorm = di("onorm", [128, L])
        self.rpbT = di("rpbT", [L, 4, 128, 31 * 64])
        self.w2e = di("w2e", [L, 2, 33, 256])
        self.c_ones = di("c_ones", [128, 128])
        self.c_rotT = di("c_rotT", [128, 128])
        self.c_cos = di("c_cos", [128, S])
        self.c_sin = di("c_sin", [128, S])
        self.c_tri = di("c_tri", [4, 128, 128])
        self.c_rm = di("c_rm", [128, 28 * 8])
        dt_ = lambda name, shape: nc.dram_tensor(name, shape, BF16).ap()
        self.d_aq = dt_("d_aq", [nseq, 8, 128, S])
        self.d_ak = dt_("d_ak", [nseq, 2, 128, S])
        self.d_av = dt_("d_av", [nseq, 16, 128, 256])
        self.d_bq = dt_("d_bq", [nseq, 4, 128, S])
        self.d_bk = dt_("d_bk", [nseq, 4, 128, S])
        self.d_bv = dt_("d_bv", [nseq, 16, 128, 512])
        self.d_cq = dt_("d_cq", [nseq, 2, 128, S])
        self.d_ck = dt_("d_ck", [nseq, 2, 128, S])
        self.d_ckt = dt_("d_ckt", [nseq, 16, 128, 256])
        self.d_cv = dt_("d_cv", [nseq, 16, 128, 512])
        self.d_cog = dt_("d_cog", [nseq, 4, 128, S])
        self.d_clr = dt_("d_clr", [nseq, 32, S])
        self.d_om = dt_("d_om", [nseq, 16, 128, S])
        self.build()

    def gcol(self, l, i, c):
        return self.gain_sb[:, (l * 6 + i) * 16 + c:(l * 6 + i) * 16 + c + 1]

    def ghcol(self, l, i, c):
        return self.gainh_sb[:, (l * 6 + i) * 16 + c:(l * 6 + i) * 16 + c + 1]

    def rms_stats(self, src, nchunks, sqbuf, bank_idx, scale_n, rstd=None):
        kb = self.kb
        nc = self.nc
        rstd = self.rstd if rstd is None else rstd
        bank = self.ps[:, bank_idx, :]
        a = kb.act.ms(nc.scalar.activation(out=sqbuf[:, 0:nchunks, :], in_=src, func=AF.Square))
        kb.pe.wait(a, self.bank_free[bank_idx])
        for c in range(nchunks):
            ins = nc.tensor.matmul(bank, self.ones_bf[:], sqbuf[:, c, :], start=(c == 0), stop=(c == nchunks - 1))
        p = kb.pe.ms(ins)
        return self.stats_finish(p, bank_idx, scale_n, rstd)

    def stats_finish(self, p_tok, bank_idx, scale_n, rstd=None):
        kb, nc = self.kb, self.nc
        rstd = self.rstd if rstd is None else rstd
        kb.act.wait(p_tok, self.rstd_free)
        a2 = kb.act.ms(nc.scalar.activation(out=rstd[:], in_=self.ps[:, bank_idx, :], func=AF.Sqrt,
                                            bias=self.eps_col[:], scale=1.0 / scale_n))
        self.bank_free[bank_idx] = a2
        kb.dve.wait(a2)
        d = kb.dve.ms(nc.vector.reciprocal(out=rstd[:], in_=rstd[:]))
        return d

    def emit_ones(self, n, a):
        kb, nc = self.kb, self.nc
        kb.pe.wait(a)
        if n == 0:
            kb.pe.wait(self.bank_free[6])
        tok = kb.pe.ms(nc.tensor.matmul(self.ps[:, 6, :], self.ones_bf[:], self.sqs[:, n % 2, :], start=(n == 0),
                                        stop=(n == NCH - 1)))
        self.sqs_free[n % 2] = tok
        return tok

    def prenorm(self, l, gi):
        kb, nc = self.kb, self.nc
        if self.xstats is not None:
            p = self.xstats
            self.xstats = None
            self.stats_finish(p, 7, D)
        else:
            kb.act.wait(self.x_ready, self.xn_free)
            self.rms_stats(self.xT[:], NCH, self.xnT, 6, D)
        kb.dve.wait(self.x_ready)
        for c in range(NCH):
            ins = nc.vector.scalar_tensor_tensor(out=self.xnT[:, c, :], in0=self.xT[:, c, :],
                                                 scalar=self.gcol(l, gi, c), in1=self.rstd[:],
                                                 op0=ALU.mult, op1=ALU.mult)
        t = kb.dve.ms(ins)
        self.rstd_free = t
        self.last_xn = t
        return t

    def postnorm_residual(self, l, gi, half, stats_tok, next_stats):
        kb, nc = self.kb, self.nc
        gsel = self.ghcol if half else self.gcol
        self.stats_finish(stats_tok, 6, D)
        if next_stats:
            kb.act.wait(self.xn_free)
            kb.pe.wait(self.bank_free[7])
        for c in range(NCH):
            nc.vector.scalar_tensor_tensor(out=self.outT[:, c, :], in0=self.outT[:, c, :], scalar=gsel(l, gi, c),
                                           in1=self.rstd[:], op0=ALU.mult, op1=ALU.mult)
            ins = nc.vector.tensor_tensor(out=self.xT[:, c, :], in0=self.xT[:, c, :], in1=self.outT[:, c, :],
                                          op=ALU.add)
            if next_stats or c == NCH - 1:
                dc = kb.dve.ms(ins)
            if next_stats:
                kb.act.wait(dc)
                a = kb.act.ms(nc.scalar.activation(out=self.xnT[:, c, :], in_=self.xT[:, c, :], func=AF.Square))
                kb.pe.wait(a)
                pins = nc.tensor.matmul(self.ps[:, 7, :], self.ones_bf[:], self.xnT[:, c, :], start=(c == 0),
                                        stop=(c == NCH - 1))
        self.x_ready = dc
        if next_stats:
            self.xstats = kb.pe.ms(pins)
        self.out_free = self.x_ready
        self.rstd_free = self.x_ready
        self.xn_free = self.xstats if next_stats else self.x_ready

    def ffn(self, l, which, next_stats):
        kb, nc = self.kb, self.nc
        w_in = (self.w_f1i if which == 0 else self.w_f2i)[l]
        w_out = (self.w_f1o if which == 0 else self.w_f2o)[l]
        PS = self.ps
        xnT, actT, outT = self.xnT, self.actT, self.outT
        xn_ready = self.prenorm(l, 0 if which == 0 else 4)
        for j in range(DFF // 128):
            par = j % 2
            gb, ub = 2 * par, 2 * par + 1
            wg, tg, sg = self.ringA.load(w_in[j], 16)
            wu, tu, su = self.ringA.load(w_in[44 + j], 16)
            kb.pe.wait(xn_ready, tg, self.bank_free[gb])
            for c in range(NCH):
                ins = nc.tensor.matmul(PS[:, gb, :], wg[:, c, :], xnT[:, c, :], start=(c == 0), stop=(c == NCH - 1))
            pg = kb.pe.ms(ins)
            self.ringA.release(sg, pg)
            kb.pe.wait(tu, self.bank_free[ub])
            for c in range(NCH):
                ins = nc.tensor.matmul(PS[:, ub, :], wu[:, c, :], xnT[:, c, :], start=(c == 0), stop=(c == NCH - 1))
            pu = kb.pe.ms(ins)
            self.ringA.release(su, pu)
            kb.act.wait(pg, self.sg_free[par])
            a = kb.act.ms(nc.scalar.activation(out=self.sgbuf[:, par, :], in_=PS[:, gb, :], func=AF.Silu))
            self.bank_free[gb] = a
            kb.dve.wait(a, pu, self.act_free)
            dd = kb.dve.ms(nc.vector.tensor_tensor(out=actT[:, j, :], in0=self.sgbuf[:, par, :], in1=PS[:, ub, :],
                                                   op=ALU.mult))
            self.bank_free[ub] = dd
            self.sg_free[par] = dd
        act_ready = dd
        self.xn_free = pu
        prev = None
        for n in range(NCH):
            bk = 4 + n % 2
            for hf in range(2):
                wo, to, so = self.ringB.load(w_out[n, :, hf * 22:(hf + 1) * 22, :], 22)
                kb.pe.wait(act_ready, to, self.bank_free[bk])
                for f in range(22):
                    fa = hf * 22 + f
                    ins = nc.tensor.matmul(PS[:, bk, :], wo[:, f, :], actT[:, fa, :], start=(fa == 0), stop=(fa == 43))
                p = kb.pe.ms(ins)
                self.ringB.release(so, p)
            kb.act.wait(p, self.out_free, self.sqs_free[n % 2])
            nc.scalar.copy(out=outT[:, n, :], in_=PS[:, bk, :])
            a = kb.act.ms(nc.scalar.activation(out=self.sqs[:, n % 2, :], in_=PS[:, bk, :], func=AF.Square))
            self.bank_free[bk] = a
            if prev is not None:
                self.emit_ones(*prev)
            prev = (n, a)
        pst = self.emit_ones(*prev)
        self.act_free = p
        self.postnorm_residual(l, 1 if which == 0 else 5, True, pst, next_stats)

    def stage_out(self, src_tok, dst_ap, src_ap, si):
        kb = self.kb
        kb.sp.wait(src_tok)
        return kb.dma(kb.sp, dst_ap, src_ap, self.stg_ds[si], track=True)

    def get_stage(self):
        i = self.stg_i % 4
        self.stg_i += 1
        return i

    def proj(self, l, s, tt):
        kb, nc = self.kb, self.nc
        PS = self.ps
        xnT = self.xnT
        tsl = slice(tt * T, (tt + 1) * T)
        xn_ready = self.prenorm(l, 2)
        kb.sp.wait(self.cs_free)
        tc1 = kb.dma(kb.sp, self.cosb[:], self.c_cos[:, tsl], self.cs_ds)
        tc2 = kb.dma(kb.sp, self.sinb[:], self.c_sin[:, tsl], self.cs_ds)
        plan = []
        for h in range(8):
            plan.append((h, "rope", self.d_aq[s, h, :, tsl], 0))
        for h in range(2):
            plan.append((8 + h, "rope", self.d_ak[s, h, :, tsl], 1))
        for i in range(2):
            plan.append((10 + i, "tok", self.d_av[s, tt * 4:(tt + 1) * 4, :, i * 128:(i + 1) * 128], None))
        for i in range(4):
            plan.append((12 + i, "copy", self.d_bq[s, i, :, tsl], 1.0))
        for i in range(4):
            plan.append((16 + i, "copy", self.d_bk[s, i, :, tsl], 1.0))
        for i in range(4):
            plan.append((20 + i, "tok", self.d_bv[s, tt * 4:(tt + 1) * 4, :, i * 128:(i + 1) * 128], None))
        for i in range(2):
            plan.append((24 + i, "copy", self.d_cq[s, i, :, tsl], 0.125))
        for i in range(2):
            plan.append((26 + i, "copy", self.d_ck[s, i, :, tsl], 1.0))
        for i in range(2):
            plan.append((26 + i, "tok", self.d_ckt[s, tt * 4:(tt + 1) * 4, :, i * 128:(i + 1) * 128], None))
        for i in range(4):
            plan.append((28 + i, "tok", self.d_cv[s, tt * 4:(tt + 1) * 4, :, i * 128:(i + 1) * 128], None))
        for i in range(4):
            plan.append((32 + i, "copy", self.d_cog[s, i, :, tsl], 1.0))
        plan.append((36, "lr", self.d_clr[s, :, tsl], 1.0))
        last_dve = None
        for it, (blk, kind, dst, par_) in enumerate(plan):
            bank = it % 4
            w, tw, sw = self.ringA.load(self.w_mix[l, blk], 16)
            kb.pe.wait(xn_ready, tw, self.bank_free[bank])
            if kind == "tok":
                for sub in range(4):
                    for c in range(NCH):
                        ins = nc.tensor.matmul(PS[:, bank, sub * 128:(sub + 1) * 128],
                                               xnT[:, c, sub * 128:(sub + 1) * 128], w[:, c, :],
                                               start=(c == 0), stop=(c == NCH - 1))
            else:
                for c in range(NCH):
                    ins = nc.tensor.matmul(PS[:, bank, :], w[:, c, :], xnT[:, c, :], start=(c == 0),
                                           stop=(c == NCH - 1))
            p = kb.pe.ms(ins)
            self.ringA.release(sw, p)
            si = self.get_stage()
            stg = self.stg[:, si, :]
            if kind in ("copy", "tok", "lr"):
                kb.act.wait(p, self.stg_free[si])
                if kind == "copy" and par_ != 1.0:
                    a = kb.act.ms(nc.scalar.mul(out=stg, in_=PS[:, bank, :], mul=float(par_)))
                else:
                    a = kb.act.ms(nc.scalar.copy(out=stg, in_=PS[:, bank, :]))
                self.bank_free[bank] = a
                if kind == "tok":
                    st = self.stage_out(a, dst.rearrange("k p n -> p k n"),
                                        self.stg[:, si, :].rearrange("p (k n) -> p k n", k=4), si)
                elif kind == "lr":
                    st = self.stage_out(a, dst, self.stg[0:32, si, :], si)
                else:
                    st = self.stage_out(a, dst, stg, si)
                self.stg_free[si] = st
            else:
                gcol = self.qkg_sb[:, l * 2 + par_:l * 2 + par_ + 1]
                kb.act.wait(p, self.sq1_free)
                a = kb.act.ms(nc.scalar.activation(out=self.sq1[:, 0, :], in_=PS[:, bank, :], func=AF.Square))
                kb.pe.wait(a, self.bank_free[5])
                p2 = kb.pe.ms(nc.tensor.matmul(PS[:, 5, :], self.ones_bf[:], self.sq1[:, 0, :], start=True, stop=True))
                self.sq1_free = p2
                kb.act.wait(p2, self.rq_free)
                a2 = kb.act.ms(nc.scalar.activation(out=self.rq[:], in_=PS[:, 5, :], func=AF.Sqrt,
                                                    bias=self.eps_col[:], scale=1.0 / 128))
                self.bank_free[5] = a2
                kb.dve.wait(a2, p, self.qn_free)
                nc.vector.reciprocal(out=self.rq[:], in_=self.rq[:])
                d1 = kb.dve.ms(nc.vector.scalar_tensor_tensor(out=self.qn[:], in0=PS[:, bank, :], scalar=gcol,
                                                              in1=self.rq[:], op0=ALU.mult, op1=ALU.mult))
                self.bank_free[bank] = d1
                self.rq_free = d1
                kb.pe.wait(d1, self.bank_free[4])
                p3 = kb.pe.ms(nc.tensor.matmul(PS[:, 4, :], self.rotT_bf[:], self.qn[:], start=True, stop=True))
                kb.dve.wait(p3, tc1, tc2, self.stg_free[si])
                nc.vector.tensor_tensor(out=self.t1[:], in0=self.qn[:], in1=self.cosb[:], op=ALU.mult)
                nc.vector.tensor_tensor(out=self.t2[:], in0=PS[:, 4, :], in1=self.sinb[:], op=ALU.mult)
                d2 = kb.dve.ms(nc.vector.tensor_tensor(out=stg, in0=self.t1[:], in1=self.t2[:], op=ALU.add))
                self.bank_free[4] = d2
                self.qn_free = d2
                last_dve = d2
                st = self.stage_out(d2, dst, stg, si)
                self.stg_free[si] = st
        self.cs_free = last_dve
        self.xn_free = p

    def merge(self, l, s, tt):
        kb, nc = self.kb, self.nc
        PS = self.ps
        xnT, actT, outT = self.xnT, self.actT, self.outT
        tsl = slice(tt * T, (tt + 1) * T)
        omT = actT[:, 0:16, :]
        mgT = actT[:, 16:32, :]
        kb.sp.wait(self.act_free)
        t_om = kb.dma(kb.sp, omT, self.d_om[s, :, :, tsl].rearrange("c p t -> p c t"), self.om_ds)
        xn_ready = self.prenorm(l, 2)
        segs = [(0, 8), (8, 12), (12, 16)]
        for n in range(NCH):
            wgs = [self.ringA.load(self.w_gate[l, br * 16 + n], 16) for br in range(3)]
            wb, tb, sbr = self.ringA.load(self.w_br[l, n], 16)
            for br in range(3):
                w, tw, sw = wgs[br]
                kb.pe.wait(xn_ready, tw, self.bank_free[br])
                for c in range(NCH):
                    ins = nc.tensor.matmul(PS[:, br, :], w[:, c, :], xnT[:, c, :], start=(c == 0), stop=(c == NCH - 1))
                pg = kb.pe.ms(ins)
                self.ringA.release(sw, pg)
                kb.act.wait(pg, self.sig_free[br])
                bcol = self.gb_sb[:, (l * 3 + br) * 16 + n:(l * 3 + br) * 16 + n + 1]
                a = kb.act.ms(nc.scalar.activation(out=self.sig[:, br, :], in_=PS[:, br, :], func=AF.Sigmoid,
                                                   bias=bcol, scale=1.0))
                self.bank_free[br] = a
                wgs[br] = a
            kb.pe.wait(tb, t_om)
            for br in range(3):
                c0, c1 = segs[br]
                kb.pe.wait(self.bank_free[3 + br])
                for c in range(c0, c1):
                    ins = nc.tensor.matmul(PS[:, 3 + br, :], wb[:, c, :], omT[:, c, :], start=(c == c0),
                                           stop=(c == c1 - 1))
            py = kb.pe.ms(ins)
            self.ringA.release(sbr, py)
            kb.dve.wait(py, wgs[0], wgs[1], wgs[2], self.mg_free)
            nc.vector.tensor_tensor(out=self.macc[:], in0=self.sig[:, 0, :], in1=PS[:, 3, :], op=ALU.mult)
            nc.vector.tensor_tensor(out=self.t1[:], in0=self.sig[:, 1, :], in1=PS[:, 4, :], op=ALU.mult)
            nc.vector.tensor_tensor(out=self.macc[:], in0=self.macc[:], in1=self.t1[:], op=ALU.add)
            nc.vector.tensor_tensor(out=self.t1[:], in0=self.sig[:, 2, :], in1=PS[:, 5, :], op=ALU.mult)
            d = kb.dve.ms(nc.vector.tensor_tensor(out=mgT[:, n, :], in0=self.macc[:], in1=self.t1[:], op=ALU.add))
            for br in range(3):
                self.bank_free[3 + br] = d
                self.sig_free[br] = d
        mg_ready = d
        self.xn_free = pg
        prev = None
        for n in range(NCH):
            bk = n % 2
            w, tw, sw = self.ringA.load(self.w_o[l, n], 16)
            kb.pe.wait(mg_ready, tw, self.bank_free[bk])
            for c in range(NCH):
                ins = nc.tensor.matmul(PS[:, bk, :], w[:, c, :], mgT[:, c, :], start=(c == 0), stop=(c == NCH - 1))
            p = kb.pe.ms(ins)
            self.ringA.release(sw, p)
            kb.act.wait(p, self.out_free, self.sqs_free[n % 2])
            nc.scalar.copy(out=outT[:, n, :], in_=PS[:, bk, :])
            a = kb.act.ms(nc.scalar.activation(out=self.sqs[:, n % 2, :], in_=PS[:, bk, :], func=AF.Square))
            self.bank_free[bk] = a
            if prev is not None:
                self.emit_ones(*prev)
            prev = (n, a)
        pst = self.emit_ones(*prev)
        self.act_free = p
        self.mg_free = p
        self.postnorm_residual(l, 3, False, pst, True)

    def tl_phase(self, l):
        kb, nc = self.kb, self.nc
        L = self.depth
        with ExitStack() as es:
            self.xT = kb.sb("xT", [128, NCH, T], F32, es)
            self.xnT = kb.sb("xnT", [128, NCH, T], BF16, es)
            self.actT = kb.sb("actT", [128, 44, T], BF16, es)
            self.outT = kb.sb("outT", [128, NCH, T], F32, es)
            self.sgbuf = kb.sb("sgbuf", [128, 2, T], F32, es)
            self.stg = kb.sb("stg", [128, 4, T], BF16, es)
            self.cosb = kb.sb("cosb", [128, T], F32, es)
            self.sinb = kb.sb("sinb", [128, T], F32, es)
            self.sqs = kb.sb("sqs", [128, 2, T], BF16, es)
            self.sq1 = self.sqs[:, 0:1, :]
            self.sqs_free = [None, None]
            self.xstats = None
            self.rq = kb.sb("rq", [128, T], F32, es)
            self.qn = kb.sb("qn", [128, T], BF16, es)
            self.t1 = kb.sb("t1", [128, T], F32, es)
            self.t2 = kb.sb("t2", [128, T], F32, es)
            self.macc = kb.sb("macc", [128, T], F32, es)
            self.sig = kb.sb("sig", [128, 3, T], F32, es)
            self.ringA.alloc(es)
            self.ringB.alloc(es)
            self.sg_free = [None, None]
            self.sig_free = [None, None, None]
            self.stg_free = [None] * 4
            self.act_free = self.out_free = self.xn_free = self.rstd_free = None
            self.sq1_free = self.rq_free = self.qn_free = self.cs_free = self.mg_free = None
            self.x_free = None
            self.last_xn = None
            self.bank_free = [None] * 8
            for s in range(self.nseq):
                for tt in range(NTT):
                    tsl = slice(tt * T, (tt + 1) * T)
                    src = self.xin if l == 0 else self.yout
                    kb.sp.wait(self.x_free, self.last_xn)
                    self.x_ready = kb.dma(kb.sp, self.xT[:], src[s, :, :, tsl].rearrange("c p t -> p c t"), self.x_ds)
                    self.xstats = None
                    if l > 0:
                        self.merge(l - 1, s, tt)
                        self.ffn(l - 1, 1, l < L)
                    if l < L:
                        self.ffn(l, 0, True)
                        self.proj(l, s, tt)
                    kb.sp.wait(self.x_ready)
                    self.x_free = kb.dma(kb.sp, self.yout[s, :, :, tsl].rearrange("c p t -> p c t"), self.xT[:],
                                         self.y_ds, track=True)
            kb.barrier()

    def attn_block(self, kT, qrhs, vfn, kcs, scale, maskfn, dst):
        kb, nc = self.kb, self.nc
        PS = self.ps
        LA = 2
        blk = self.ablk
        self.ablk += 1
        ob, db = 4 + blk % 2, 6 + blk % 2
        n = len(kcs)
        qk_tok = [None] * n
        qk_bank = [None] * n
        pv = None
        for i in range(n + LA):
            if i < n:
                bank = self.sbank % 4
                self.sbank += 1
                kb.pe.wait(self.bank_free[bank])
                qk_tok[i] = kb.pe.ms(nc.tensor.matmul(PS[:, bank, :], kT(kcs[i]), qrhs, start=True, stop=True))
                qk_bank[i] = bank
            j = i - LA
            if j >= 0:
                slot = self.pslot % 4
                self.pslot += 1
                pb = self.pbuf[:, slot, :]
                kb.act.wait(qk_tok[j], self.pbuf_free[slot])
                a = kb.act.ms(nc.scalar.activation(out=pb, in_=PS[:, qk_bank[j], :], func=AF.Exp, scale=float(scale)))
                self.bank_free[qk_bank[j]] = a
                ptok = a
                if maskfn is not None:
                    ge, rm = maskfn(kcs[j])
                    kb.dve.wait(a)
                    pb3 = pb.rearrange("p (r c) -> p r c", r=8)
                    nc.vector.tensor_tensor(out=pb3, in0=pb3, in1=ge, op=ALU.mult)
                    ptok = kb.dve.ms(nc.vector.tensor_tensor(out=pb3, in0=pb3, in1=rm, op=ALU.mult))
                kb.pe.wait(ptok)
                if j == 0:
                    kb.pe.wait(self.bank_free[ob], self.bank_free[db])
                if maskfn is None:
                    pv = kb.pe.ms(nc.tensor.matmul(PS[:, ob, :], vfn(kcs[j]), pb, start=(j == 0), stop=(j == n - 1)))
                    acc = self.dacc[:, blk % 2, :]
                    kb.dve.wait(a)
                    if j == 0:
                        kb.dve.wait(self.dacc_free[blk % 2])
                        dt_ = kb.dve.ms(nc.vector.tensor_copy(out=acc, in_=pb))
                    else:
                        dt_ = kb.dve.ms(nc.vector.tensor_tensor(out=acc, in0=acc, in1=pb, op=ALU.add))
                    self.pbuf_free[slot] = [pv, dt_]
                    if j == n - 1:
                        kb.pe.wait(dt_)
                        pv = kb.pe.ms(nc.tensor.matmul(PS[:, db, :], self.ones_f[:], acc, start=True, stop=True))
                        self.dacc_free[blk % 2] = pv
                else:
                    nc.tensor.matmul(PS[:, ob, :], vfn(kcs[j]), pb, start=(j == 0), stop=(j == n - 1))
                    pv = kb.pe.ms(nc.tensor.matmul(PS[:, db, :], self.ones_bf[:], pb, start=(j == 0),
                                                   stop=(j == n - 1)))
                    self.pbuf_free[slot] = pv
        so = blk % 2
        kb.dve.wait(pv, self.ost_free[so])
        nc.vector.reciprocal(out=self.rden[:], in_=PS[:, db, :])
        d = kb.dve.ms(nc.vector.tensor_tensor(out=self.ost[:, so, :], in0=PS[:, ob, :], in1=self.rden[:], op=ALU.mult))
        self.bank_free[ob] = d
        self.bank_free[db] = d
        kb.sp.wait(d)
        self.ost_free[so] = kb.dma(kb.sp, dst, self.ost[:, so, :], self.ost_ds[so], track=True)

    def attn_common_alloc(self, es):
        kb = self.kb
        self.pbuf = kb.sb("pbuf", [128, 4, 512], BF16, es)
        self.ost = kb.sb("ost", [128, 2, 512], BF16, es)
        self.rden = kb.sb("rden", [128, 512], F32, es)
        self.dacc = kb.sb("dacc", [128, 2, 512], F32, es)
        self.dacc_free = [None, None]
        self.pbuf_free = [None] * 4
        self.ost_free = [None, None]
        self.bank_free = [None] * 8
        self.ablk = 0
        self.sbank = 0
        self.pslot = 0

    def mixer_a(self, l, s):
        kb, nc = self.kb, self.nc
        with ExitStack() as es:
            q = kb.sb("a_q", [128, 8, S], BF16, es)
            k = kb.sb("a_k", [128, 2, S], BF16, es)
            v = kb.sb("a_v", [128, 16, 256], BF16, es)
            self.attn_common_alloc(es)
            toks = [kb.dma(kb.sp, k[:], self.d_ak[s].rearrange("c p t -> p c t"), self.ld_ds),
                    kb.dma(kb.sp, v[:], self.d_av[s].rearrange("k p n -> p k n"), self.ld_ds)]
            for h in range(8):
                toks.append(kb.dma(kb.sp, q[:, h, :], self.d_aq[s, h], self.ld_ds))
            kb.pe.wait(toks)
            for h in range(8):
                kv = h // 4
                for qt in range(4):
                    self.attn_block(lambda kc: k[:, kv, kc * 128:(kc + 1) * 128], q[:, h, qt * 512:(qt + 1) * 512],
                                    lambda kc: v[:, kc, kv * 128:(kv + 1) * 128], list(range(16)),
                                    128 ** -0.5, None, self.d_om[s, h, :, qt * 512:(qt + 1) * 512])
            kb.barrier()

    def mixer_b(self, l, s):
        kb, nc = self.kb, self.nc
        with ExitStack() as es:
            q = kb.sb("b_q", [128, 4, S], BF16, es)
            k = kb.sb("b_k", [128, 4, S], BF16, es)
            v = kb.sb("b_v", [128, 16, 512], BF16, es)
            ge = kb.sb("b_ge", [128, 4, 31 * 64], BF16, es)
            gst = kb.sb("b_gst", [128, 31 * 64], F32, es)
            self.attn_common_alloc(es)
            toks = [kb.dma(kb.sp, q[:], self.d_bq[s].rearrange("c p t -> p c t"), self.ld_ds),
                    kb.dma(kb.sp, k[:], self.d_bk[s].rearrange("c p t -> p c t"), self.ld_ds),
                    kb.dma(kb.sp, v[:], self.d_bv[s].rearrange("k p n -> p k n"), self.ld_ds)]
            gfree = None
            for h in range(4):
                kb.sp.wait(gfree)
                tg = kb.dma(kb.sp, gst[:], self.rpbT[l, h], self.ld_ds)
                kb.act.wait(tg)
                gfree = kb.act.ms(nc.scalar.activation(out=ge[:, h, :], in_=gst[:], func=AF.Exp))
            kb.dve.wait(gfree)
            kb.pe.wait(toks)
            KCS = [list(range(0, 6)), list(range(2, 10)), list(range(6, 14)), list(range(10, 16))]
            pair_idx = {}
            pi = 0
            for qt in range(4):
                for kc in KCS[qt]:
                    pair_idx[(qt, kc)] = pi
                    pi += 1
            for h in range(4):
                for qt in range(4):
                    def maskfn(kc, h=h, qt=qt):
                        a0 = 8 * qt - 2 * kc + 7 + 8
                        g = ge[:, h, a0 * 64:(a0 + 8) * 64].rearrange("p (r c) -> p r c", r=8)
                        pidx = pair_idx[(qt, kc)]
                        rm = self.rm_bf[:, pidx * 8:(pidx + 1) * 8, None].broadcast_to([128, 8, 64])
                        return g, rm
                    self.attn_block(lambda kc: k[:, h, kc * 128:(kc + 1) * 128], q[:, h, qt * 512:(qt + 1) * 512],
                                    lambda kc: v[:, kc, h * 128:(h + 1) * 128], KCS[qt],
                                    128 ** -0.5, maskfn, self.d_om[s, 8 + h, :, qt * 512:(qt + 1) * 512])
            kb.barrier()

    def mixer_c(self, l, s):
        kb, nc = self.kb, self.nc
        PS = self.ps
        NC_ = 16
        with ExitStack() as es:
            q = kb.sb("c_q", [128, 2, S], BF16, es)
            k = kb.sb("c_k", [128, 2, S], BF16, es)
            kt = kb.sb("c_kt", [128, 16, 256], BF16, es)
            v = kb.sb("c_v", [128, 16, 512], BF16, es)
            lr = kb.sb("c_lr", [33, S], BF16, es)
            w2f = kb.sb("c_w2f", [33, 2, 256], F32, es)
            w2 = kb.sb("c_w2", [33, 2, 256], BF16, es)
            G = kb.sb("c_G", [128, 16, 2, 256], F32, es)
            etmp = kb.sb("c_etmp", [128, 2, 512], F32, es)
            ebuf = etmp[:, 0, :]
            qf = kb.sb("c_qf", [128, 2, S], BF16, es)
            qb = kb.sb("c_qb", [128, 2, S], BF16, es)
            kf = kb.sb("c_kf", [128, 4, S], BF16, es)
            kbw = kb.sb("c_kb", [128, 4, S], BF16, es)
            kdf = kb.sb("c_kdf", [128, 16, 256], BF16, es)
            kdb = kb.sb("c_kdb", [128, 16, 256], BF16, es)
            decf = kb.sb("c_decf", [128, 2, 16], F32, es)
            decb = kb.sb("c_decb", [128, 2, 16], F32, es)
            Sf = kb.sb("c_Sf", [128, 2, 128], F32, es)
            Sb = kb.sb("c_Sb", [128, 2, 128], F32, es)
            Sfa = kb.sb("c_Sfa", [128, 16, 4, 128], BF16, es)
            Sba = kb.sb("c_Sba", [128, 16, 4, 128], BF16, es)
            attn = kb.sb("c_attn", [128, 2, 4, 128], BF16, es)
            atmp = kb.sb("c_atmp", [128, 4, 128], F32, es)
            ogb = kb.sb("c_og", [128, 2, 512], BF16, es)
            sq = kb.sb("c_sq", [128, 1, 512], BF16, es)
            rs = kb.sb("c_rs", [128, 512], F32, es)
            on = kb.sb("c_on", [128, 512], F32, es)
            sgo = kb.sb("c_sgo", [128, 512], F32, es)
            ost = kb.sb("c_ost", [128, 2, 512], BF16, es)
            self.bank_free = [None] * 8
            self.rstd_free = None
            ld = [kb.dma(kb.sp, q[:], self.d_cq[s].rearrange("c p t -> p c t"), self.ld_ds),
                  kb.dma(kb.sp, k[:], self.d_ck[s].rearrange("c p t -> p c t"), self.ld_ds),
                  kb.dma(kb.sp, kt[:], self.d_ckt[s].rearrange("k p n -> p k n"), self.ld_ds),
                  kb.dma(kb.sp, v[:], self.d_cv[s].rearrange("k p n -> p k n"), self.ld_ds),
                  kb.dma(kb.sp, lr[0:32, :], self.d_clr[s], self.ld_ds),
                  kb.dma(kb.sp, w2f[:], self.w2e[l].rearrange("d k n -> k d n"), self.ld_ds)]
            for e in (kb.pe, kb.act, kb.dve):
                e.wait(ld)
            nc.vector.memset(lr[32:33, :], 1.0)
            nc.vector.memset(Sf[:], 0.0)
            nc.vector.memset(Sb[:], 0.0)
            nc.vector.memset(kf[:], 0.0)
            nc.vector.memset(kbw[:], 0.0)
            nc.vector.memset(Sfa[:], 0.0)
            nc.vector.memset(Sba[:], 0.0)
            d0 = kb.dve.ms(nc.vector.tensor_copy(out=w2[:], in_=w2f[:]))
            kb.pe.wait(d0)
            for n in range(NC_):
                bank = n % 2
                kb.pe.wait(self.bank_free[bank])
                for dr in range(2):
                    ins = nc.tensor.matmul(PS[:, bank, dr * 256:(dr + 1) * 256], lr[0:33, n * 128:(n + 1) * 128],
                                           w2[0:33, dr, :], start=True, stop=True)
                p = kb.pe.ms(ins)
                kb.act.wait(p)
                nc.scalar.activation(out=ebuf, in_=PS[:, bank, :], func=AF.Exp, scale=-1.0)
                a = kb.act.ms(nc.scalar.activation(out=G[:, n, :, :].rearrange("p d n -> p (d n)"), in_=ebuf,
                                                   func=AF.Ln, bias=self.one_col[:], scale=1.0))
                self.bank_free[bank] = a
            g_ready = a
            CS = 9
            if CS <= 1:
                kb.barrier(); return
            kb.pe.wait(g_ready)
            for ft in range(2):
                for tt in range(4):
                    tsl = slice(tt * 512, (tt + 1) * 512)
                    kb.pe.wait(self.bank_free[0], self.bank_free[1])
                    for dr in range(2):
                        for cc in range(4):
                            n = tt * 4 + cc
                            ins = nc.tensor.matmul(PS[:, dr, cc * 128:(cc + 1) * 128],
                                                   G[:, n, dr, ft * 128:(ft + 1) * 128], self.tri_f[:, 2 * dr, :],
                                                   start=True, stop=True)
                    p = kb.pe.ms(ins)
                    kb.act.wait(p, self.bank_free[2])
                    nc.scalar.activation(out=etmp[:, 0, :], in_=PS[:, 0, :], func=AF.Exp, scale=-1.0 / 16)
                    a1 = kb.act.ms(nc.scalar.activation(out=etmp[:, 1, :], in_=PS[:, 0, :], func=AF.Exp, scale=1.0 / 16))
                    kb.dve.wait(a1)
                    nc.vector.tensor_tensor(out=qf[:, ft, tsl], in0=q[:, ft, tsl], in1=etmp[:, 0, :], op=ALU.mult)
                    for hp in range(2):
                        r0 = hp * 64
                        nc.vector.tensor_tensor(out=kf[r0:r0 + 64, 2 * ft + hp, tsl], in0=k[r0:r0 + 64, ft, tsl],
                                                in1=etmp[r0:r0 + 64, 1, :], op=ALU.mult)
                    d1 = kb.dve.ms(nc.vector.tensor_copy(
                        out=decf[:, ft, tt * 4:(tt + 1) * 4],
                        in_=etmp[:, 0, :].rearrange("p (c t) -> p c t", c=4)[:, :, 127]))
                    kb.act.wait(d1)
                    nc.scalar.activation(out=etmp[:, 0, :], in_=PS[:, 1, :], func=AF.Exp, scale=-1.0 / 16)
                    a2 = kb.act.ms(nc.scalar.activation(out=etmp[:, 1, :], in_=PS[:, 1, :], func=AF.Exp, scale=1.0 / 16))
                    self.bank_free[0] = a2
                    self.bank_free[1] = a2
                    kb.dve.wait(a2)
                    nc.vector.tensor_tensor(out=qb[:, ft, tsl], in0=q[:, ft, tsl], in1=etmp[:, 0, :], op=ALU.mult)
                    for hp in range(2):
                        r0 = hp * 64
                        nc.vector.tensor_tensor(out=kbw[r0:r0 + 64, 2 * ft + hp, tsl], in0=k[r0:r0 + 64, ft, tsl],
                                                in1=etmp[r0:r0 + 64, 1, :], op=ALU.mult)
                    d2 = kb.dve.ms(nc.vector.tensor_copy(
                        out=decb[:, ft, tt * 4:(tt + 1) * 4],
                        in_=etmp[:, 0, :].rearrange("p (c t) -> p c t", c=4)[:, :, 0]))
                    kb.act.wait(d2)
            if CS <= 2:
                kb.barrier(); return
            for n in range(NC_):
                bank = 2 + n % 2
                kb.pe.wait(self.bank_free[bank])
                nc.tensor.matmul(PS[:, bank, 0:256], self.tri_f[:, 1, :], G[:, n, 0, :], start=True, stop=True)
                p = kb.pe.ms(nc.tensor.matmul(PS[:, bank, 256:512], self.tri_f[:, 3, :], G[:, n, 1, :], start=True,
                                              stop=True))
                kb.act.wait(p, d2)
                a = kb.act.ms(nc.scalar.activation(out=etmp[:, n % 2, :], in_=PS[:, bank, :], func=AF.Exp,
                                                   scale=-1.0 / 16))
                self.bank_free[bank] = a
                kb.dve.wait(a)
                nc.vector.tensor_tensor(out=kdf[:, n, :], in0=kt[:, n, :], in1=etmp[:, n % 2, 0:256], op=ALU.mult)
                d2 = kb.dve.ms(nc.vector.tensor_tensor(out=kdb[:, n, :], in0=kt[:, n, :], in1=etmp[:, n % 2, 256:512],
                                                       op=ALU.mult))
            kd_ready = d2
            if CS <= 3:
                kb.barrier(); return
            kb.pe.wait(kd_ready)
            for step in range(NC_):
                for dr, (Sx, Sall, kd, dec) in enumerate(((Sf, Sfa, kdf, decf), (Sb, Sba, kdb, decb))):
                    n = step if dr == 0 else NC_ - 1 - step
                    nc.vector.tensor_copy(out=Sall[0:64, n, 0:4:2, :], in_=Sx[0:64, :, :])
                    dsn = kb.dve.ms(nc.vector.tensor_copy(out=Sall[64:128, n, 1:4:2, :], in_=Sx[64:128, :, :]))
                    if step == NC_ - 1:
                        continue
                    bank = 4 + dr
                    kb.pe.wait(self.bank_free[bank])
                    for h in range(4):
                        ft = h // 2
                        ins = nc.tensor.matmul(PS[:, bank, h * 128:(h + 1) * 128], kd[:, n, ft * 128:(ft + 1) * 128],
                                               v[:, n, h * 128:(h + 1) * 128], start=True, stop=True)
                    p = kb.pe.ms(ins)
                    kb.dve.wait(p)
                    for h in range(4):
                        ft, r0 = h // 2, (h % 2) * 64
                        ins = nc.vector.scalar_tensor_tensor(
                            out=Sx[r0:r0 + 64, ft, :], in0=Sx[r0:r0 + 64, ft, :], scalar=dec[r0:r0 + 64, ft, n:n + 1],
                            in1=PS[r0:r0 + 64, bank, h * 128:(h + 1) * 128], op0=ALU.mult, op1=ALU.add)
                    self.bank_free[bank] = kb.dve.ms(ins)
            st_ready = (kb.dve.sem, kb.dve.cnt)
            if CS <= 4:
                kb.barrier(); return
            MF = self.tri_f[:, 0, None, :].broadcast_to([128, 4, 128])
            MB = self.tri_f[:, 2, None, :].broadcast_to([128, 4, 128])
            kb.pe.wait(st_ready)
            og_ds_free = [None, None]
            ost_free = [None, None]
            attn_free = [None, None]
            it = 0
            for tt in range(4):
                tsl = slice(tt * 512, (tt + 1) * 512)
                for h in range(4):
                    kb.pe.wait(self.bank_free[4 + h])
                for cc in range(4):
                    n = tt * 4 + cc
                    csl = slice(n * 128, (n + 1) * 128)
                    ap_ = it % 2
                    it += 1
                    kb.pe.wait(self.bank_free[0], self.bank_free[1])
                    for h in range(4):
                        ft, r0 = h // 2, (h % 2) * 64
                        nc.tensor.matmul(PS[:, 0, h * 128:(h + 1) * 128], kf[:, h, csl],
                                         qf[:, ft, csl], start=True, stop=True)
                        ins = nc.tensor.matmul(PS[:, 1, h * 128:(h + 1) * 128], kbw[:, h, csl],
                                               qb[:, ft, csl], start=True, stop=True)
                    p = kb.pe.ms(ins)
                    kb.dve.wait(p, attn_free[ap_])
                    nc.vector.tensor_tensor(out=atmp[:], in0=PS[:, 0, :].rearrange("p (h t) -> p h t", h=4), in1=MF,
                                            op=ALU.mult)
                    nc.vector.tensor_tensor(out=attn[:, ap_, :, :], in0=PS[:, 1, :].rearrange("p (h t) -> p h t", h=4),
                                            in1=MB, op=ALU.mult)
                    d = kb.dve.ms(nc.vector.tensor_tensor(out=attn[:, ap_, :, :], in0=attn[:, ap_, :, :], in1=atmp[:],
                                                          op=ALU.add))
                    self.bank_free[0] = d
                    self.bank_free[1] = d
                    kb.pe.wait(d)
                    for h in range(4):
                        ft, r0 = h // 2, (h % 2) * 64
                        ob = PS[:, 4 + h, cc * 128:(cc + 1) * 128]
                        nc.tensor.matmul(ob, v[:, n, h * 128:(h + 1) * 128], attn[:, ap_, h, :], start=True, stop=False)
                        nc.tensor.matmul(ob, Sfa[:, n, h, :], qf[:, ft, csl], start=False, stop=False)
                        ins = nc.tensor.matmul(ob, Sba[:, n, h, :], qb[:, ft, csl], start=False, stop=True)
                    attn_free[ap_] = kb.pe.ms(ins)
                o_ready = attn_free[(it - 1) % 2]
                for h in range(4 if CS > 5 else 0):
                    so = h % 2
                    kb.sp.wait(og_ds_free[so])
                    tog = kb.dma(kb.sp, ogb[:, so, :], self.d_cog[s, h, :, tsl], self.og_ds[so])
                    kb.act.wait(o_ready)
                    self.rms_stats(PS[:, 4 + h, :].rearrange("p (c t) -> p c t", c=1), 1, sq, 2, 128, rstd=rs)
                    kb.act.wait(tog)
                    a = kb.act.ms(nc.scalar.activation(out=sgo[:], in_=ogb[:, so, :], func=AF.Silu))
                    og_ds_free[so] = a
                    nc.vector.scalar_tensor_tensor(out=on[:], in0=PS[:, 4 + h, :], scalar=self.onorm_sb[:, l:l + 1],
                                                   in1=rs[:], op0=ALU.mult, op1=ALU.mult)
                    kb.dve.wait(a, ost_free[so])
                    d = kb.dve.ms(nc.vector.tensor_tensor(out=ost[:, so, :], in0=on[:], in1=sgo[:], op=ALU.mult))
                    self.bank_free[4 + h] = d
                    self.rstd_free = d
                    kb.act.wait(d)
                    kb.sp.wait(d)
                    ost_free[so] = kb.dma(kb.sp, self.d_om[s, 12 + h, :, tsl], ost[:, so, :], self.ost_ds[so],
                                          track=True)
            kb.barrier()

    def build(self):
        kb, nc = self.kb, self.nc
        L = self.depth
        sb = kb.sb
        self.gain_sb = sb("gain_sb", [128, L * 6 * 16], F32)
        self.gainh_sb = sb("gainh_sb", [128, L * 6 * 16], F32)
        self.gb_sb = sb("gb_sb", [128, L * 3 * 16], F32)
        self.qkg_sb = sb("qkg_sb", [128, L * 2], F32)
        self.onorm_sb = sb("onorm_sb", [128, L], F32)
        self.ones_f = sb("ones_f", [128, 128], F32)
        self.ones_bf = sb("ones_bf", [128, 128], BF16)
        self.rot_f = sb("rot_f", [128, 128], F32)
        self.rotT_bf = sb("rotT_bf", [128, 128], BF16)
        self.tri_f = sb("tri_f", [128, 4, 128], F32)
        self.rm_f = sb("rm_f", [128, 28 * 8], F32)
        self.rm_bf = sb("rm_bf", [128, 28 * 8], BF16)
        self.rstd = sb("rstd", [128, T], F32)
        self.eps_col = sb("eps_col", [128, 1], F32)
        self.one_col = sb("one_col", [128, 1], F32)
        self.ps = kb.es.enter_context(nc.psum_tensor("ps", [128, 8, 512], F32))
        self.ringA = Ring(kb, "rA", 5, 16)
        self.ringB = Ring(kb, "rB", 4, 22)
        self.bank_free = [None] * 8
        self.rstd_free = None
        self.x_ds = DSem(kb, "xld")
        self.y_ds = DSem(kb, "yst")
        self.stg_ds = [DSem(kb, f"stg{i}") for i in range(4)]
        self.cs_ds = DSem(kb, "csld")
        self.om_ds = DSem(kb, "omld")
        self.ld_ds = DSem(kb, "mixld")
        self.ost_ds = [DSem(kb, "ost0"), DSem(kb, "ost1")]
        self.og_ds = [DSem(kb, "og0"), DSem(kb, "og1")]
        self.stg_i = 0
        cs = DSem(kb, "cst")
        toks = [kb.dma(kb.sp, self.gain_sb[:], self.gains, cs),
                kb.dma(kb.sp, self.gb_sb[:], self.gbias, cs),
                kb.dma(kb.sp, self.qkg_sb[:], self.qkg, cs),
                kb.dma(kb.sp, self.onorm_sb[:], self.onorm, cs),
                kb.dma(kb.sp, self.ones_f[:], self.c_ones, cs),
                kb.dma(kb.sp, self.rot_f[:], self.c_rotT, cs),
                kb.dma(kb.sp, self.tri_f[:], self.c_tri.rearrange("k p n -> p k n"), cs),
                kb.dma(kb.sp, self.rm_f[:], self.c_rm, cs)]
        kb.dve.wait(toks)
        nc.vector.memset(self.eps_col[:], EPS)
        nc.vector.memset(self.one_col[:], 1.0)
        nc.vector.tensor_copy(out=self.ones_bf[:], in_=self.ones_f[:])
        nc.vector.tensor_copy(out=self.rotT_bf[:], in_=self.rot_f[:])
        nc.vector.tensor_copy(out=self.rm_bf[:], in_=self.rm_f[:])
        ins = nc.vector.tensor_scalar(out=self.gainh_sb[:], in0=self.gain_sb[:], scalar1=0.5, scalar2=None,
                                      op0=ALU.mult)
        kb.dve.ms(ins)
        kb.barrier()
        for l in range(L + 1):
            self.tl_phase(l)
            if l < L:
                for s in range(self.nseq):
                    if self.mode in ("full", "A"):
                        self.mixer_a(l, s)
                    if self.mode in ("full", "B"):
                        self.mixer_b(l, s)
                    if self.mode in ("full", "C"):
                        self.mixer_c(l, s)


def _consts():
    c = {}
    c["c_ones"] = np.ones((128, 128), np.float32)
    R = np.zeros((128, 128), np.float32)
    for i in range(64):
        R[2 * i, 2 * i + 1] = -1.0
        R[2 * i + 1, 2 * i] = 1.0
    c["c_rotT"] = np.ascontiguousarray(R.T)
    t = np.arange(S)
    pos_r = (t // GRID_W).astype(np.float32)
    pos_c = (t % GRID_W).astype(np.float32)
    half = 64
    inv = (np.float32(10000.0) ** (-np.arange(0, half, 2, dtype=np.float32) / half)).astype(np.float32)
    ang = np.concatenate([pos_r[:, None] * inv, pos_c[:, None] * inv], axis=-1).astype(np.float32)
    cosT = np.cos(ang).T.astype(np.float32)
    sinT = np.sin(ang).T.astype(np.float32)
    c["c_cos"] = np.ascontiguousarray(np.repeat(cosT, 2, axis=0))
    c["c_sin"] = np.ascontiguousarray(np.repeat(sinT, 2, axis=0))
    j = np.arange(128)[:, None]
    i = np.arange(128)[None, :]
    c["c_tri"] = np.stack([(j <= i), (j > i), (j >= i), (j < i)]).astype(np.float32)
    KCS = [list(range(0, 6)), list(range(2, 10)), list(range(6, 14)), list(range(10, 16))]
    rm = np.zeros((128, 28, 8), np.float32)
    pi = 0
    for qt in range(4):
        for kc in KCS[qt]:
            for krl in range(2):
                kr = 2 * kc + krl
                for qrl in range(8):
                    qr = 8 * qt + qrl
                    rs = min(max(qr - 4, 0), 24)
                    if rs <= kr <= rs + 7:
                        rm[krl * 64:(krl + 1) * 64, pi, qrl] = 1.0
            pi += 1
    c["c_rm"] = rm.reshape(128, 28 * 8)
    return c


def _rpb_table(rpb):
    L = rpb.shape[0]
    out = np.full((L, 4, 128, 31, 64), -30000.0, np.float32)
    kcol = np.arange(64)[:, None]
    qcol = np.arange(64)[None, :]
    cs = np.clip(qcol - 8, 0, 48)
    cm = (kcol >= cs) & (kcol <= cs + 15)
    dc = np.clip(kcol - qcol + 15, 0, 30)
    for krl in range(2):
        for a2 in range(31):
            a = a2 - 8 - krl
            if 0 <= a <= 14:
                vals = rpb[:, :, 14 - a, :][:, :, dc]
                out[:, :, krl * 64:(krl + 1) * 64, a2, :] = np.where(cm[None, None], vals, np.float32(-30000.0))
    return out.reshape(L, 4, 128, 31 * 64)


def host_prep(inputs, depth):
    g = lambda k: np.asarray(inputs[k], dtype=np.float32)
    L = depth
    out = {}
    out["w_f1i"] = np.stack([pretile(g("w_ffn1_in")[l]) for l in range(L)])
    out["w_f1o"] = np.stack([pretile(g("w_ffn1_out")[l]) for l in range(L)])
    out["w_f2i"] = np.stack([pretile(g("w_ffn2_in")[l]) for l in range(L)])
    out["w_f2o"] = np.stack([pretile(g("w_ffn2_out")[l]) for l in range(L)])
    w_in = g("w_in")
    wm = np.zeros((L, D, NMIXB * 128), np.float32)
    wm[:, :, :NMIX] = w_in[:L, :, :NMIX]
    out["w_mix"] = np.stack([pretile(wm[l]) for l in range(L)])
    out["w_gate"] = np.stack([pretile(w_in[l][:, NMIX:]) for l in range(L)])
    out["w_br"] = np.stack([pretile(np.concatenate([g("w_br_a")[l], g("w_br_b")[l], g("w_br_c")[l]], axis=0))
                            for l in range(L)])
    out["w_o"] = np.stack([pretile(g("w_out")[l]) for l in range(L)])
    ng = g("norm_gains")[:L]
    out["gains"] = np.ascontiguousarray(ng.reshape(L, 6, 16, 128).transpose(3, 0, 1, 2).reshape(128, L * 6 * 16))
    gbi = g("gate_bias")[:L]
    out["gbias"] = np.ascontiguousarray(gbi.reshape(L, 3, 16, 128).transpose(3, 0, 1, 2).reshape(128, L * 3 * 16))
    out["qkg"] = np.ascontiguousarray(g("qk_norm_a")[:L].transpose(2, 0, 1).reshape(128, L * 2))
    out["onorm"] = np.ascontiguousarray(g("onorm_c")[:L].T)
    out["rpbT"] = _rpb_table(g("rpb_b")[:L])
    w2 = g("w_decay_c")[:L]
    b2 = g("b_decay_c")[:L]
    w2e = np.zeros((L, 2, 33, 256), np.float32)
    w2e[:, 0, 0:16, :] = w2[:, 0]
    w2e[:, 1, 16:32, :] = w2[:, 1]
    w2e[:, :, 32, :] = b2
    out["w2e"] = w2e
    out.update(_consts())
    return out


def to_fm(x):
    n = x.shape[0]
    return np.ascontiguousarray(x.reshape(n, S, NCH, 128).transpose(0, 2, 3, 1))


def from_fm(y):
    n = y.shape[0]
    return np.ascontiguousarray(y.transpose(0, 3, 1, 2).reshape(n, S, D))


_PROG = {}


def kernel(**inputs):
    n_cores = 8
    depth = int(np.asarray(inputs["norm_gains"]).shape[0])
    xp = np.asarray(inputs["x_prompt"], dtype=np.float32)
    xs = np.asarray(inputs["x_sample"], dtype=np.float32)
    nb_p, nb_s = xp.shape[0], xs.shape[0]
    xall = np.concatenate([xp, xs], axis=0)
    nseq = xall.shape[0] // n_cores
    key = (nseq, depth)
    if key not in _PROG:
        _PROG[key] = Prog(nseq, depth)
    prog = _PROG[key]
    hp = host_prep(inputs, depth)
    in_maps = []
    for c in range(n_cores):
        m = dict(hp)
        m["xin"] = to_fm(xall[c * nseq:(c + 1) * nseq])
        in_maps.append(m)
    res = run_bass_kernel_spmd(prog.nc, in_maps, core_ids=list(range(n_cores)))
    ys = [from_fm(np.asarray(r["yout"])) for r in res.results]
    yall = np.concatenate(ys, axis=0)
    return yall[:nb_p], yall[nb_p:nb_p + nb_s]
```

```python
import numpy as np
from contextlib import ExitStack
import concourse.bass as bass
import concourse.mybir as mybir
from concourse.bass_utils import run_bass_kernel_spmd

F32 = mybir.dt.float32
BF16 = mybir.dt.bfloat16
AF = mybir.ActivationFunctionType
ALU = mybir.AluOpType

D = 2048
S = 2048
DFF = 5632
NCH = 16
T = 512
NTT = S // T
EPS = 1e-6
NMIX = 4640
NMIXB = 37
GRID_W = 64


class Eng:
    def __init__(self, kb, raw, name):
        self.raw = raw
        self.name = name
        self.sem = kb.newsem("s_" + name)
        self.cnt = 0
        self.seen = {}

    def wait(self, *toks):
        for t in toks:
            if t is None:
                continue
            if isinstance(t, list):
                self.wait(*t)
                continue
            sem, v = t
            if self.seen.get(id(sem), 0) >= v:
                continue
            self.raw.wait_ge(sem, v)
            self.seen[id(sem)] = v

    def ms(self, ins):
        self.cnt += 1
        ins.then_inc(self.sem, 1)
        return (self.sem, self.cnt)


class DSem:
    def __init__(self, kb, name):
        self.sem = kb.newsem(name)
        self.cnt = 0
        kb.dsems.append(self)


class KB:
    def __init__(self):
        self.nc = bass.Bass("TRN2", target_bir_lowering=False)
        self.es = ExitStack()
        nc = self.nc
        self.work_sems = []
        self.dsems = []
        self.bar_a = self.es.enter_context(nc.semaphore("bar_a"))
        self.bar_b = self.es.enter_context(nc.semaphore("bar_b"))
        self.bar_k = 0
        self.pe = Eng(self, nc.tensor, "pe")
        self.act = Eng(self, nc.scalar, "act")
        self.dve = Eng(self, nc.vector, "dve")
        self.pool = Eng(self, nc.gpsimd, "pool")
        self.sp = Eng(self, nc.sync, "sp")
        self.engs = [self.pe, self.act, self.dve, self.pool, self.sp]
        self.pending = []

    def newsem(self, name):
        sem = self.es.enter_context(self.nc.semaphore(name))
        self.work_sems.append(sem)
        return sem

    def sb(self, name, shape, dt, es=None):
        self.uid = getattr(self, "uid", 0) + 1
        return (es or self.es).enter_context(self.nc.sbuf_tensor(f"{name}_{self.uid}", shape, dt))

    def dma(self, q, out, in_, ds, track=False):
        ds.cnt += 16
        q.raw.dma_start(out=out, in_=in_).then_inc(ds.sem, 16)
        tok = (ds.sem, ds.cnt)
        if track:
            self.pending.append(tok)
        return tok

    def barrier(self):
        best = {}
        for sem, v in [(e.sem, e.cnt) for e in self.engs if e.cnt > 0] + self.pending:
            if id(sem) not in best or best[id(sem)][1] < v:
                best[id(sem)] = (sem, v)
        toks = list(best.values())
        for e in self.engs:
            e.wait(*toks)
        self.pending = []


class Ring:
    def __init__(self, kb, name, n, kc):
        self.kb = kb
        self.n = n
        self.kc = kc
        self.name = name
        self.ds = [DSem(kb, f"{name}d{i}") for i in range(n)]
        self.bufs = None
        self.free = [None] * n
        self.i = 0

    def alloc(self, es):
        self.bufs = [self.kb.sb(f"{self.name}{i}", [128, self.kc, 128], BF16, es) for i in range(self.n)]
        self.free = [None] * self.n

    def load(self, src, kc):
        s = self.i % self.n
        self.i += 1
        kb = self.kb
        kb.pool.wait(self.free[s])
        tok = kb.dma(kb.pool, self.bufs[s][:, 0:kc, :], src, self.ds[s])
        return self.bufs[s], tok, s

    def release(self, s, tok):
        self.free[s] = tok


def pretile(w):
    K_, N_ = w.shape
    return np.ascontiguousarray(w.reshape(K_ // 128, 128, N_ // 128, 128).transpose(2, 1, 0, 3))


class Prog:
    def __init__(self, nseq, depth, mode="full"):
        self.nseq = nseq
        self.depth = depth
        self.mode = mode
        kb = self.kb = KB()
        nc = self.nc = kb.nc
        L = depth
        di = lambda name, shape, dt=F32: nc.dram_tensor(name, shape, dt, kind="ExternalInput").ap()
        self.xin = di("xin", [nseq, NCH, 128, S])
        self.yout = nc.dram_tensor("yout", [nseq, NCH, 128, S], F32, kind="ExternalOutput").ap()
        self.w_f1i = di("w_f1i", [L, 88, 128, 16, 128])
        self.w_f1o = di("w_f1o", [L, 16, 128, 44, 128])
        self.w_f2i = di("w_f2i", [L, 88, 128, 16, 128])
        self.w_f2o = di("w_f2o", [L, 16, 128, 44, 128])
        self.w_mix = di("w_mix", [L, NMIXB, 128, 16, 128])
        self.w_gate = di("w_gate", [L, 48, 128, 16, 128])
        self.w_br = di("w_br", [L, 16, 128, 16, 128])
        self.w_o = di("w_o", [L, 16, 128, 16, 128])
        self.gains = di("gains", [128, L * 6 * 16])
        self.gbias = di("gbias", [128, L * 3 * 16])
        self.qkg = di("qkg", [128, L * 2])
        self.onorm = di("onorm", [128, L])
        self.rpbT = di("rpbT", [L, 4, 128, 31 * 64])
        self.w2e = di("w2e", [L, 2, 33, 256])
        self.c_ones = di("c_ones", [128, 128])
        self.c_rotT = di("c_rotT", [128, 128])
        self.c_cos = di("c_cos", [128, S])
        self.c_sin = di("c_sin", [128, S])
        self.c_tri = di("c_tri", [4, 128, 128])
        self.c_rm = di("c_rm", [128, 28 * 8])
        dt_ = lambda name, shape: nc.dram_tensor(name, shape, BF16).ap()
        self.d_aq = dt_("d_aq", [nseq, 8, 128, S])
        self.d_ak = dt_("d_ak", [nseq, 2, 128, S])
        self.d_av = dt_("d_av", [nseq, 16, 128, 256])
        self.d_bq = dt_("d_bq", [nseq, 4, 128, S])
        self.d_bk = dt_("d_bk", [nseq, 4, 128, S])
        self.d_bv = dt_("d_bv", [nseq, 16, 128, 512])
        self.d_cq = dt_("d_cq", [nseq, 2, 128, S])
        self.d_ck = dt_("d_ck", [nseq, 2, 128, S])
        self.d_ckt = dt_("d_ckt", [nseq, 16, 128, 256])
        self.d_cv = dt_("d_cv", [nseq, 16, 128, 512])
        self.d_cog = dt_("d_cog", [nseq, 4, 128, S])
        self.d_clr = dt_("d_clr", [nseq, 32, S])
        self.d_om = dt_("d_om", [nseq, 16, 128, S])
        self.build()

    def gcol(self, l, i, c):
        return self.gain_sb[:, (l * 6 + i) * 16 + c:(l * 6 + i) * 16 + c + 1]

    def ghcol(self, l, i, c):
        return self.gainh_sb[:, (l * 6 + i) * 16 + c:(l * 6 + i) * 16 + c + 1]

    def rms_stats(self, src, nchunks, sqbuf, bank_idx, scale_n, rstd=None):
        kb = self.kb
        nc = self.nc
        rstd = self.rstd if rstd is None else rstd
        bank = self.ps[:, bank_idx, :]
        a = kb.act.ms(nc.scalar.activation(out=sqbuf[:, 0:nchunks, :], in_=src, func=AF.Square))
        kb.pe.wait(a, self.bank_free[bank_idx])
        for c in range(nchunks):
            ins = nc.tensor.matmul(bank, self.ones_bf[:], sqbuf[:, c, :], start=(c == 0), stop=(c == nchunks - 1))
        p = kb.pe.ms(ins)
        return self.stats_finish(p, bank_idx, scale_n, rstd)

    def stats_finish(self, p_tok, bank_idx, scale_n, rstd=None):
        kb, nc = self.kb, self.nc
        rstd = self.rstd if rstd is None else rstd
        kb.act.wait(p_tok, self.rstd_free)
        a2 = kb.act.ms(nc.scalar.activation(out=rstd[:], in_=self.ps[:, bank_idx, :], func=AF.Sqrt,
                                            bias=self.eps_col[:], scale=1.0 / scale_n))
        self.bank_free[bank_idx] = a2
        kb.dve.wait(a2)
        d = kb.dve.ms(nc.vector.reciprocal(out=rstd[:], in_=rstd[:]))
        return d

    def emit_ones(self, n, a):
        kb, nc = self.kb, self.nc
        kb.pe.wait(a)
        if n == 0:
            kb.pe.wait(self.bank_free[6])
        tok = kb.pe.ms(nc.tensor.matmul(self.ps[:, 6, :], self.ones_bf[:], self.sqs[:, n % 2, :], start=(n == 0),
                                        stop=(n == NCH - 1)))
        self.sqs_free[n % 2] = tok
        return tok

    def prenorm(self, l, gi):
        kb, nc = self.kb, self.nc
        if self.xstats is not None:
            p = self.xstats
            self.xstats = None
            self.stats_finish(p, 7, D)
        else:
            kb.act.wait(self.x_ready, self.xn_free)
            self.rms_stats(self.xT[:], NCH, self.xnT, 6, D)
        kb.dve.wait(self.x_ready)
        for c in range(NCH):
            ins = nc.vector.scalar_tensor_tensor(out=self.xnT[:, c, :], in0=self.xT[:, c, :],
                                                 scalar=self.gcol(l, gi, c), in1=self.rstd[:],
                                                 op0=ALU.mult, op1=ALU.mult)
        t = kb.dve.ms(ins)
        self.rstd_free = t
        self.last_xn = t
        return t

    def postnorm_residual(self, l, gi, half, stats_tok, next_stats):
        kb, nc = self.kb, self.nc
        gsel = self.ghcol if half else self.gcol
        self.stats_finish(stats_tok, 6, D)
        if next_stats:
            kb.act.wait(self.xn_free)
            kb.pe.wait(self.bank_free[7])
        for c in range(NCH):
            nc.vector.scalar_tensor_tensor(out=self.outT[:, c, :], in0=self.outT[:, c, :], scalar=gsel(l, gi, c),
                                           in1=self.rstd[:], op0=ALU.mult, op1=ALU.mult)
            ins = nc.vector.tensor_tensor(out=self.xT[:, c, :], in0=self.xT[:, c, :], in1=self.outT[:, c, :],
                                          op=ALU.add)
            if next_stats or c == NCH - 1:
                dc = kb.dve.ms(ins)
            if next_stats:
                kb.act.wait(dc)
                a = kb.act.ms(nc.scalar.activation(out=self.xnT[:, c, :], in_=self.xT[:, c, :], func=AF.Square))
                kb.pe.wait(a)
                pins = nc.tensor.matmul(self.ps[:, 7, :], self.ones_bf[:], self.xnT[:, c, :], start=(c == 0),
                                        stop=(c == NCH - 1))
        self.x_ready = dc
        if next_stats:
            self.xstats = kb.pe.ms(pins)
        self.out_free = self.x_ready
        self.rstd_free = self.x_ready
        self.xn_free = self.xstats if next_stats else self.x_ready

    def ffn(self, l, which, next_stats):
        kb, nc = self.kb, self.nc
        w_in = (self.w_f1i if which == 0 else self.w_f2i)[l]
        w_out = (self.w_f1o if which == 0 else self.w_f2o)[l]
        PS = self.ps
        xnT, actT, outT = self.xnT, self.actT, self.outT
        xn_ready = self.prenorm(l, 0 if which == 0 else 4)
        for j in range(DFF // 128):
            par = j % 2
            gb, ub = 2 * par, 2 * par + 1
            wg, tg, sg = self.ringA.load(w_in[j], 16)
            wu, tu, su = self.ringA.load(w_in[44 + j], 16)
            kb.pe.wait(xn_ready, tg, self.bank_free[gb])
            for c in range(NCH):
                ins = nc.tensor.matmul(PS[:, gb, :], wg[:, c, :], xnT[:, c, :], start=(c == 0), stop=(c == NCH - 1))
            pg = kb.pe.ms(ins)
            self.ringA.release(sg, pg)
            kb.pe.wait(tu, self.bank_free[ub])
            for c in range(NCH):
                ins = nc.tensor.matmul(PS[:, ub, :], wu[:, c, :], xnT[:, c, :], start=(c == 0), stop=(c == NCH - 1))
            pu = kb.pe.ms(ins)
            self.ringA.release(su, pu)
            kb.act.wait(pg, self.sg_free[par])
            a = kb.act.ms(nc.scalar.activation(out=self.sgbuf[:, par, :], in_=PS[:, gb, :], func=AF.Silu))
            self.bank_free[gb] = a
            kb.dve.wait(a, pu, self.act_free)
            dd = kb.dve.ms(nc.vector.tensor_tensor(out=actT[:, j, :], in0=self.sgbuf[:, par, :], in1=PS[:, ub, :],
                                                   op=ALU.mult))
            self.bank_free[ub] = dd
            self.sg_free[par] = dd
        act_ready = dd
        self.xn_free = pu
        prev = None
        for n in range(NCH):
            bk = 4 + n % 2
            wo, to, so = self.ringB.load(w_out[n], 44)
            kb.pe.wait(act_ready, to, self.bank_free[bk])
            for f in range(44):
                ins = nc.tensor.matmul(PS[:, bk, :], wo[:, f, :], actT[:, f, :], start=(f == 0), stop=(f == 43))
            p = kb.pe.ms(ins)
            self.ringB.release(so, p)
            kb.act.wait(p, self.out_free, self.sqs_free[n % 2])
            nc.scalar.copy(out=outT[:, n, :], in_=PS[:, bk, :])
            a = kb.act.ms(nc.scalar.activation(out=self.sqs[:, n % 2, :], in_=PS[:, bk, :], func=AF.Square))
            self.bank_free[bk] = a
            if prev is not None:
                self.emit_ones(*prev)
            prev = (n, a)
        pst = self.emit_ones(*prev)
        self.act_free = p
        self.postnorm_residual(l, 1 if which == 0 else 5, True, pst, next_stats)

    def stage_out(self, src_tok, dst_ap, src_ap, si):
        kb = self.kb
        kb.sp.wait(src_tok)
        return kb.dma(kb.sp, dst_ap, src_ap, self.stg_ds[si], track=True)

    def get_stage(self):
        i = self.stg_i % 4
        self.stg_i += 1
        return i

    def proj(self, l, s, tt):
        kb, nc = self.kb, self.nc
        PS = self.ps
        xnT = self.xnT
        tsl = slice(tt * T, (tt + 1) * T)
        xn_ready = self.prenorm(l, 2)
        kb.sp.wait(self.cs_free)
        tc1 = kb.dma(kb.sp, self.cosb[:], self.c_cos[:, tsl], self.cs_ds)
        tc2 = kb.dma(kb.sp, self.sinb[:], self.c_sin[:, tsl], self.cs_ds)
        plan = []
        for h in range(8):
            plan.append((h, "rope", self.d_aq[s, h, :, tsl], 0))
        for h in range(2):
            plan.append((8 + h, "rope", self.d_ak[s, h, :, tsl], 1))
        for i in range(2):
            plan.append((10 + i, "tok", self.d_av[s, tt * 4:(tt + 1) * 4, :, i * 128:(i + 1) * 128], None))
        for i in range(4):
            plan.append((12 + i, "copy", self.d_bq[s, i, :, tsl], 1.0))
        for i in range(4):
            plan.append((16 + i, "copy", self.d_bk[s, i, :, tsl], 1.0))
        for i in range(4):
            plan.append((20 + i, "tok", self.d_bv[s, tt * 4:(tt + 1) * 4, :, i * 128:(i + 1) * 128], None))
        for i in range(2):
            plan.append((24 + i, "copy", self.d_cq[s, i, :, tsl], 0.125))
        for i in range(2):
            plan.append((26 + i, "copy", self.d_ck[s, i, :, tsl], 1.0))
        for i in range(2):
            plan.append((26 + i, "tok", self.d_ckt[s, tt * 4:(tt + 1) * 4, :, i * 128:(i + 1) * 128], None))
        for i in range(4):
            plan.append((28 + i, "tok", self.d_cv[s, tt * 4:(tt + 1) * 4, :, i * 128:(i + 1) * 128], None))
        for i in range(4):
            plan.append((32 + i, "copy", self.d_cog[s, i, :, tsl], 1.0))
        plan.append((36, "lr", self.d_clr[s, :, tsl], 1.0))
        last_dve = None
        for it, (blk, kind, dst, par_) in enumerate(plan):
            bank = it % 4
            w, tw, sw = self.ringA.load(self.w_mix[l, blk], 16)
            kb.pe.wait(xn_ready, tw, self.bank_free[bank])
            if kind == "tok":
                for sub in range(4):
                    for c in range(NCH):
                        ins = nc.tensor.matmul(PS[:, bank, sub * 128:(sub + 1) * 128],
                                               xnT[:, c, sub * 128:(sub + 1) * 128], w[:, c, :],
                                               start=(c == 0), stop=(c == NCH - 1))
            else:
                for c in range(NCH):
                    ins = nc.tensor.matmul(PS[:, bank, :], w[:, c, :], xnT[:, c, :], start=(c == 0),
                                           stop=(c == NCH - 1))
            p = kb.pe.ms(ins)
            self.ringA.release(sw, p)
            si = self.get_stage()
            stg = self.stg[:, si, :]
            if kind in ("copy", "tok", "lr"):
                kb.act.wait(p, self.stg_free[si])
                if kind == "copy" and par_ != 1.0:
                    a = kb.act.ms(nc.scalar.mul(out=stg, in_=PS[:, bank, :], mul=float(par_)))
                else:
                    a = kb.act.ms(nc.scalar.copy(out=stg, in_=PS[:, bank, :]))
                self.bank_free[bank] = a
                if kind == "tok":
                    st = self.stage_out(a, dst.rearrange("k p n -> p k n"),
                                        self.stg[:, si, :].rearrange("p (k n) -> p k n", k=4), si)
                elif kind == "lr":
                    st = self.stage_out(a, dst, self.stg[0:32, si, :], si)
                else:
                    st = self.stage_out(a, dst, stg, si)
                self.stg_free[si] = st
            else:
                gcol = self.qkg_sb[:, l * 2 + par_:l * 2 + par_ + 1]
                kb.act.wait(p, self.sq1_free)
                a = kb.act.ms(nc.scalar.activation(out=self.sq1[:, 0, :], in_=PS[:, bank, :], func=AF.Square))
                kb.pe.wait(a, self.bank_free[5])
                p2 = kb.pe.ms(nc.tensor.matmul(PS[:, 5, :], self.ones_bf[:], self.sq1[:, 0, :], start=True, stop=True))
                self.sq1_free = p2
                kb.act.wait(p2, self.rq_free)
                a2 = kb.act.ms(nc.scalar.activation(out=self.rq[:], in_=PS[:, 5, :], func=AF.Sqrt,
                                                    bias=self.eps_col[:], scale=1.0 / 128))
                self.bank_free[5] = a2
                kb.dve.wait(a2, p, self.qn_free)
                nc.vector.reciprocal(out=self.rq[:], in_=self.rq[:])
                d1 = kb.dve.ms(nc.vector.scalar_tensor_tensor(out=self.qn[:], in0=PS[:, bank, :], scalar=gcol,
                                                              in1=self.rq[:], op0=ALU.mult, op1=ALU.mult))
                self.bank_free[bank] = d1
                self.rq_free = d1
                kb.pe.wait(d1, self.bank_free[4])
                p3 = kb.pe.ms(nc.tensor.matmul(PS[:, 4, :], self.rotT_bf[:], self.qn[:], start=True, stop=True))
                kb.dve.wait(p3, tc1, tc2, self.stg_free[si])
                nc.vector.tensor_tensor(out=self.t1[:], in0=self.qn[:], in1=self.cosb[:], op=ALU.mult)
                nc.vector.tensor_tensor(out=self.t2[:], in0=PS[:, 4, :], in1=self.sinb[:], op=ALU.mult)
                d2 = kb.dve.ms(nc.vector.tensor_tensor(out=stg, in0=self.t1[:], in1=self.t2[:], op=ALU.add))
                self.bank_free[4] = d2
                self.qn_free = d2
                last_dve = d2
                st = self.stage_out(d2, dst, stg, si)
                self.stg_free[si] = st
        self.cs_free = last_dve
        self.xn_free = p

    def merge(self, l, s, tt):
        kb, nc = self.kb, self.nc
        PS = self.ps
        xnT, actT, outT = self.xnT, self.actT, self.outT
        tsl = slice(tt * T, (tt + 1) * T)
        omT = actT[:, 0:16, :]
        mgT = actT[:, 16:32, :]
        kb.sp.wait(self.act_free)
        t_om = kb.dma(kb.sp, omT, self.d_om[s, :, :, tsl].rearrange("c p t -> p c t"), self.om_ds)
        xn_ready = self.prenorm(l, 2)
        segs = [(0, 8), (8, 12), (12, 16)]
        for n in range(NCH):
            wgs = [self.ringA.load(self.w_gate[l, br * 16 + n], 16) for br in range(3)]
            wb, tb, sbr = self.ringA.load(self.w_br[l, n], 16)
            for br in range(3):
                w, tw, sw = wgs[br]
                kb.pe.wait(xn_ready, tw, self.bank_free[br])
                for c in range(NCH):
                    ins = nc.tensor.matmul(PS[:, br, :], w[:, c, :], xnT[:, c, :], start=(c == 0), stop=(c == NCH - 1))
                pg = kb.pe.ms(ins)
                self.ringA.release(sw, pg)
                kb.act.wait(pg, self.sig_free[br])
                bcol = self.gb_sb[:, (l * 3 + br) * 16 + n:(l * 3 + br) * 16 + n + 1]
                a = kb.act.ms(nc.scalar.activation(out=self.sig[:, br, :], in_=PS[:, br, :], func=AF.Sigmoid,
                                                   bias=bcol, scale=1.0))
                self.bank_free[br] = a
                wgs[br] = a
            kb.pe.wait(tb, t_om)
            for br in range(3):
                c0, c1 = segs[br]
                kb.pe.wait(self.bank_free[3 + br])
                for c in range(c0, c1):
                    ins = nc.tensor.matmul(PS[:, 3 + br, :], wb[:, c, :], omT[:, c, :], start=(c == c0),
                                           stop=(c == c1 - 1))
            py = kb.pe.ms(ins)
            self.ringA.release(sbr, py)
            kb.dve.wait(py, wgs[0], wgs[1], wgs[2], self.mg_free)
            nc.vector.tensor_tensor(out=self.macc[:], in0=self.sig[:, 0, :], in1=PS[:, 3, :], op=ALU.mult)
            nc.vector.tensor_tensor(out=self.t1[:], in0=self.sig[:, 1, :], in1=PS[:, 4, :], op=ALU.mult)
            nc.vector.tensor_tensor(out=self.macc[:], in0=self.macc[:], in1=self.t1[:], op=ALU.add)
            nc.vector.tensor_tensor(out=self.t1[:], in0=self.sig[:, 2, :], in1=PS[:, 5, :], op=ALU.mult)
            d = kb.dve.ms(nc.vector.tensor_tensor(out=mgT[:, n, :], in0=self.macc[:], in1=self.t1[:], op=ALU.add))
            for br in range(3):
                self.bank_free[3 + br] = d
                self.sig_free[br] = d
        mg_ready = d
        self.xn_free = pg
        prev = None
        for n in range(NCH):
            bk = n % 2
            w, tw, sw = self.ringA.load(self.w_o[l, n], 16)
            kb.pe.wait(mg_ready, tw, self.bank_free[bk])
            for c in range(NCH):
                ins = nc.tensor.matmul(PS[:, bk, :], w[:, c, :], mgT[:, c, :], start=(c == 0), stop=(c == NCH - 1))
            p = kb.pe.ms(ins)
            self.ringA.release(sw, p)
            kb.act.wait(p, self.out_free, self.sqs_free[n % 2])
            nc.scalar.copy(out=outT[:, n, :], in_=PS[:, bk, :])
            a = kb.act.ms(nc.scalar.activation(out=self.sqs[:, n % 2, :], in_=PS[:, bk, :], func=AF.Square))
            self.bank_free[bk] = a
            if prev is not None:
                self.emit_ones(*prev)
            prev = (n, a)
        pst = self.emit_ones(*prev)
        self.act_free = p
        self.mg_free = p
        self.postnorm_residual(l, 3, False, pst, True)

    def tl_phase(self, l):
        kb, nc = self.kb, self.nc
        L = self.depth
        with ExitStack() as es:
            self.xT = kb.sb("xT", [128, NCH, T], F32, es)
            self.xnT = kb.sb("xnT", [128, NCH, T], BF16, es)
            self.actT = kb.sb("actT", [128, 44, T], BF16, es)
            self.outT = kb.sb("outT", [128, NCH, T], F32, es)
            self.sgbuf = kb.sb("sgbuf", [128, 2, T], F32, es)
            self.stg = kb.sb("stg", [128, 4, T], BF16, es)
            self.cosb = kb.sb("cosb", [128, T], F32, es)
            self.sinb = kb.sb("sinb", [128, T], F32, es)
            self.sqs = kb.sb("sqs", [128, 2, T], BF16, es)
            self.sq1 = self.sqs[:, 0:1, :]
            self.sqs_free = [None, None]
            self.xstats = None
            self.rq = kb.sb("rq", [128, T], F32, es)
            self.qn = kb.sb("qn", [128, T], BF16, es)
            self.t1 = kb.sb("t1", [128, T], F32, es)
            self.t2 = kb.sb("t2", [128, T], F32, es)
            self.macc = kb.sb("macc", [128, T], F32, es)
            self.sig = kb.sb("sig", [128, 3, T], F32, es)
            self.ringA.alloc(es)
            self.ringB.alloc(es)
            self.sg_free = [None, None]
            self.sig_free = [None, None, None]
            self.stg_free = [None] * 4
            self.act_free = self.out_free = self.xn_free = self.rstd_free = None
            self.sq1_free = self.rq_free = self.qn_free = self.cs_free = self.mg_free = None
            self.x_free = None
            self.last_xn = None
            self.bank_free = [None] * 8
            for s in range(self.nseq):
                for tt in range(NTT):
                    tsl = slice(tt * T, (tt + 1) * T)
                    src = self.xin if l == 0 else self.yout
                    kb.sp.wait(self.x_free, self.last_xn)
                    self.x_ready = kb.dma(kb.sp, self.xT[:], src[s, :, :, tsl].rearrange("c p t -> p c t"), self.x_ds)
                    self.xstats = None
                    if l > 0:
                        self.merge(l - 1, s, tt)
                        self.ffn(l - 1, 1, l < L)
                    if l < L:
                        self.ffn(l, 0, True)
                    kb.sp.wait(self.x_ready)
                    self.x_free = kb.dma(kb.sp, self.yout[s, :, :, tsl].rearrange("c p t -> p c t"), self.xT[:],
                                         self.y_ds, track=True)
                    if l < L:
                        self.proj(l, s, tt)
            kb.barrier()

    def attn_block(self, kT, qrhs, vfn, kcs, scale, maskfn, dst):
        kb, nc = self.kb, self.nc
        PS = self.ps
        LA = 2
        blk = self.ablk
        self.ablk += 1
        ob, db = 4 + blk % 2, 6 + blk % 2
        n = len(kcs)
        qk_tok = [None] * n
        qk_bank = [None] * n
        pv = None
        for i in range(n + LA):
            if i < n:
                bank = self.sbank % 4
                self.sbank += 1
                kb.pe.wait(self.bank_free[bank])
                qk_tok[i] = kb.pe.ms(nc.tensor.matmul(PS[:, bank, :], kT(kcs[i]), qrhs, start=True, stop=True))
                qk_bank[i] = bank
            j = i - LA
            if j >= 0:
                slot = self.pslot % 4
                self.pslot += 1
                pb = self.pbuf[:, slot, :]
                kb.act.wait(qk_tok[j], self.pbuf_free[slot])
                a = kb.act.ms(nc.scalar.activation(out=pb, in_=PS[:, qk_bank[j], :], func=AF.Exp, scale=float(scale)))
                self.bank_free[qk_bank[j]] = a
                ptok = a
                if maskfn is not None:
                    ge, rm = maskfn(kcs[j])
                    kb.dve.wait(a)
                    pb3 = pb.rearrange("p (r c) -> p r c", r=8)
                    nc.vector.tensor_tensor(out=pb3, in0=pb3, in1=ge, op=ALU.mult)
                    ptok = kb.dve.ms(nc.vector.tensor_tensor(out=pb3, in0=pb3, in1=rm, op=ALU.mult))
                kb.pe.wait(ptok)
                if j == 0:
                    kb.pe.wait(self.bank_free[ob], self.bank_free[db])
                nc.tensor.matmul(PS[:, ob, :], vfn(kcs[j]), pb, start=(j == 0), stop=(j == n - 1))
                pv = kb.pe.ms(nc.tensor.matmul(PS[:, db, :], self.ones_bf[:], pb, start=(j == 0), stop=(j == n - 1)))
                self.pbuf_free[slot] = pv
        so = blk % 2
        kb.dve.wait(pv, self.ost_free[so])
        nc.vector.reciprocal(out=self.rden[:], in_=PS[:, db, :])
        d = kb.dve.ms(nc.vector.tensor_tensor(out=self.ost[:, so, :], in0=PS[:, ob, :], in1=self.rden[:], op=ALU.mult))
        self.bank_free[ob] = d
        self.bank_free[db] = d
        kb.sp.wait(d)
        self.ost_free[so] = kb.dma(kb.sp, dst, self.ost[:, so, :], self.ost_ds[so], track=True)

    def attn_common_alloc(self, es):
        kb = self.kb
        self.pbuf = kb.sb("pbuf", [128, 4, 512], BF16, es)
        self.ost = kb.sb("ost", [128, 2, 512], BF16, es)
        self.rden = kb.sb("rden", [128, 512], F32, es)
        self.pbuf_free = [None] * 4
        self.ost_free = [None, None]
        self.bank_free = [None] * 8
        self.ablk = 0
        self.sbank = 0
        self.pslot = 0

    def mixer_a(self, l, s):
        kb, nc = self.kb, self.nc
        with ExitStack() as es:
            q = kb.sb("a_q", [128, 8, S], BF16, es)
            k = kb.sb("a_k", [128, 2, S], BF16, es)
            v = kb.sb("a_v", [128, 16, 256], BF16, es)
            self.attn_common_alloc(es)
            toks = [kb.dma(kb.sp, k[:], self.d_ak[s].rearrange("c p t -> p c t"), self.ld_ds),
                    kb.dma(kb.sp, v[:], self.d_av[s].rearrange("k p n -> p k n"), self.ld_ds)]
            for h in range(8):
                toks.append(kb.dma(kb.sp, q[:, h, :], self.d_aq[s, h], self.ld_ds))
            kb.pe.wait(toks)
            for h in range(8):
                kv = h // 4
                for qt in range(4):
                    self.attn_block(lambda kc: k[:, kv, kc * 128:(kc + 1) * 128], q[:, h, qt * 512:(qt + 1) * 512],
                                    lambda kc: v[:, kc, kv * 128:(kv + 1) * 128], list(range(16)),
                                    128 ** -0.5, None, self.d_om[s, h, :, qt * 512:(qt + 1) * 512])
            kb.barrier()

    def mixer_b(self, l, s):
        kb, nc = self.kb, self.nc
        with ExitStack() as es:
            q = kb.sb("b_q", [128, 4, S], BF16, es)
            k = kb.sb("b_k", [128, 4, S], BF16, es)
            v = kb.sb("b_v", [128, 16, 512], BF16, es)
            ge = kb.sb("b_ge", [128, 4, 31 * 64], BF16, es)
            gst = kb.sb("b_gst", [128, 31 * 64], F32, es)
            self.attn_common_alloc(es)
            toks = [kb.dma(kb.sp, q[:], self.d_bq[s].rearrange("c p t -> p c t"), self.ld_ds),
                    kb.dma(kb.sp, k[:], self.d_bk[s].rearrange("c p t -> p c t"), self.ld_ds),
                    kb.dma(kb.sp, v[:], self.d_bv[s].rearrange("k p n -> p k n"), self.ld_ds)]
            gfree = None
            for h in range(4):
                kb.sp.wait(gfree)
                tg = kb.dma(kb.sp, gst[:], self.rpbT[l, h], self.ld_ds)
                kb.act.wait(tg)
                gfree = kb.act.ms(nc.scalar.activation(out=ge[:, h, :], in_=gst[:], func=AF.Exp))
            kb.dve.wait(gfree)
            kb.pe.wait(toks)
            KCS = [list(range(0, 6)), list(range(2, 10)), list(range(6, 14)), list(range(10, 16))]
            pair_idx = {}
            pi = 0
            for qt in range(4):
                for kc in KCS[qt]:
                    pair_idx[(qt, kc)] = pi
                    pi += 1
            for h in range(4):
                for qt in range(4):
                    def maskfn(kc, h=h, qt=qt):
                        a0 = 8 * qt - 2 * kc + 7 + 8
                        g = ge[:, h, a0 * 64:(a0 + 8) * 64].rearrange("p (r c) -> p r c", r=8)
                        pidx = pair_idx[(qt, kc)]
                        rm = self.rm_bf[:, pidx * 8:(pidx + 1) * 8, None].broadcast_to([128, 8, 64])
                        return g, rm
                    self.attn_block(lambda kc: k[:, h, kc * 128:(kc + 1) * 128], q[:, h, qt * 512:(qt + 1) * 512],
                                    lambda kc: v[:, kc, h * 128:(h + 1) * 128], KCS[qt],
                                    128 ** -0.5, maskfn, self.d_om[s, 8 + h, :, qt * 512:(qt + 1) * 512])
            kb.barrier()

    def mixer_c(self, l, s):
        kb, nc = self.kb, self.nc
        PS = self.ps
        NC_ = 16
        with ExitStack() as es:
            q = kb.sb("c_q", [128, 2, S], BF16, es)
            k = kb.sb("c_k", [128, 2, S], BF16, es)
            kt = kb.sb("c_kt", [128, 16, 256], BF16, es)
            v = kb.sb("c_v", [128, 16, 512], BF16, es)
            lr = kb.sb("c_lr", [33, S], BF16, es)
            w2f = kb.sb("c_w2f", [33, 2, 256], F32, es)
            w2 = kb.sb("c_w2", [33, 2, 256], BF16, es)
            G = kb.sb("c_G", [128, 16, 2, 256], F32, es)
            etmp = kb.sb("c_etmp", [128, 2, 512], F32, es)
            ebuf = etmp[:, 0, :]
            qf = kb.sb("c_qf", [128, 2, S], BF16, es)
            qb = kb.sb("c_qb", [128, 2, S], BF16, es)
            kf = kb.sb("c_kf", [128, 4, S], BF16, es)
            kbw = kb.sb("c_kb", [128, 4, S], BF16, es)
            kdf = kb.sb("c_kdf", [128, 16, 256], BF16, es)
            kdb = kb.sb("c_kdb", [128, 16, 256], BF16, es)
            decf = kb.sb("c_decf", [128, 2, 16], F32, es)
            decb = kb.sb("c_decb", [128, 2, 16], F32, es)
            Sf = kb.sb("c_Sf", [128, 2, 128], F32, es)
            Sb = kb.sb("c_Sb", [128, 2, 128], F32, es)
            Sfa = kb.sb("c_Sfa", [128, 16, 4, 128], BF16, es)
            Sba = kb.sb("c_Sba", [128, 16, 4, 128], BF16, es)
            attn = kb.sb("c_attn", [128, 2, 4, 128], BF16, es)
            atmp = kb.sb("c_atmp", [128, 4, 128], F32, es)
            ogb = kb.sb("c_og", [128, 2, 512], BF16, es)
            sq = kb.sb("c_sq", [128, 1, 512], BF16, es)
            rs = kb.sb("c_rs", [128, 512], F32, es)
            on = kb.sb("c_on", [128, 512], F32, es)
            sgo = kb.sb("c_sgo", [128, 512], F32, es)
            ost = kb.sb("c_ost", [128, 2, 512], BF16, es)
            self.bank_free = [None] * 8
            self.rstd_free = None
            ld = [kb.dma(kb.sp, q[:], self.d_cq[s].rearrange("c p t -> p c t"), self.ld_ds),
                  kb.dma(kb.sp, k[:], self.d_ck[s].rearrange("c p t -> p c t"), self.ld_ds),
                  kb.dma(kb.sp, kt[:], self.d_ckt[s].rearrange("k p n -> p k n"), self.ld_ds),
                  kb.dma(kb.sp, v[:], self.d_cv[s].rearrange("k p n -> p k n"), self.ld_ds),
                  kb.dma(kb.sp, lr[0:32, :], self.d_clr[s], self.ld_ds),
                  kb.dma(kb.sp, w2f[:], self.w2e[l].rearrange("d k n -> k d n"), self.ld_ds)]
            for e in (kb.pe, kb.act, kb.dve):
                e.wait(ld)
            nc.vector.memset(lr[32:33, :], 1.0)
            nc.vector.memset(Sf[:], 0.0)
            nc.vector.memset(Sb[:], 0.0)
            nc.vector.memset(kf[:], 0.0)
            nc.vector.memset(kbw[:], 0.0)
            nc.vector.memset(Sfa[:], 0.0)
            nc.vector.memset(Sba[:], 0.0)
            d0 = kb.dve.ms(nc.vector.tensor_copy(out=w2[:], in_=w2f[:]))
            kb.pe.wait(d0)
            for n in range(NC_):
                bank = n % 2
                kb.pe.wait(self.bank_free[bank])
                for dr in range(2):
                    ins = nc.tensor.matmul(PS[:, bank, dr * 256:(dr + 1) * 256], lr[0:33, n * 128:(n + 1) * 128],
                                           w2[0:33, dr, :], start=True, stop=True)
                p = kb.pe.ms(ins)
                kb.act.wait(p)
                nc.scalar.activation(out=ebuf, in_=PS[:, bank, :], func=AF.Exp, scale=-1.0)
                a = kb.act.ms(nc.scalar.activation(out=G[:, n, :, :].rearrange("p d n -> p (d n)"), in_=ebuf,
                                                   func=AF.Ln, bias=self.one_col[:], scale=1.0))
                self.bank_free[bank] = a
            g_ready = a
            CS = 9
            if CS <= 1:
                kb.barrier(); return
            kb.pe.wait(g_ready)
            for ft in range(2):
                for tt in range(4):
                    tsl = slice(tt * 512, (tt + 1) * 512)
                    kb.pe.wait(self.bank_free[0], self.bank_free[1])
                    for dr in range(2):
                        for cc in range(4):
                            n = tt * 4 + cc
                            ins = nc.tensor.matmul(PS[:, dr, cc * 128:(cc + 1) * 128],
                                                   G[:, n, dr, ft * 128:(ft + 1) * 128], self.tri_f[:, 2 * dr, :],
                                                   start=True, stop=True)
                    p = kb.pe.ms(ins)
                    kb.act.wait(p, self.bank_free[2])
                    nc.scalar.activation(out=etmp[:, 0, :], in_=PS[:, 0, :], func=AF.Exp, scale=-1.0 / 16)
                    a1 = kb.act.ms(nc.scalar.activation(out=etmp[:, 1, :], in_=PS[:, 0, :], func=AF.Exp, scale=1.0 / 16))
                    kb.dve.wait(a1)
                    nc.vector.tensor_tensor(out=qf[:, ft, tsl], in0=q[:, ft, tsl], in1=etmp[:, 0, :], op=ALU.mult)
                    for hp in range(2):
                        r0 = hp * 64
                        nc.vector.tensor_tensor(out=kf[r0:r0 + 64, 2 * ft + hp, tsl], in0=k[r0:r0 + 64, ft, tsl],
                                                in1=etmp[r0:r0 + 64, 1, :], op=ALU.mult)
                    d1 = kb.dve.ms(nc.vector.tensor_copy(
                        out=decf[:, ft, tt * 4:(tt + 1) * 4],
                        in_=etmp[:, 0, :].rearrange("p (c t) -> p c t", c=4)[:, :, 127]))
                    kb.act.wait(d1)
                    nc.scalar.activation(out=etmp[:, 0, :], in_=PS[:, 1, :], func=AF.Exp, scale=-1.0 / 16)
                    a2 = kb.act.ms(nc.scalar.activation(out=etmp[:, 1, :], in_=PS[:, 1, :], func=AF.Exp, scale=1.0 / 16))
                    self.bank_free[0] = a2
                    self.bank_free[1] = a2
                    kb.dve.wait(a2)
                    nc.vector.tensor_tensor(out=qb[:, ft, tsl], in0=q[:, ft, tsl], in1=etmp[:, 0, :], op=ALU.mult)
                    for hp in range(2):
                        r0 = hp * 64
                        nc.vector.tensor_tensor(out=kbw[r0:r0 + 64, 2 * ft + hp, tsl], in0=k[r0:r0 + 64, ft, tsl],
                                                in1=etmp[r0:r0 + 64, 1, :], op=ALU.mult)
                    d2 = kb.dve.ms(nc.vector.tensor_copy(
                        out=decb[:, ft, tt * 4:(tt + 1) * 4],
                        in_=etmp[:, 0, :].rearrange("p (c t) -> p c t", c=4)[:, :, 0]))
                    kb.act.wait(d2)
            if CS <= 2:
                kb.barrier(); return
            for n in range(NC_):
                bank = 2 + n % 2
                kb.pe.wait(self.bank_free[bank])
                nc.tensor.matmul(PS[:, bank, 0:256], self.tri_f[:, 1, :], G[:, n, 0, :], start=True, stop=True)
                p = kb.pe.ms(nc.tensor.matmul(PS[:, bank, 256:512], self.tri_f[:, 3, :], G[:, n, 1, :], start=True,
                                              stop=True))
                kb.act.wait(p, d2)
                a = kb.act.ms(nc.scalar.activation(out=etmp[:, n % 2, :], in_=PS[:, bank, :], func=AF.Exp,
                                                   scale=-1.0 / 16))
                self.bank_free[bank] = a
                kb.dve.wait(a)
                nc.vector.tensor_tensor(out=kdf[:, n, :], in0=kt[:, n, :], in1=etmp[:, n % 2, 0:256], op=ALU.mult)
                d2 = kb.dve.ms(nc.vector.tensor_tensor(out=kdb[:, n, :], in0=kt[:, n, :], in1=etmp[:, n % 2, 256:512],
                                                       op=ALU.mult))
            kd_ready = d2
            if CS <= 3:
                kb.barrier(); return
            kb.pe.wait(kd_ready)
            for step in range(NC_):
                for dr, (Sx, Sall, kd, dec) in enumerate(((Sf, Sfa, kdf, decf), (Sb, Sba, kdb, decb))):
                    n = step if dr == 0 else NC_ - 1 - step
                    nc.vector.tensor_copy(out=Sall[0:64, n, 0:4:2, :], in_=Sx[0:64, :, :])
                    dsn = kb.dve.ms(nc.vector.tensor_copy(out=Sall[64:128, n, 1:4:2, :], in_=Sx[64:128, :, :]))
                    if step == NC_ - 1:
                        continue
                    bank = 4 + dr
                    kb.pe.wait(self.bank_free[bank])
                    for h in range(4):
                        ft = h // 2
                        ins = nc.tensor.matmul(PS[:, bank, h * 128:(h + 1) * 128], kd[:, n, ft * 128:(ft + 1) * 128],
                                               v[:, n, h * 128:(h + 1) * 128], start=True, stop=True)
                    p = kb.pe.ms(ins)
                    kb.dve.wait(p)
                    for h in range(4):
                        ft, r0 = h // 2, (h % 2) * 64
                        ins = nc.vector.scalar_tensor_tensor(
                            out=Sx[r0:r0 + 64, ft, :], in0=Sx[r0:r0 + 64, ft, :], scalar=dec[r0:r0 + 64, ft, n:n + 1],
                            in1=PS[r0:r0 + 64, bank, h * 128:(h + 1) * 128], op0=ALU.mult, op1=ALU.add)
                    self.bank_free[bank] = kb.dve.ms(ins)
            st_ready = (kb.dve.sem, kb.dve.cnt)
            if CS <= 4:
                kb.barrier(); return
            MF = self.tri_f[:, 0, None, :].broadcast_to([128, 4, 128])
            MB = self.tri_f[:, 2, None, :].broadcast_to([128, 4, 128])
            kb.pe.wait(st_ready)
            og_ds_free = [None, None]
            ost_free = [None, None]
            attn_free = [None, None]
            it = 0
            for tt in range(4):
                tsl = slice(tt * 512, (tt + 1) * 512)
                for h in range(4):
                    kb.pe.wait(self.bank_free[4 + h])
                for cc in range(4):
                    n = tt * 4 + cc
                    csl = slice(n * 128, (n + 1) * 128)
                    ap_ = it % 2
                    it += 1
                    kb.pe.wait(self.bank_free[0], self.bank_free[1])
                    for h in range(4):
                        ft, r0 = h // 2, (h % 2) * 64
                        nc.tensor.matmul(PS[:, 0, h * 128:(h + 1) * 128], kf[:, h, csl],
                                         qf[:, ft, csl], start=True, stop=True)
                        ins = nc.tensor.matmul(PS[:, 1, h * 128:(h + 1) * 128], kbw[:, h, csl],
                                               qb[:, ft, csl], start=True, stop=True)
                    p = kb.pe.ms(ins)
                    kb.dve.wait(p, attn_free[ap_])
                    nc.vector.tensor_tensor(out=atmp[:], in0=PS[:, 0, :].rearrange("p (h t) -> p h t", h=4), in1=MF,
                                            op=ALU.mult)
                    nc.vector.tensor_tensor(out=attn[:, ap_, :, :], in0=PS[:, 1, :].rearrange("p (h t) -> p h t", h=4),
                                            in1=MB, op=ALU.mult)
                    d = kb.dve.ms(nc.vector.tensor_tensor(out=attn[:, ap_, :, :], in0=attn[:, ap_, :, :], in1=atmp[:],
                                                          op=ALU.add))
                    self.bank_free[0] = d
                    self.bank_free[1] = d
                    kb.pe.wait(d)
                    for h in range(4):
                        ft, r0 = h // 2, (h % 2) * 64
                        ob = PS[:, 4 + h, cc * 128:(cc + 1) * 128]
                        nc.tensor.matmul(ob, v[:, n, h * 128:(h + 1) * 128], attn[:, ap_, h, :], start=True, stop=False)
                        nc.tensor.matmul(ob, Sfa[:, n, h, :], qf[:, ft, csl], start=False, stop=False)
                        ins = nc.tensor.matmul(ob, Sba[:, n, h, :], qb[:, ft, csl], start=False, stop=True)
                    attn_free[ap_] = kb.pe.ms(ins)
                o_ready = attn_free[(it - 1) % 2]
                for h in range(4 if CS > 5 else 0):
                    so = h % 2
                    kb.sp.wait(og_ds_free[so])
                    tog = kb.dma(kb.sp, ogb[:, so, :], self.d_cog[s, h, :, tsl], self.og_ds[so])
                    kb.act.wait(o_ready)
                    self.rms_stats(PS[:, 4 + h, :].rearrange("p (c t) -> p c t", c=1), 1, sq, 2, 128, rstd=rs)
                    kb.act.wait(tog)
                    a = kb.act.ms(nc.scalar.activation(out=sgo[:], in_=ogb[:, so, :], func=AF.Silu))
                    og_ds_free[so] = a
                    nc.vector.scalar_tensor_tensor(out=on[:], in0=PS[:, 4 + h, :], scalar=self.onorm_sb[:, l:l + 1],
                                                   in1=rs[:], op0=ALU.mult, op1=ALU.mult)
                    kb.dve.wait(a, ost_free[so])
                    d = kb.dve.ms(nc.vector.tensor_tensor(out=ost[:, so, :], in0=on[:], in1=sgo[:], op=ALU.mult))
                    self.bank_free[4 + h] = d
                    self.rstd_free = d
                    kb.act.wait(d)
                    kb.sp.wait(d)
                    ost_free[so] = kb.dma(kb.sp, self.d_om[s, 12 + h, :, tsl], ost[:, so, :], self.ost_ds[so],
                                          track=True)
            kb.barrier()

    def build(self):
        kb, nc = self.kb, self.nc
        L = self.depth
        sb = kb.sb
        self.gain_sb = sb("gain_sb", [128, L * 6 * 16], F32)
        self.gainh_sb = sb("gainh_sb", [128, L * 6 * 16], F32)
        self.gb_sb = sb("gb_sb", [128, L * 3 * 16], F32)
        self.qkg_sb = sb("qkg_sb", [128, L * 2], F32)
        self.onorm_sb = sb("onorm_sb", [128, L], F32)
        self.ones_f = sb("ones_f", [128, 128], F32)
        self.ones_bf = sb("ones_bf", [128, 128], BF16)
        self.rot_f = sb("rot_f", [128, 128], F32)
        self.rotT_bf = sb("rotT_bf", [128, 128], BF16)
        self.tri_f = sb("tri_f", [128, 4, 128], F32)
        self.rm_f = sb("rm_f", [128, 28 * 8], F32)
        self.rm_bf = sb("rm_bf", [128, 28 * 8], BF16)
        self.rstd = sb("rstd", [128, T], F32)
        self.eps_col = sb("eps_col", [128, 1], F32)
        self.one_col = sb("one_col", [128, 1], F32)
        self.ps = kb.es.enter_context(nc.psum_tensor("ps", [128, 8, 512], F32))
        self.ringA = Ring(kb, "rA", 5, 16)
        self.ringB = Ring(kb, "rB", 2, 44)
        self.bank_free = [None] * 8
        self.rstd_free = None
        self.x_ds = DSem(kb, "xld")
        self.y_ds = DSem(kb, "yst")
        self.stg_ds = [DSem(kb, f"stg{i}") for i in range(4)]
        self.cs_ds = DSem(kb, "csld")
        self.om_ds = DSem(kb, "omld")
        self.ld_ds = DSem(kb, "mixld")
        self.ost_ds = [DSem(kb, "ost0"), DSem(kb, "ost1")]
        self.og_ds = [DSem(kb, "og0"), DSem(kb, "og1")]
        self.stg_i = 0
        cs = DSem(kb, "cst")
        toks = [kb.dma(kb.sp, self.gain_sb[:], self.gains, cs),
                kb.dma(kb.sp, self.gb_sb[:], self.gbias, cs),
                kb.dma(kb.sp, self.qkg_sb[:], self.qkg, cs),
                kb.dma(kb.sp, self.onorm_sb[:], self.onorm, cs),
                kb.dma(kb.sp, self.ones_f[:], self.c_ones, cs),
                kb.dma(kb.sp, self.rot_f[:], self.c_rotT, cs),
                kb.dma(kb.sp, self.tri_f[:], self.c_tri.rearrange("k p n -> p k n"), cs),
                kb.dma(kb.sp, self.rm_f[:], self.c_rm, cs)]
        kb.dve.wait(toks)
        nc.vector.memset(self.eps_col[:], EPS)
        nc.vector.memset(self.one_col[:], 1.0)
        nc.vector.tensor_copy(out=self.ones_bf[:], in_=self.ones_f[:])
        nc.vector.tensor_copy(out=self.rotT_bf[:], in_=self.rot_f[:])
        nc.vector.tensor_copy(out=self.rm_bf[:], in_=self.rm_f[:])
        ins = nc.vector.tensor_scalar(out=self.gainh_sb[:], in0=self.gain_sb[:], scalar1=0.5, scalar2=None,
                                      op0=ALU.mult)
        kb.dve.ms(ins)
        kb.barrier()
        for l in range(L + 1):
            self.tl_phase(l)
            if l < L:
                for s in range(self.nseq):
                    if self.mode in ("full", "A"):
                        self.mixer_a(l, s)
                    if self.mode in ("full", "B"):
                        self.mixer_b(l, s)
                    if self.mode in ("full", "C"):
                        self.mixer_c(l, s)


def _consts():
    c = {}
    c["c_ones"] = np.ones((128, 128), np.float32)
    R = np.zeros((128, 128), np.float32)
    for i in range(64):
        R[2 * i, 2 * i + 1] = -1.0
        R[2 * i + 1, 2 * i] = 1.0
    c["c_rotT"] = np.ascontiguousarray(R.T)
    t = np.arange(S)
    pos_r = (t // GRID_W).astype(np.float32)
    pos_c = (t % GRID_W).astype(np.float32)
    half = 64
    inv = (np.float32(10000.0) ** (-np.arange(0, half, 2, dtype=np.float32) / half)).astype(np.float32)
    ang = np.concatenate([pos_r[:, None] * inv, pos_c[:, None] * inv], axis=-1).astype(np.float32)
    cosT = np.cos(ang).T.astype(np.float32)
    sinT = np.sin(ang).T.astype(np.float32)
    c["c_cos"] = np.ascontiguousarray(np.repeat(cosT, 2, axis=0))
    c["c_sin"] = np.ascontiguousarray(np.repeat(sinT, 2, axis=0))
    j = np.arange(128)[:, None]
    i = np.arange(128)[None, :]
    c["c_tri"] = np.stack([(j <= i), (j > i), (j >= i), (j < i)]).astype(np.float32)
    KCS = [list(range(0, 6)), list(range(2, 10)), list(range(6, 14)), list(range(10, 16))]
    rm = np.zeros((128, 28, 8), np.float32)
    pi = 0
    for qt in range(4):
        for kc in KCS[qt]:
            for krl in range(2):
                kr = 2 * kc + krl
                for qrl in range(8):
                    qr = 8 * qt + qrl
                    rs = min(max(qr - 4, 0), 24)
                    if rs <= kr <= rs + 7:
                        rm[krl * 64:(krl + 1) * 64, pi, qrl] = 1.0
            pi += 1
    c["c_rm"] = rm.reshape(128, 28 * 8)
    return c


def _rpb_table(rpb):
    L = rpb.shape[0]
    out = np.full((L, 4, 128, 31, 64), -30000.0, np.float32)
    kcol = np.arange(64)[:, None]
    qcol = np.arange(64)[None, :]
    cs = np.clip(qcol - 8, 0, 48)
    cm = (kcol >= cs) & (kcol <= cs + 15)
    dc = np.clip(kcol - qcol + 15, 0, 30)
    for krl in range(2):
        for a2 in range(31):
            a = a2 - 8 - krl
            if 0 <= a <= 14:
                vals = rpb[:, :, 14 - a, :][:, :, dc]
                out[:, :, krl * 64:(krl + 1) * 64, a2, :] = np.where(cm[None, None], vals, np.float32(-30000.0))
    return out.reshape(L, 4, 128, 31 * 64)


def host_prep(inputs, depth):
    g = lambda k: np.asarray(inputs[k], dtype=np.float32)
    L = depth
    out = {}
    out["w_f1i"] = np.stack([pretile(g("w_ffn1_in")[l]) for l in range(L)])
    out["w_f1o"] = np.stack([pretile(g("w_ffn1_out")[l]) for l in range(L)])
    out["w_f2i"] = np.stack([pretile(g("w_ffn2_in")[l]) for l in range(L)])
    out["w_f2o"] = np.stack([pretile(g("w_ffn2_out")[l]) for l in range(L)])
    w_in = g("w_in")
    wm = np.zeros((L, D, NMIXB * 128), np.float32)
    wm[:, :, :NMIX] = w_in[:L, :, :NMIX]
    out["w_mix"] = np.stack([pretile(wm[l]) for l in range(L)])
    out["w_gate"] = np.stack([pretile(w_in[l][:, NMIX:]) for l in range(L)])
    out["w_br"] = np.stack([pretile(np.concatenate([g("w_br_a")[l], g("w_br_b")[l], g("w_br_c")[l]], axis=0))
                            for l in range(L)])
    out["w_o"] = np.stack([pretile(g("w_out")[l]) for l in range(L)])
    ng = g("norm_gains")[:L]
    out["gains"] = np.ascontiguousarray(ng.reshape(L, 6, 16, 128).transpose(3, 0, 1, 2).reshape(128, L * 6 * 16))
    gbi = g("gate_bias")[:L]
    out["gbias"] = np.ascontiguousarray(gbi.reshape(L, 3, 16, 128).transpose(3, 0, 1, 2).reshape(128, L * 3 * 16))
    out["qkg"] = np.ascontiguousarray(g("qk_norm_a")[:L].transpose(2, 0, 1).reshape(128, L * 2))
    out["onorm"] = np.ascontiguousarray(g("onorm_c")[:L].T)
    out["rpbT"] = _rpb_table(g("rpb_b")[:L])
    w2 = g("w_decay_c")[:L]
    b2 = g("b_decay_c")[:L]
    w2e = np.zeros((L, 2, 33, 256), np.float32)
    w2e[:, 0, 0:16, :] = w2[:, 0]
    w2e[:, 1, 16:32, :] = w2[:, 1]
    w2e[:, :, 32, :] = b2
    out["w2e"] = w2e
    out.update(_consts())
    return out


def to_fm(x):
    n = x.shape[0]
    return np.ascontiguousarray(x.reshape(n, S, NCH, 128).transpose(0, 2, 3, 1))


def from_fm(y):
    n = y.shape[0]
    return np.ascontiguousarray(y.transpose(0, 3, 1, 2).reshape(n, S, D))


_PROG = {}


def kernel(**inputs):
    n_cores = 8
    depth = int(np.asarray(inputs["norm_gains"]).shape[0])
    xp = np.asarray(inputs["x_prompt"], dtype=np.float32)
    xs = np.asarray(inputs["x_sample"], dtype=np.float32)
    nb_p, nb_s = xp.shape[0], xs.shape[0]
    xall = np.concatenate([xp, xs], axis=0)
    nseq = xall.shape[0] // n_cores
    key = (nseq, depth)
    if key not in _PROG:
        _PROG[key] = Prog(nseq, depth)
    prog = _PROG[key]
    hp = host_prep(inputs, depth)
    in_maps = []
    for c in range(n_cores):
        m = dict(hp)
        m["xin"] = to_fm(xall[c * nseq:(c + 1) * nseq])
        in_maps.append(m)
    res = run_bass_kernel_spmd(prog.nc, in_maps, core_ids=list(range(n_cores)))
    ys = [from_fm(np.asarray(r["yout"])) for r in res.results]
    yall = np.concatenate(ys, axis=0)
    return yall[:nb_p], yall[nb_p:nb_p + nb_s]
```
